# Optimizing a Trainium2 kernel written in Bass

```python
import math
import jax, jax.numpy as jnp
from jax import lax
import numpy as np

D_MODEL = 1024
BATCH = 4
SEQ = 8192
DEPTH = 1
DEC_BATCH = 8
DEC_SEQ = 8192
PAST_LEN = 128

H_A = 8
NOPE_DIM = 64
ROPE_DIM = 32
QK_DIM = NOPE_DIM + ROPE_DIM
V_DIM = 64
Q_LORA = 384
KV_LORA = 256
ROPE_THETA = 10000.0
Q_BLOCK = 128
H_R = 8
HEAD_N = 64
C_R = H_R * HEAD_N
W_LORA = 64
A_LORA = 64
G_LORA = 128
GN_EPS = 64e-5
D_MIX = H_A * V_DIM + C_R
RWKV_IN = 3 * C_R + W_LORA + A_LORA + G_LORA
OFF_Q = 0
OFF_KV = OFF_Q + Q_LORA
OFF_KR = OFF_KV + KV_LORA
OFF_RWKV = OFF_KR + ROPE_DIM
IN_COLS = OFF_RWKV + RWKV_IN
N_KEYS = 128
N_EXPERTS = N_KEYS * N_KEYS
P_HEADS = 8
P_TOPK = 16
D_KEY = 256
HALF_KEY = D_KEY // 2
TOKEN_BLOCK = 128
NORM_EPS = 1e-6

kernel_name = "hybrid_mla_rwkv7_peer_encoder"


def rms_norm(x, g):
    xf = x.astype(jnp.float32)
    y = xf * lax.rsqrt(jnp.mean(xf * xf, axis=-1, keepdims=True) + NORM_EPS)
    return (y * g.astype(jnp.float32)).astype(x.dtype)


def apply_rope(x, pos):
    half = x.shape[-1] // 2
    inv = 1.0 / (ROPE_THETA ** (jnp.arange(half, dtype=jnp.float32) / half))
    ang = pos[:, None] * inv[None, :]
    cos = jnp.cos(ang)[None, :, None, :].astype(x.dtype)
    sin = jnp.sin(ang)[None, :, None, :].astype(x.dtype)
    x1, x2 = x[..., :half], x[..., half:]
    return jnp.concatenate([x1 * cos - x2 * sin, x1 * sin + x2 * cos], axis=-1)


def block_attention(q, k, v):
    B, S, H, Dq = q.shape
    nb = S // Q_BLOCK
    qb = q.reshape(B, nb, Q_BLOCK, H, Dq).transpose(1, 0, 2, 3, 4)
    scale = Dq ** -0.5

    def one(qi):
        s = jnp.einsum('bqhd,bkhd->bhqk', qi, k).astype(jnp.float32) * scale
        p = jax.nn.softmax(s, axis=-1).astype(v.dtype)
        return jnp.einsum('bhqk,bkhd->bqhd', p, v)

    o = lax.map(one, qb)
    return o.transpose(1, 0, 2, 3, 4).reshape(B, S, H, v.shape[-1])


def wkv_scan(r, k, v, decay, kk, a, reverse):
    B, S, H, N = r.shape
    seq = tuple(t.transpose(1, 0, 2, 3) for t in (r, k, v, decay, kk, a))

    def step(state, inp):
        r_t, k_t, v_t, w_t, kk_t, a_t = inp
        sa = jnp.einsum('bhvk,bhk->bhv', state, -kk_t)
        state = (state * w_t[:, :, None, :]
                 + sa[..., None] * (kk_t * a_t)[:, :, None, :]
                 + v_t[..., None] * k_t[:, :, None, :])
        return state, jnp.einsum('bhvk,bhk->bhv', state, r_t)

    s0 = jnp.zeros((B, H, N, N), jnp.float32)
    _, ys = lax.scan(step, s0, seq, reverse=reverse)
    return ys.transpose(1, 0, 2, 3)


def peer_ffn(xn, w_pq, sub_keys, expert_u, expert_v):
    B, S, D = xn.shape
    T = B * S
    dt = xn.dtype
    xt = xn.reshape(T, D)
    q = (xt @ w_pq).astype(jnp.float32).reshape(T, P_HEADS, 2, HALF_KEY)
    sk = sub_keys.astype(jnp.float32)
    s1 = jnp.einsum('thc,nc->thn', q[:, :, 0], sk[0])
    s2 = jnp.einsum('thc,nc->thn', q[:, :, 1], sk[1])
    v1, i1 = lax.top_k(s1, P_TOPK)
    v2, i2 = lax.top_k(s2, P_TOPK)
    cand_s = (v1[..., :, None] + v2[..., None, :]).reshape(T, P_HEADS, P_TOPK * P_TOPK)
    cand_i = (i1[..., :, None] * N_KEYS + i2[..., None, :]).reshape(T, P_HEADS, P_TOPK * P_TOPK)
    best_s, best_pos = lax.top_k(cand_s, P_TOPK)
    idx = jnp.take_along_axis(cand_i, best_pos, axis=-1)
    gate = jax.nn.softmax(best_s, axis=-1)
    PK = P_HEADS * P_TOPK
    nb = T // TOKEN_BLOCK

    def expert_block(args):
        xb, ib, gb = args
        u = expert_u[ib]
        act = jax.nn.gelu(jnp.einsum('td,tkd->tk', xb, u).astype(jnp.float32), approximate=False)
        return jnp.einsum('tk,tkd->td', (gb * act).astype(dt), expert_v[ib])

    out = lax.map(expert_block, (xt.reshape(nb, TOKEN_BLOCK, D),
                                 idx.reshape(nb, TOKEN_BLOCK, PK),
                                 gate.reshape(nb, TOKEN_BLOCK, PK)))
    return out.reshape(B, S, D)


def encoder_layer(x, norm1_g, w_in, q_lat_g, w_uq, kv_lat_g, w_ukv, q_norm_g, k_norm_g,
                  attn_out_g, mu_prev, mu_next, w0, w_up, a0, a_up, g_up, k_k, k_a, r_k,
                  ln_x_g, ln_x_b, w_out, norm2_g, w_pq, sub_keys, expert_u, expert_v):
    B, S, _ = x.shape
    dt = x.dtype
    f32 = jnp.float32
    xn = rms_norm(x, norm1_g)
    proj = xn @ w_in

    q_lat = rms_norm(proj[..., OFF_Q:OFF_KV], q_lat_g)
    kv_lat = rms_norm(proj[..., OFF_KV:OFF_KR], kv_lat_g)
    k_pe = proj[..., OFF_KR:OFF_RWKV]
    q = (q_lat @ w_uq).reshape(B, S, H_A, QK_DIM)
    kv = (kv_lat @ w_ukv).reshape(B, S, H_A, NOPE_DIM + V_DIM)
    k = jnp.concatenate([kv[..., :NOPE_DIM],
                         jnp.broadcast_to(k_pe[:, :, None, :], (B, S, H_A, ROPE_DIM))], axis=-1)
    v = kv[..., NOPE_DIM:]
    q = rms_norm(q, q_norm_g)
    k = rms_norm(k, k_norm_g)
    pos = jnp.arange(S, dtype=f32)
    q = jnp.concatenate([q[..., :NOPE_DIM], apply_rope(q[..., NOPE_DIM:], pos)], axis=-1)
    k = jnp.concatenate([k[..., :NOPE_DIM], apply_rope(k[..., NOPE_DIM:], pos)], axis=-1)
    attn = block_attention(q, k, v).reshape(B, S, H_A * V_DIM)
    attn = rms_norm(attn, attn_out_g)

    z = proj[..., OFF_RWKV:]
    z_prev = jnp.pad(z[:, :-1], ((0, 0), (1, 0), (0, 0)))
    z_next = jnp.pad(z[:, 1:], ((0, 0), (0, 1), (0, 0)))
    z = z + mu_prev * (z_prev - z) + mu_next * (z_next - z)
    o1, o2, o3 = C_R, 2 * C_R, 3 * C_R
    o4, o5 = o3 + W_LORA, o3 + W_LORA + A_LORA
    hs = (B, S, H_R, HEAD_N)
    r = z[..., :o1].astype(f32).reshape(hs)
    kr = z[..., o1:o2].astype(f32).reshape(hs)
    vr = z[..., o2:o3].astype(f32).reshape(hs)
    zw = jnp.tanh(z[..., o3:o4].astype(f32))
    za = z[..., o4:o5].astype(f32)
    g = jax.nn.sigmoid(z[..., o5:]) @ g_up
    kk = kr * k_k.astype(f32).reshape(H_R, HEAD_N)
    kk = kk / jnp.maximum(jnp.sqrt(jnp.sum(kk * kk, axis=-1, keepdims=True)), 1e-12)
    k_a_h = k_a.astype(f32).reshape(H_R, HEAD_N)

    def direction(d, reverse):
        w = -jax.nn.softplus(-(w0[d].astype(f32) + zw @ w_up[d].astype(f32))) - 0.5
        decay = jnp.exp(-jnp.exp(w)).reshape(hs)
        a = jax.nn.sigmoid(a0[d].astype(f32) + za @ a_up[d].astype(f32)).reshape(hs)
        k_d = kr * (1.0 + (a - 1.0) * k_a_h)
        return wkv_scan(r, k_d, vr, decay, kk, a, reverse), k_d

    y_f, k_f = direction(0, False)
    y_b, k_b = direction(1, True)
    y = y_f + y_b
    mu = jnp.mean(y, axis=-1, keepdims=True)
    var = jnp.mean(jnp.square(y - mu), axis=-1, keepdims=True)
    yn = ((y - mu) * lax.rsqrt(var + GN_EPS)).reshape(B, S, C_R)
    yn = yn * ln_x_g.astype(f32) + ln_x_b.astype(f32)
    k_mean = 0.5 * (k_f + k_b)
    bonus = jnp.sum(r * k_mean * r_k.astype(f32).reshape(H_R, HEAD_N), axis=-1, keepdims=True) * vr
    rw = ((yn + bonus.reshape(B, S, C_R)) * g.astype(f32)).astype(dt)

    h = x + jnp.concatenate([attn, rw], axis=-1) @ w_out

    hn = rms_norm(h, norm2_g)
    return h + peer_ffn(hn, w_pq, sub_keys, expert_u, expert_v)


def setup_inputs(seed: int = 0) -> dict:
    key = jax.random.key(seed)
    ks = jax.random.split(key, 32)
    L = DEPTH
    f32 = jnp.float32

    def nrm(k, shape, scale):
        return jax.random.normal(k, shape, f32) * scale

    def gain(k, shape):
        return 1.0 + 0.01 * jax.random.normal(k, shape, f32)

    return {
        "x_prompt": jax.random.normal(ks[0], (BATCH, SEQ, D_MODEL), f32),
        "x_sample": jax.random.normal(ks[1], (DEC_BATCH, DEC_SEQ, D_MODEL), f32),
        "norm1_g": gain(ks[2], (L, D_MODEL)),
        "w_in": nrm(ks[3], (L, D_MODEL, IN_COLS), D_MODEL ** -0.5),
        "q_lat_g": gain(ks[4], (L, Q_LORA)),
        "w_uq": nrm(ks[5], (L, Q_LORA, H_A * QK_DIM), Q_LORA ** -0.5),
        "kv_lat_g": gain(ks[6], (L, KV_LORA)),
        "w_ukv": nrm(ks[7], (L, KV_LORA, H_A * (NOPE_DIM + V_DIM)), KV_LORA ** -0.5),
        "q_norm_g": gain(ks[8], (L, QK_DIM)),
        "k_norm_g": gain(ks[9], (L, QK_DIM)),
        "attn_out_g": gain(ks[10], (L, H_A * V_DIM)),
        "mu_prev": jax.random.uniform(ks[11], (L, RWKV_IN), f32, 0.0, 0.5),
        "mu_next": jax.random.uniform(ks[12], (L, RWKV_IN), f32, 0.0, 0.5),
        "w0": jax.random.uniform(ks[13], (L, 2, C_R), f32, -6.0, 1.0),
        "w_up": nrm(ks[14], (L, 2, W_LORA, C_R), 0.1 * W_LORA ** -0.5),
        "a0": nrm(ks[15], (L, 2, C_R), 0.5),
        "a_up": nrm(ks[16], (L, 2, A_LORA, C_R), 0.5 * A_LORA ** -0.5),
        "g_up": nrm(ks[17], (L, G_LORA, C_R), G_LORA ** -0.5),
        "k_k": 0.85 + 0.05 * jax.random.normal(ks[18], (L, C_R), f32),
        "k_a": 1.0 + 0.05 * jax.random.normal(ks[19], (L, C_R), f32),
        "r_k": nrm(ks[20], (L, C_R), 0.1),
        "ln_x_g": gain(ks[21], (L, C_R)),
        "ln_x_b": nrm(ks[22], (L, C_R), 0.01),
        "w_out": nrm(ks[23], (L, D_MIX, D_MODEL), D_MIX ** -0.5),
        "norm2_g": gain(ks[24], (L, D_MODEL)),
        "w_pq": nrm(ks[25], (L, D_MODEL, P_HEADS * D_KEY), D_MODEL ** -0.5),
        "sub_keys": nrm(ks[26], (L, 2, N_KEYS, HALF_KEY), HALF_KEY ** -0.5),
        "expert_u": nrm(ks[27], (L, N_EXPERTS, D_MODEL), D_MODEL ** -0.5),
        "expert_v": nrm(ks[28], (L, N_EXPERTS, D_MODEL), (P_HEADS * P_TOPK) ** -0.5),
    }


def reference(x_prompt, x_sample, norm1_g, w_in, q_lat_g, w_uq, kv_lat_g, w_ukv, q_norm_g,
              k_norm_g, attn_out_g, mu_prev, mu_next, w0, w_up, a0, a_up, g_up, k_k, k_a, r_k,
              ln_x_g, ln_x_b, w_out, norm2_g, w_pq, sub_keys, expert_u, expert_v):
    layer_weights = (norm1_g, w_in, q_lat_g, w_uq, kv_lat_g, w_ukv, q_norm_g, k_norm_g,
                     attn_out_g, mu_prev, mu_next, w0, w_up, a0, a_up, g_up, k_k, k_a, r_k,
                     ln_x_g, ln_x_b, w_out, norm2_g, w_pq, sub_keys, expert_u, expert_v)

    def trunk(x):
        for l in range(DEPTH):
            x = encoder_layer(x, *(w[l] for w in layer_weights))
        return x

    y_prompt = trunk(x_prompt)
    y_sample = trunk(x_sample)
    return (y_prompt, y_sample)
```

```python
import numpy as np
import concourse.bass as bass
import concourse.mybir as mybir
from concourse.bass_utils import run_bass_kernel_spmd
from contextlib import ExitStack

F32 = mybir.dt.float32
BF16 = mybir.dt.bfloat16
U32 = mybir.dt.uint32
AF = mybir.ActivationFunctionType
ALU = mybir.AluOpType
AX = mybir.AxisListType

D = 1024
IN_COLS = 2464
OFF_KV = 384
OFF_KR = 640
OFF_RWKV = 672
RWKV_IN = 1792
NE = 16384
EPS = 1e-6
GN_EPS = 64e-5
KDMA = 6


class Buf:
    __slots__ = ("name", "writers", "readers", "war", "full", "excl")

    def __init__(self, name="", excl=False):
        self.name = name
        self.excl = excl
        self.writers = []
        self.readers = []
        self.war = []
        self.full = None


class Op:
    __slots__ = ("eng", "fns", "deps", "signal", "dma", "sem", "val")


class Prog:
    ENGS = ("pe", "dve", "act", "pool", "sp")

    def __init__(self, nc, es):
        self.nc = nc
        self.es = es
        self.streams = {e: [] for e in self.ENGS}
        self.allops = []
        self.dma_since = []
        self.nsem = 0

    def op(self, eng, fns, r=(), w=(), pw=(), dma=False):
        import os
        lim = int(os.environ.get("K_MAXOPS", "0"))
        self.nrec = getattr(self, "nrec", 0) + 1
        if lim and self.nrec > lim:
            o = Op()
            o.deps = set()
            o.signal = False
            o.dma = dma
            o.sem = None
            o.val = 0
            o.fns = []
            o.eng = eng
            return o
        if os.environ.get("K_TRACE"):
            import inspect
            fr = inspect.stack()[1]
            print("OP", self.nrec, eng, fr.lineno)
        o = Op()
        o.eng = eng
        o.fns = list(fns) if isinstance(fns, (list, tuple)) else [fns]
        o.dma = dma
        o.signal = dma
        o.sem = None
        o.val = 0
        deps = set()
        for b in r:
            deps.update(b.writers)
            if b.excl:
                deps.update(b.readers)
        for b in w:
            b.war = b.readers + b.writers
            deps.update(b.war)
        for b in pw:
            if b.readers:
                b.war = b.readers + b.writers
                b.writers = []
                b.readers = []
                b.full = None
            deps.update(b.war)
            if b.full is not None:
                deps.add(b.full)
        for b in w:
            b.writers = [o]
            b.readers = []
            b.full = o
        for b in pw:
            b.writers.append(o)
        for b in r:
            if (b not in w) and (b not in pw):
                b.readers.append(o)
        deps.discard(o)
        o.deps = deps
        self.streams[eng].append(o)
        self.allops.append(o)
        if dma:
            self.dma_since.append(o)
        return o

    def barrier(self):
        deps = set(self.dma_since)
        self.dma_since = []
        for e in self.ENGS:
            for o in reversed(self.streams[e]):
                if o.fns and not o.dma:
                    deps.add(o)
                    break
        for e in self.ENGS:
            o = Op()
            o.eng = e
            o.fns = []
            o.dma = False
            o.signal = False
            o.sem = None
            o.val = 0
            o.deps = set(deps)
            self.streams[e].append(o)
            self.allops.append(o)

    def newsem(self):
        self.nsem += 1
        return self.es.enter_context(self.nc.semaphore(f"s{self.nsem}"))

    def finalize(self):
        nc = self.nc
        for o in self.allops:
            for d in o.deps:
                d.signal = True
        slots = {e: [dict(sem=None, cnt=0, last=None) for _ in range(KDMA)] for e in self.ENGS}
        rr = {e: 0 for e in self.ENGS}
        for o in self.allops:
            if o.dma:
                sl = slots[o.eng][rr[o.eng] % KDMA]
                rr[o.eng] += 1
                if sl["sem"] is None or sl["cnt"] + 16 > 65000:
                    sl["sem"] = self.newsem()
                    sl["cnt"] = 0
                if sl["last"] is not None:
                    o.deps.add(sl["last"])
                sl["cnt"] += 16
                o.sem = sl["sem"]
                o.val = sl["cnt"]
                sl["last"] = o
        for e in self.ENGS:
            sem = None
            cnt = 0
            for o in self.streams[e]:
                if o.dma or not o.signal:
                    continue
                if sem is None or cnt >= 60000:
                    sem = self.newsem()
                    cnt = 0
                cnt += 1
                o.sem = sem
                o.val = cnt
        streams = self.streams

        def run(engname, eobj):
            seen = {}
            for o in streams[engname]:
                need = {}
                for d in o.deps:
                    k = id(d.sem)
                    if d.val > seen.get(k, 0) and d.val > need.get(k, (0, None))[0]:
                        need[k] = (d.val, d.sem)
                for k, (v, s) in need.items():
                    eobj.wait_ge(s, v)
                    seen[k] = v
                ins = None
                for f in o.fns:
                    ins = f(eobj)
                if o.signal:
                    ins.then_inc(o.sem, 16 if o.dma else 1)

        with nc.Block() as block:
            block.tensor(lambda e: run("pe", e))
            block.vector(lambda e: run("dve", e))
            block.scalar(lambda e: run("act", e))
            block.gpsimd(lambda e: run("pool", e))
            block.sync(lambda e: run("sp", e))


class Arena:
    def __init__(self, nc, es, nwords):
        self.t = es.enter_context(nc.sbuf_tensor("arena", [128, nwords], F32))
        self.n = nwords
        self.base = 0
        self.p = 0

    def mark(self):
        self.base = self.p

    def reset(self):
        self.p = self.base

    def alloc(self, shape, dt, parts=128):
        n = 1
        for s_ in shape:
            n *= s_
        words = (n * (2 if dt == BF16 else 4) + 3) // 4
        words = (words + 7) // 8 * 8
        assert self.p + words <= self.n, f"arena overflow {self.p + words} > {self.n}"
        ap = self.t[:, self.p:self.p + words]
        self.p += words
        if dt != F32:
            ap = ap.bitcast(dt)
        ap = ap[:, 0:n]
        if len(shape) == 2:
            ap = ap.rearrange("p (a b) -> p a b", b=shape[1])
        elif len(shape) == 3:
            ap = ap.rearrange("p (a b c) -> p a b c", b=shape[1], c=shape[2])
        if parts != 128:
            ap = ap[0:parts]
        return ap


def bc(ap, shape):
    return ap.to_broadcast(list(shape))


def build(SEQ=8192, NS=2, dbg=False, phases=(1, 2, 3, 4)):
    NT = SEQ // 128
    nc = bass.Bass("TRN2", target_bir_lowering=False)
    es = ExitStack()

    def din(name, shape, dt=F32):
        return nc.dram_tensor(name, list(shape), dt, kind="ExternalInput").ap()

    def dscr(name, shape, dt=F32):
        kind = "ExternalOutput" if dbg else "Internal"
        return nc.dram_tensor(name, list(shape), dt, kind=kind).ap()

    x = din("x", [NS, SEQ, D])
    norm1_g = din("norm1_g", [D])
    w_in = din("w_in", [D, IN_COLS])
    q_lat_g = din("q_lat_g", [384])
    w_uq = din("w_uq", [384, 768])
    kv_lat_g = din("kv_lat_g", [256])
    w_ukv = din("w_ukv", [256, 1024])
    q_norm_g = din("q_norm_g", [96])
    k_norm_g = din("k_norm_g", [96])
    attn_out_g = din("attn_out_g", [512])
    mu_prev = din("mu_prev", [RWKV_IN])
    mu_next = din("mu_next", [RWKV_IN])
    w0 = din("w0", [2, 512])
    w_up = din("w_up", [2, 64, 512])
    a0 = din("a0", [2, 512])
    a_up = din("a_up", [2, 64, 512])
    g_up = din("g_up", [128, 512])
    k_k = din("k_k", [512])
    k_a = din("k_a", [512])
    r_k = din("r_k", [512])
    ln_x_g = din("ln_x_g", [512])
    ln_x_b = din("ln_x_b", [512])
    w_out = din("w_out", [D, D])
    norm2_g = din("norm2_g", [D])
    w_pq = din("w_pq", [D, 2048])
    sub_keys = din("sub_keys", [2, 128, 128])
    expert_u = din("expert_u", [NE, D])
    expert_v = din("expert_v", [NE, D])
    ident_d = din("ident", [128, 128])
    rope_d = din("rope", [SEQ, 32])
    cmask_d = din("cmask", [128, 1792])

    y = nc.dram_tensor("y", [NS, SEQ, D], F32, kind="ExternalOutput").ap()
    qT_d = dscr("qT_s", [NS, 8, 96, SEQ], BF16)
    kT_d = dscr("kT_s", [NS, 8, 96, SEQ], BF16)
    v_d = dscr("v_s", [NS, SEQ, 520], BF16)
    z_d = dscr("z_s", [NS, SEQ + 2, RWKV_IN])
    attn_d = dscr("attn_s", [NS, SEQ, 512])
    rw_d = dscr("rw_s", [NS, SEQ, 512])

    P = Prog(nc, es)
    ar = Arena(nc, es, 45000)
    psum = es.enter_context(nc.psum_tensor("psum", [128, 4096], F32))
    pbank = [Buf(f"pb{i}", excl=True) for i in range(8)]

    def PS(b, n=512, parts=128, off=0):
        a = psum[:, b * 512 + off:b * 512 + off + n]
        return a if parts == 128 else a[0:parts]

    def PSB(b, n=1024, parts=128, off=0):
        a = psum[:, b * 512:(b + 1) * 512].bitcast(BF16)[:, off:off + n]
        return a if parts == 128 else a[0:parts]

    ident_f = ar.alloc([128], F32)
    ident_b = ar.alloc([128], BF16)
    B_ident = Buf("ident")
    P.op("sp", lambda e: e.dma_start(out=ident_f, in_=ident_d), w=[B_ident], dma=True)
    P.op("dve", lambda e: e.tensor_copy(out=ident_b, in_=ident_f), r=[B_ident], pw=[B_ident])
    mhalf = ar.alloc([16], F32)
    B_mh = Buf("mhalf")
    P.op("pool", lambda e: e.memset(mhalf, -0.5), w=[B_mh])
    ar.mark()

    def rstd(out, in_, mul, n, r, pw, eps=EPS):
        P.op("pool", lambda e: e.tensor_scalar(out=out, in0=in_, scalar1=float(mul), scalar2=float(eps), op0=ALU.mult,
                                               op1=ALU.add), r=r, pw=pw)
        P.op("pool", lambda e: e.tensor_tensor(out=out, in0=out, in1=mhalf[:, 0:n], op=ALU.pow), r=r + [B_mh], pw=pw)

    def _phase(n):
        def deco(f):
            if n in phases:
                f()
            return f
        return deco

    zt = [[Buf(f"z{s}_{t}") for t in range(NT + 2)] for s in range(NS)]
    qkv_t = [[Buf(f"qkv{s}_{t}") for t in range(NT)] for s in range(NS)]
    attn_t = [[Buf(f"at{s}_{t}") for t in range(NT)] for s in range(NS)]
    rw_t = [[Buf(f"rw{s}_{t}") for t in range(NT)] for s in range(NS)]
    out_ops = []

    def load_colT(dst, src_vec, nchunk, buf):
        P.op("sp", lambda e: e.dma_start(out=dst, in_=src_vec.rearrange("(c p) -> p c", p=128),
                                         allow_slow_non_contiguous=True), w=[buf], dma=True)

    @_phase(1)
    def _p1():
        ar.reset()
        w_in_b = ar.alloc([8, IN_COLS], BF16)
        w_uq_b = ar.alloc([3, 768], BF16)
        w_ukv_b = ar.alloc([2, 1024], BF16)
        gT = ar.alloc([16], F32)
        gq_b = ar.alloc([96], F32)
        gk_b = ar.alloc([96], F32)
        stage = [ar.alloc([IN_COLS], F32) for _ in range(2)]
        B_w = Buf("w1")
        B_gT = Buf("gT")
        B_gqk = Buf("gqk")
        B_stage = [Buf("st0"), Buf("st1")]
        load_colT(gT[:, 0:8], norm1_g, 8, B_gT)
        P.op("sp", lambda e: e.dma_start(out=gT[:, 8:11], in_=q_lat_g.rearrange("(c p) -> p c", p=128),
                                         allow_slow_non_contiguous=True), pw=[B_gT], dma=True)
        P.op("sp", lambda e: e.dma_start(out=gT[:, 11:13], in_=kv_lat_g.rearrange("(c p) -> p c", p=128),
                                         allow_slow_non_contiguous=True), pw=[B_gT], dma=True)
        P.op("sp", lambda e: e.dma_start(out=gq_b, in_=bc(q_norm_g.unsqueeze(0), [128, 96])), w=[B_gqk], dma=True)
        P.op("sp", lambda e: e.dma_start(out=gk_b, in_=bc(k_norm_g.unsqueeze(0), [128, 96])), pw=[B_gqk], dma=True)
        P.op("dve", lambda e: e.tensor_scalar(out=gq_b, in0=gq_b, scalar1=float(96 ** -0.5), scalar2=None,
                                              op0=ALU.mult), r=[B_gqk], pw=[B_gqk])
        k = 0
        jobs = [(w_in[c * 128:(c + 1) * 128, :], w_in_b[:, c, :], IN_COLS, c) for c in range(8)]
        jobs += [(w_uq[c * 128:(c + 1) * 128, :], w_uq_b[:, c, :], 768, 8 + c) for c in range(3)]
        jobs += [(w_ukv[c * 128:(c + 1) * 128, :], w_ukv_b[:, c, :], 1024, 11 + c) for c in range(2)]
        for (src, dst, n, gc) in jobs:
            st = stage[k % 2]
            bs = B_stage[k % 2]
            P.op("sp", lambda e, st=st, src=src, n=n: e.dma_start(out=st[:, 0:n], in_=src), w=[bs], dma=True)
            P.op("dve", lambda e, st=st, dst=dst, n=n, gc=gc: e.tensor_scalar(
                out=dst, in0=st[:, 0:n], scalar1=gT[:, gc:gc + 1], scalar2=None, op0=ALU.mult),
                r=[bs, B_gT], pw=[B_w])
            k += 1

        xs = [ar.alloc([D], F32) for _ in range(2)]
        B_xs = [Buf("xs0"), Buf("xs1")]
        junk = ar.alloc([D], F32)
        B_junk = Buf("junk")
        xb = ar.alloc([D], BF16)
        B_xb = Buf("xb")
        xT = ar.alloc([8, 128], BF16)
        B_xT = Buf("xT")
        st4 = ar.alloc([8], F32)
        B_st = Buf("st4")
        proj = ar.alloc([IN_COLS], F32)
        B_proj = Buf("proj")
        latb = ar.alloc([640], BF16)
        B_latb = Buf("latb")
        latT = ar.alloc([5, 128], BF16)
        B_latT = Buf("latT")
        q_sb = ar.alloc([8, 96], F32)
        k_sb = ar.alloc([8, 96], F32)
        B_q = Buf("q")
        B_k = Buf("k")
        sq = ar.alloc([8, 96], F32)
        B_sq = Buf("sq")
        sq2 = ar.alloc([8, 96], F32)
        B_sq2 = Buf("sq2")
        hst = ar.alloc([32], F32)
        B_hq = Buf("hq")
        B_hk = Buf("hk")
        vb = ar.alloc([8, 65], BF16)
        B_vb = Buf("vb")
        qb = ar.alloc([8, 96], BF16)
        kb = ar.alloc([8, 96], BF16)
        B_qb = Buf("qb")
        B_kb = Buf("kb")
        rt = ar.alloc([8, 6, 16], F32)
        B_rtq = Buf("rtq")
        B_rtk = Buf("rtk")
        cs = [ar.alloc([32], F32) for _ in range(2)]
        B_cs = [Buf("cs0"), Buf("cs1")]
        qT_sb = ar.alloc([8, 128], BF16)
        kT_sb = ar.alloc([8, 128], BF16)
        B_qT = Buf("qTsb")
        B_kT = Buf("kTsb")
        zero_t = ar.alloc([RWKV_IN], F32)
        B_zero = Buf("zero")

        P.op("dve", lambda e: e.memset(vb, 1.0), w=[B_vb])
        P.op("dve", lambda e: e.memset(zero_t, 0.0), w=[B_zero])
        for s in range(NS):
            P.op("sp", lambda e, s=s: e.dma_start(out=z_d[s, 0:1, :], in_=zero_t[0:1, :]), r=[B_zero],
                 w=[zt[s][0]], dma=True)
            P.op("sp", lambda e, s=s: e.dma_start(out=z_d[s, SEQ + 1:SEQ + 2, :], in_=zero_t[0:1, :]), r=[B_zero],
                 w=[zt[s][NT + 1]], dma=True)

        it = 0
        for s in range(NS):
            for t in range(NT):
                par = it % 2
                it += 1
                xsp, bxs = xs[par], B_xs[par]
                csp, bcs = cs[par], B_cs[par]
                r0 = t * 128
                P.op("sp", lambda e, xsp=xsp, s=s, r0=r0: e.dma_start(out=xsp, in_=x[s, r0:r0 + 128, :]),
                     w=[bxs], dma=True)
                P.op("sp", lambda e, csp=csp, r0=r0: e.dma_start(out=csp, in_=rope_d[r0:r0 + 128, :]),
                     w=[bcs], dma=True)
                P.op("act", lambda e, xsp=xsp: e.activation(out=junk, in_=xsp, func=AF.Square, scale=1.0 / 32.0,
                                                            accum_out=st4[:, 0:1]),
                     r=[bxs], w=[B_junk], pw=[B_st])
                P.op("dve", lambda e, xsp=xsp: e.tensor_copy(out=xb, in_=xsp), r=[bxs], w=[B_xb])
                rstd(st4[:, 1:2], st4[:, 0:1], 1.0, 1, [B_st], [B_st])
                P.op("pe", [lambda e, c=c: e.transpose(out=PSB(0, 128, off=c * 128), in_=xb[:, c * 128:(c + 1) * 128],
                                                       identity=ident_b) for c in range(8)],
                     r=[B_xb, B_ident], w=[pbank[0]])
                P.op("act", lambda e: e.copy(out=xT.rearrange("p a b -> p (a b)"), in_=PSB(0)), r=[pbank[0]], w=[B_xT])
                fns = []
                for j in range(5):
                    a, b_ = j * 512, min((j + 1) * 512, IN_COLS)
                    for c in range(8):
                        fns.append(lambda e, j=j, a=a, b_=b_, c=c: e.matmul(
                            out=PS(1 + j, b_ - a), lhsT=xT[:, c, :], rhs=w_in_b[:, c, a:b_],
                            start=(c == 0), stop=(c == 7)))
                P.op("pe", fns, r=[B_xT, B_w], w=[pbank[1], pbank[2], pbank[3], pbank[4], pbank[5]])
                for j in range(5):
                    a, b_ = j * 512, min((j + 1) * 512, IN_COLS)
                    P.op("act", lambda e, j=j, a=a, b_=b_: e.activation(
                        out=proj[:, a:b_], in_=PS(1 + j, b_ - a), func=AF.Copy, scale=st4[:, 1:2]),
                        r=[pbank[1 + j], B_st], pw=[B_proj])
                P.op("sp", lambda e, s=s, r0=r0: e.dma_start(out=z_d[s, 1 + r0:1 + r0 + 128, :],
                                                             in_=proj[:, OFF_RWKV:IN_COLS]),
                     r=[B_proj], w=[zt[s][t + 1]], dma=True)
                P.op("act", lambda e: e.activation(out=junk[:, 0:384], in_=proj[:, 0:384], func=AF.Square,
                                                   scale=float(384 ** -0.5), accum_out=st4[:, 2:3]),
                     r=[B_proj], w=[B_junk], pw=[B_st])
                P.op("act", lambda e: e.activation(out=junk[:, 0:256], in_=proj[:, 384:640], func=AF.Square,
                                                   scale=float(256 ** -0.5), accum_out=st4[:, 3:4]),
                     r=[B_proj], w=[B_junk], pw=[B_st])
                rstd(st4[:, 4:6], st4[:, 2:4], 1.0, 2, [B_st], [B_st])
                P.op("dve", lambda e: e.tensor_scalar(out=latb[:, 0:384], in0=proj[:, 0:384], scalar1=st4[:, 4:5],
                                                      scalar2=None, op0=ALU.mult), r=[B_proj, B_st], w=[B_latb])
                P.op("dve", lambda e: e.tensor_scalar(out=latb[:, 384:640], in0=proj[:, 384:640], scalar1=st4[:, 5:6],
                                                      scalar2=None, op0=ALU.mult), r=[B_proj, B_st], pw=[B_latb])
                P.op("pe", [lambda e, c=c: e.transpose(out=PSB(0, 128, off=c * 128), in_=latb[:, c * 128:(c + 1) * 128],
                                                       identity=ident_b) for c in range(5)],
                     r=[B_latb, B_ident], w=[pbank[0]])
                P.op("act", lambda e: e.copy(out=latT.rearrange("p a b -> p (a b)"), in_=PSB(0, 640)),
                     r=[pbank[0]], w=[B_latT])
                fns = []
                for (bk, a, b_) in ((6, 0, 512), (7, 512, 768)):
                    for c in range(3):
                        fns.append(lambda e, bk=bk, a=a, b_=b_, c=c: e.matmul(
                            out=PS(bk, b_ - a), lhsT=latT[:, c, :], rhs=w_uq_b[:, c, a:b_],
                            start=(c == 0), stop=(c == 2)))
                P.op("pe", fns, r=[B_latT, B_w], w=[pbank[6], pbank[7]])
                fns = []
                for (bk, a, b_) in ((1, 0, 512), (2, 512, 1024)):
                    for c in range(2):
                        fns.append(lambda e, bk=bk, a=a, b_=b_, c=c: e.matmul(
                            out=PS(bk, 512), lhsT=latT[:, 3 + c, :], rhs=w_ukv_b[:, c, a:b_],
                            start=(c == 0), stop=(c == 1)))
                P.op("pe", fns, r=[B_latT, B_w], w=[pbank[1], pbank[2]])
                qf = q_sb.rearrange("p a b -> p (a b)")
                P.op("act", lambda e: e.copy(out=qf[:, 0:512], in_=PS(6)), r=[pbank[6]], w=[B_q])
                P.op("act", lambda e: e.copy(out=qf[:, 512:768], in_=PS(7, 256)), r=[pbank[7]], pw=[B_q])
                for hh in range(2):
                    kvv = PS(1 + hh).rearrange("p (h d) -> p h d", d=128)
                    import os as _os
                    P.op(_os.environ.get("K_E67", "act"), lambda e, hh=hh, kvv=kvv: (e.tensor_copy if _os.environ.get("K_E67", "act") == "dve" else e.copy)(out=k_sb[:, hh * 4:(hh + 1) * 4, 0:64],
                                                                        in_=kvv[:, :, 0:64]),
                         r=[pbank[1 + hh]], pw=[B_k] if hh else [], w=[] if hh else [B_k])
                    P.op("act", lambda e, hh=hh, kvv=kvv: e.copy(out=vb[:, hh * 4:(hh + 1) * 4, 0:64],
                                                                 in_=kvv[:, :, 64:128]),
                         r=[pbank[1 + hh]], pw=[B_vb])
                P.op("dve", lambda e: e.tensor_copy(out=k_sb[:, :, 64:96],
                                                    in_=bc(proj[:, OFF_KR:OFF_RWKV].unsqueeze(1), [128, 8, 32])),
                     r=[B_proj], pw=[B_k])
                P.op("sp", lambda e, s=s, r0=r0: e.dma_start(out=v_d[s, r0:r0 + 128, :],
                                                             in_=vb.rearrange("p a b -> p (a b)")),
                     r=[B_vb], pw=[qkv_t[s][t]], dma=True)
                for (tsb, Bt, sqt, Bsq, ho, Bh, gb, ob, Bo, ro, Brt, eng) in (
                        (q_sb, B_q, sq, B_sq, 0, B_hq, gq_b, qb, B_qb, 0, B_rtq, "dve"),
                        (k_sb, B_k, sq2, B_sq2, 16, B_hk, gk_b, kb, B_kb, 3, B_rtk, "pool")):
                    TT = lambda e, **kw: e.tensor_tensor(**kw)
                    P.op(eng, lambda e, tsb=tsb, sqt=sqt: e.tensor_tensor(out=sqt, in0=tsb, in1=tsb, op=ALU.mult),
                         r=[Bt], w=[Bsq])
                    P.op("dve", lambda e, sqt=sqt, ho=ho: e.tensor_reduce(out=hst[:, ho:ho + 8], in_=sqt, axis=AX.X,
                                                                         op=ALU.add), r=[Bsq], w=[Bh])
                    rstd(hst[:, ho + 8:ho + 16], hst[:, ho:ho + 8], 1.0 / 96.0, 8, [Bh], [Bh])
                    P.op(eng, lambda e, tsb=tsb, ho=ho: e.tensor_tensor(
                        out=tsb, in0=tsb, in1=bc(hst[:, ho + 8:ho + 16].unsqueeze(2), [128, 8, 96]), op=ALU.mult),
                        r=[Bt, Bh], w=[Bt])
                    P.op(eng, lambda e, tsb=tsb, gb=gb: e.tensor_tensor(
                        out=tsb, in0=tsb, in1=bc(gb.unsqueeze(1), [128, 8, 96]), op=ALU.mult),
                        r=[Bt, B_gqk], w=[Bt])
                    cosb = bc(csp[:, 0:16].unsqueeze(1), [128, 8, 16])
                    sinb = bc(csp[:, 16:32].unsqueeze(1), [128, 8, 16])
                    x1 = tsb[:, :, 64:80]
                    x2 = tsb[:, :, 80:96]
                    P.op(eng, lambda e, x1=x1, cosb=cosb, ro=ro: e.tensor_tensor(out=rt[:, :, ro, :], in0=x1, in1=cosb,
                                                                               op=ALU.mult), r=[Bt, bcs], w=[Brt])
                    P.op(eng, lambda e, x2=x2, sinb=sinb, ro=ro: e.tensor_tensor(out=rt[:, :, ro + 1, :], in0=x2,
                                                                                in1=sinb, op=ALU.mult),
                         r=[Bt, bcs], pw=[Brt])
                    P.op(eng, lambda e, ob=ob, ro=ro: e.tensor_tensor(out=ob[:, :, 64:80], in0=rt[:, :, ro, :],
                                                                     in1=rt[:, :, ro + 1, :], op=ALU.subtract),
                         r=[Brt], w=[Bo])
                    P.op(eng, lambda e, x1=x1, sinb=sinb, ro=ro: e.tensor_tensor(out=rt[:, :, ro, :], in0=x1, in1=sinb,
                                                                                op=ALU.mult), r=[Bt, bcs], w=[Brt])
                    P.op(eng, lambda e, x2=x2, cosb=cosb, ro=ro: e.tensor_tensor(out=rt[:, :, ro + 1, :], in0=x2,
                                                                                in1=cosb, op=ALU.mult),
                         r=[Bt, bcs], pw=[Brt])
                    P.op(eng, lambda e, ob=ob, ro=ro: e.tensor_tensor(out=ob[:, :, 80:96], in0=rt[:, :, ro, :],
                                                                     in1=rt[:, :, ro + 1, :], op=ALU.add),
                         r=[Brt], pw=[Bo])
                    P.op(eng, lambda e, ob=ob, tsb=tsb: e.tensor_copy(out=ob[:, :, 0:64], in_=tsb[:, :, 0:64]),
                         r=[Bt], pw=[Bo])
                for (ob, Bo, pb, dst_sb, Bd, dst_d) in ((qb, B_qb, 6, qT_sb, B_qT, qT_d), (kb, B_kb, 7, kT_sb, B_kT, kT_d)):
                    P.op("pe", [lambda e, h=h, ob=ob, pb=pb: e.transpose(out=PSB(pb, 128, parts=96, off=h * 128),
                                                                       in_=ob[:, h, :], identity=ident_b)
                                for h in range(8)], r=[Bo, B_ident], w=[pbank[pb]])
                    P.op("act", lambda e, pb=pb, dst_sb=dst_sb: e.copy(out=dst_sb[0:96].rearrange("p a b -> p (a b)"),
                                                                      in_=PSB(pb, 1024, parts=96)),
                         r=[pbank[pb]], w=[Bd])
                    P.op("sp", lambda e, dst_sb=dst_sb, dst_d=dst_d, s=s, r0=r0: e.dma_start(
                        out=dst_d[s, :, :, r0:r0 + 128].rearrange("h d t -> d h t"), in_=dst_sb[0:96]),
                        r=[Bd], pw=[qkv_t[s][t]], dma=True)
        P.barrier()

    @_phase(2)
    def _p2():
        ar.reset()
        QG = min(512, SEQ)
        NQG = SEQ // QG
        kT_h = [ar.alloc([SEQ], BF16, parts=96) for _ in range(2)]
        qT_h = [ar.alloc([SEQ], BF16, parts=96) for _ in range(2)]
        B_kq = [Buf("kq0"), Buf("kq1")]
        v_all = ar.alloc([NT, 520], BF16)
        B_vall = Buf("vall")
        pT = [ar.alloc([QG], BF16) for _ in range(3)]
        B_pT = [Buf(f"pT{i}") for i in range(3)]
        oT = ar.alloc([QG], F32, parts=65)
        B_oT = Buf("oT")
        rc = ar.alloc([4], F32)
        B_rc = Buf("rc")
        ao = [ar.alloc([4, 64], F32) for _ in range(2)]
        B_ao = [Buf("ao0"), Buf("ao1")]
        hi = 0
        gi = 0
        si = 0
        for s in range(NS):
            P.op("sp", lambda e, s=s: e.dma_start(out=v_all, in_=v_d[s].rearrange("(c p) f -> p c f", p=128)),
                 r=qkv_t[s], w=[B_vall], dma=True)
            for h in range(8):
                par = hi % 2
                hi += 1
                P.op("sp", lambda e, s=s, h=h, par=par: e.dma_start(out=kT_h[par], in_=kT_d[s, h]),
                     r=qkv_t[s], w=[B_kq[par]], dma=True)
                P.op("sp", lambda e, s=s, h=h, par=par: e.dma_start(out=qT_h[par], in_=qT_d[s, h]),
                     r=qkv_t[s], pw=[B_kq[par]], dma=True)
                for qg in range(NQG):
                    ob = 3 + (gi % 2)
                    gi += 1
                    q_ap = qT_h[par][:, qg * QG:(qg + 1) * QG]
                    steps = []
                    for kc in range(NT):
                        sb = si % 3
                        si += 1
                        steps.append((kc, sb))

                    def emit_S(kc, sb, par=par, q_ap=q_ap):
                        P.op("pe", lambda e, kc=kc, sb=sb: e.matmul(out=PS(sb, QG), lhsT=kT_h[par][:, kc * 128:(kc + 1) * 128],
                                                                     rhs=q_ap, start=True, stop=True),
                             r=[B_kq[par]], w=[pbank[sb]])

                    def emit_E(kc, sb):
                        P.op("act", lambda e, sb=sb: e.activation(out=pT[sb], in_=PS(sb, QG), func=AF.Exp),
                             r=[pbank[sb]], w=[B_pT[sb]])

                    def emit_PV(kc, sb, ob=ob, h=h):
                        P.op("pe", lambda e, kc=kc, sb=sb: e.matmul(out=PS(ob, QG, parts=65),
                                                                     lhsT=v_all[:, kc, h * 65:(h + 1) * 65], rhs=pT[sb],
                                                                     start=(kc == 0), stop=(kc == NT - 1)),
                             r=[B_pT[sb], B_vall], w=[pbank[ob]] if kc == 0 else [], pw=[] if kc == 0 else [pbank[ob]])

                    emit_S(*steps[0])
                    for i_, (kc, sb) in enumerate(steps):
                        if i_ + 1 < len(steps):
                            emit_S(*steps[i_ + 1])
                        emit_E(kc, sb)
                        emit_PV(kc, sb)
                    P.op("act", lambda e, ob=ob: e.copy(out=oT, in_=PS(ob, QG, parts=65)), r=[pbank[ob]], w=[B_oT])
                    nj = QG // 128
                    P.op("pe", [lambda e, j=j: e.transpose(out=PS(5, 65, off=j * 128), in_=oT[:, j * 128:(j + 1) * 128],
                                                           identity=ident_f[0:65, 0:65]) for j in range(nj)],
                         r=[B_oT, B_ident], w=[pbank[5]])
                    o5 = PS(5).rearrange("p (j d) -> p j d", d=128)
                    P.op("dve", lambda e, o5=o5, nj=nj: e.reciprocal(out=rc[:, 0:nj], in_=o5[:, 0:nj, 64]),
                         r=[pbank[5]], w=[B_rc])
                    ap_ = gi % 2
                    P.op("dve", lambda e, o5=o5, nj=nj, ap_=ap_: e.tensor_tensor(
                        out=ao[ap_][:, 0:nj, :], in0=o5[:, 0:nj, 0:64],
                        in1=bc(rc[:, 0:nj].unsqueeze(2), [128, nj, 64]), op=ALU.mult),
                        r=[pbank[5], B_rc], w=[B_ao[ap_]])
                    t0 = qg * nj
                    P.op("sp", lambda e, s=s, h=h, qg=qg, nj=nj, ap_=ap_: e.dma_start(
                        out=attn_d[s, qg * QG:(qg + 1) * QG, h * 64:(h + 1) * 64].rearrange("(j p) d -> p j d", p=128),
                        in_=ao[ap_][:, 0:nj, :]),
                        r=[B_ao[ap_]], pw=[attn_t[s][t0 + j] for j in range(nj)], dma=True)
        P.barrier()

    @_phase(3)
    def _p3():
        ar.reset()
        CE = float(np.exp(-0.5))
        mup = ar.alloc([RWKV_IN], F32)
        mun = ar.alloc([RWKV_IN], F32)
        kkb = ar.alloc([512], F32)
        kab = ar.alloc([512], F32)
        rkb = ar.alloc([512], F32)
        lgb = ar.alloc([512], F32)
        lbb = ar.alloc([512], F32)
        w0b = ar.alloc([2, 512], F32)
        a0b = ar.alloc([2, 512], F32)
        wup = ar.alloc([2, 512], BF16)
        aup = ar.alloc([2, 512], BF16)
        gup = ar.alloc([512], BF16)
        cm = ar.alloc([2, 896], F32)
        negc = ar.alloc([2], F32)
        B_c = Buf("p3c")
        first = [True]

        def cload(dst, src):
            P.op("sp", lambda e: e.dma_start(out=dst, in_=src), w=[B_c] if first[0] else [], pw=[] if first[0] else [B_c],
                 dma=True)
            first[0] = False

        cload(mup, bc(mu_prev.unsqueeze(0), [128, RWKV_IN]))
        cload(mun, bc(mu_next.unsqueeze(0), [128, RWKV_IN]))
        for (dst, src) in ((kkb, k_k), (kab, k_a), (rkb, r_k), (lgb, ln_x_g), (lbb, ln_x_b)):
            cload(dst, bc(src.unsqueeze(0), [128, 512]))
        for d in range(2):
            cload(w0b[:, d, :], bc(w0[d:d + 1, :], [128, 512]))
            cload(a0b[:, d, :], bc(a0[d:d + 1, :], [128, 512]))
        cload(cm.rearrange("p a b -> p (a b)"), cmask_d)
        stg = ar.alloc([1024], F32)
        B_stg = Buf("stg")
        for (dst, src) in ((wup, w_up), (aup, a_up)):
            P.op("sp", lambda e, src=src: e.dma_start(out=stg[0:64].rearrange("p (a b) -> p a b", b=512),
                                                      in_=src.rearrange("d j c -> j d c")), w=[B_stg], dma=True)
            P.op("dve", lambda e, dst=dst: e.tensor_copy(out=dst[0:64].rearrange("p a b -> p (a b)"), in_=stg[0:64]),
                 r=[B_stg], pw=[B_c])
        P.op("sp", lambda e: e.dma_start(out=stg[:, 0:512], in_=g_up), w=[B_stg], dma=True)
        P.op("dve", lambda e: e.tensor_copy(out=gup, in_=stg[:, 0:512]), r=[B_stg], pw=[B_c])
        P.op("dve", lambda e: e.memset(negc, -CE), pw=[B_c])

        yd_d = [dscr(f"yd{d}_s", [NS, SEQ, 520]) for d in range(2)]
        gv_d = dscr("gv_s", [NS, SEQ, 1024])
        yd_t = [[[Buf(f"yd{d}_{s}_{t}") for t in range(NT)] for s in range(NS)] for d in range(2)]
        gv_t = [[Buf(f"gv{s}_{t}") for t in range(NT)] for s in range(NS)]

        _pb3 = [Buf(f"p3b{b}", excl=True) for b in range(8)]
        pq = [[_pb3[b]] * 4 for b in range(8)]

        def PQ(b, q, n=128, parts=128):
            a = psum[:, b * 512 + q * 128:b * 512 + q * 128 + n]
            return a if parts == 128 else a[0:parts]

        zc = ar.alloc([RWKV_IN], F32)
        zp = ar.alloc([RWKV_IN], F32)
        zn = ar.alloc([RWKV_IN], F32)
        B_zc, B_zp, B_zn = Buf("zc"), Buf("zp"), Buf("zn")
        lat = ar.alloc([256], BF16)
        latT = ar.alloc([384], BF16)
        B_lat, B_latT = Buf("lat"), Buf("latT")
        names = ["tmp", "sg", "a", "E1", "E2", "E3", "Ex", "kk", "sq", "kd", "b", "t2", "g"]
        W = {n_: ar.alloc([512], F32) for n_ in names}
        BW = {n_: Buf("w_" + n_) for n_ in names}
        bnames = ["rbar", "abar", "bbar", "kbar", "Bt", "Kt", "vq"]
        WB = {n_: ar.alloc([512], BF16) for n_ in bnames}
        BWB = {n_: Buf("wb_" + n_) for n_ in bnames}
        ytile = ar.alloc([520], F32)
        B_y = Buf("ytile")
        s8 = ar.alloc([16], F32)
        B_s8 = Buf("s8")
        gC = [[ar.alloc([8], F32, parts=64) for d in range(2)] for s in range(NS)]
        B_gC = [[Buf(f"gC{s}{d}") for d in range(2)] for s in range(NS)]
        ST = [[[ar.alloc([64], F32, parts=64) for h in range(8)] for d in range(2)] for s in range(NS)]
        STb = [[[ar.alloc([64], BF16, parts=64) for h in range(8)] for d in range(2)] for s in range(NS)]
        B_ST = [[[Buf(f"ST{s}{d}{h}") for h in range(8)] for d in range(2)] for s in range(NS)]
        B_STb = [[[Buf(f"STb{s}{d}{h}") for h in range(8)] for d in range(2)] for s in range(NS)]
        TT = [ar.alloc([512], BF16, parts=64) for h in range(8)]
        MM = [ar.alloc([512], BF16) for h in range(8)]
        Zf = [ar.alloc([128], F32) for h in range(8)]
        Zb = [ar.alloc([128], BF16) for h in range(8)]
        WbT = [ar.alloc([128], BF16, parts=64) for h in range(8)]
        Ub = [ar.alloc([64], BF16) for h in range(8)]
        PA = [[ar.alloc([128], BF16) for k_ in range(2)] for h in range(8)]
        PB = [[ar.alloc([128], BF16) for k_ in range(2)] for h in range(8)]
        B_TT = [Buf(f"TT{h}") for h in range(8)]
        B_MM = [Buf(f"MM{h}") for h in range(8)]
        B_Zf = [Buf(f"Zf{h}") for h in range(8)]
        B_Zb = [Buf(f"Zb{h}") for h in range(8)]
        B_WbT = [Buf(f"WbT{h}") for h in range(8)]
        B_Ub = [Buf(f"Ub{h}") for h in range(8)]
        B_PA = [[Buf(f"PA{h}{k_}") for k_ in range(2)] for h in range(8)]
        B_PB = [[Buf(f"PB{h}{k_}") for k_ in range(2)] for h in range(8)]

        for s in range(NS):
            for d in range(2):
                for h in range(8):
                    P.op("pool", lambda e, s=s, d=d, h=h: e.memset(ST[s][d][h], 0.0), w=[B_ST[s][d][h]])
                    P.op("pool", lambda e, s=s, d=d, h=h: e.memset(STb[s][d][h], 0.0), w=[B_STb[s][d][h]])

        qrot = [0]

        def qslot():
            q = qrot[0] % 4
            qrot[0] += 1
            return q

        def process(s, d, c):
            r0 = c * 128
            P.op("sp", lambda e: e.dma_start(out=zc, in_=z_d[s, 1 + r0:1 + r0 + 128, :]), w=[B_zc], dma=True)
            P.op("sp", lambda e: e.dma_start(out=zp, in_=z_d[s, r0:r0 + 128, :]), w=[B_zp], dma=True)
            P.op("sp", lambda e: e.dma_start(out=zn, in_=z_d[s, 2 + r0:2 + r0 + 128, :]), w=[B_zn], dma=True)
            TTo = lambda eng, o, a_, b_, op, r, w: P.op(eng, lambda e: e.tensor_tensor(out=o, in0=a_, in1=b_, op=op), r=r, w=w)
            TTo("dve", zp, zp, zc, ALU.subtract, [B_zp, B_zc], [B_zp])
            TTo("dve", zp, zp, mup, ALU.mult, [B_zp, B_c], [B_zp])
            TTo("pool", zn, zn, zc, ALU.subtract, [B_zn, B_zc], [B_zn])
            TTo("pool", zn, zn, mun, ALU.mult, [B_zn, B_c], [B_zn])
            TTo("dve", zc, zc, zp, ALU.add, [B_zc, B_zp], [B_zc])
            TTo("dve", zc, zc, zn, ALU.add, [B_zc, B_zn], [B_zc])
            r_ = zc[:, 0:512]
            kr = zc[:, 512:1024]
            vr = zc[:, 1024:1536]
            P.op("act", lambda e: e.activation(out=lat[:, 0:64], in_=zc[:, 1536:1600], func=AF.Tanh), r=[B_zc], w=[B_lat])
            P.op("act", lambda e: e.activation(out=lat[:, 128:256], in_=zc[:, 1664:1792], func=AF.Sigmoid), r=[B_zc],
                 pw=[B_lat])
            P.op("dve", lambda e: e.tensor_copy(out=lat[:, 64:128], in_=zc[:, 1600:1664]), r=[B_zc], pw=[B_lat])
            P.op("pe", [lambda e: e.transpose(out=PSB(0, 128, parts=64, off=0), in_=lat[:, 0:64], identity=ident_b),
                        lambda e: e.transpose(out=PSB(0, 128, parts=64, off=128), in_=lat[:, 64:128], identity=ident_b),
                        lambda e: e.transpose(out=PSB(0, 128, off=256), in_=lat[:, 128:256], identity=ident_b)],
                 r=[B_lat, B_ident], w=pq[0])
            P.op("act", lambda e: e.copy(out=latT[0:64, 0:256], in_=PSB(0, 256, parts=64)), r=pq[0], w=[B_latT])
            P.op("act", lambda e: e.copy(out=latT[:, 256:384], in_=PSB(0, 128, off=256)), r=pq[0], pw=[B_latT])
            P.op("pe", lambda e: e.matmul(out=PS(2), lhsT=latT[0:64, 0:128], rhs=wup[0:64, d, :], start=True, stop=True),
                 r=[B_latT, B_c], w=pq[2])
            P.op("pe", lambda e: e.matmul(out=PS(3), lhsT=latT[0:64, 128:256], rhs=aup[0:64, d, :], start=True, stop=True),
                 r=[B_latT, B_c], w=pq[3])
            P.op("dve", lambda e: e.tensor_tensor(out=W["tmp"], in0=PS(2), in1=w0b[:, d, :], op=ALU.add),
                 r=pq[2] + [B_c], w=[BW["tmp"]])
            P.op("act", lambda e: e.activation(out=W["sg"], in_=W["tmp"], func=AF.Sigmoid), r=[BW["tmp"]], w=[BW["sg"]])
            P.op("dve", lambda e: e.tensor_tensor(out=W["tmp"], in0=PS(3), in1=a0b[:, d, :], op=ALU.add),
                 r=pq[3] + [B_c], w=[BW["tmp"]])
            P.op("act", lambda e: e.activation(out=W["a"], in_=W["tmp"], func=AF.Sigmoid), r=[BW["tmp"]], w=[BW["a"]])
            if d == 0:
                P.op("pe", lambda e: e.matmul(out=PS(1), lhsT=latT[:, 256:384], rhs=gup, start=True, stop=True),
                     r=[B_latT, B_c], w=pq[1])
                P.op("act", lambda e: e.copy(out=W["g"], in_=PS(1)), r=pq[1], w=[BW["g"]])
                P.op("sp", lambda e: e.dma_start(out=gv_d[s, r0:r0 + 128, 0:512], in_=W["g"]), r=[BW["g"]],
                     pw=[gv_t[s][c]], dma=True)
                P.op("sp", lambda e: e.dma_start(out=gv_d[s, r0:r0 + 128, 512:1024], in_=vr), r=[B_zc],
                     pw=[gv_t[s][c]], dma=True)
            P.op("pe", lambda e: e.matmul(out=PS(2), lhsT=cm[:, d, 640:768], rhs=W["sg"], start=True, stop=True),
                 r=[BW["sg"], B_c], w=pq[2])
            P.op("pe", lambda e: e.matmul(out=PS(3), lhsT=cm[:, d, 768:896], rhs=W["sg"], start=True, stop=True),
                 r=[BW["sg"], B_c], w=pq[3])
            gq = qslot()
            P.op("pe", [lambda e, h=h: e.matmul(out=PQ(4, gq, 1, parts=64)[:, 0:1] if False else psum[0:64, 4 * 512 + gq * 128 + h:4 * 512 + gq * 128 + h + 1],
                                                  lhsT=W["sg"][:, h * 64:(h + 1) * 64], rhs=negc[:, 0:1], start=True, stop=True)
                        for h in range(8)], r=[BW["sg"], B_c], w=[pq[4][gq]])
            P.op("act", lambda e: e.activation(out=gC[s][d], in_=PQ(4, gq, 8, parts=64), func=AF.Exp), r=[pq[4][gq]],
                 w=[B_gC[s][d]])
            P.op("act", lambda e: e.activation(out=W["E1"], in_=PS(2), func=AF.Exp), r=pq[2], w=[BW["E1"]])
            P.op("act", lambda e: e.activation(out=W["E2"], in_=PS(2), func=AF.Exp, scale=-1.0), r=pq[2], w=[BW["E2"]])
            P.op("act", lambda e: e.activation(out=W["E3"], in_=PS(3), func=AF.Exp), r=pq[3], w=[BW["E3"]])
            P.op("dve", lambda e: e.scalar_tensor_tensor(out=W["tmp"], in0=W["sg"], scalar=CE, in1=PS(2), op0=ALU.mult,
                                                         op1=ALU.add), r=[BW["sg"]] + pq[2], w=[BW["tmp"]])
            P.op("act", lambda e: e.activation(out=W["Ex"], in_=W["tmp"], func=AF.Exp), r=[BW["tmp"]], w=[BW["Ex"]])
            TTo("dve", W["kk"], kr, kkb, ALU.mult, [B_zc, B_c], [BW["kk"]])
            TTo("pool", W["sq"], W["kk"], W["kk"], ALU.mult, [BW["kk"]], [BW["sq"]])
            P.op("dve", lambda e: e.tensor_reduce(out=s8[:, 0:8], in_=W["sq"].rearrange("p (h k) -> p h k", k=64),
                                                  axis=AX.X, op=ALU.add), r=[BW["sq"]], w=[B_s8])
            rstd(s8[:, 8:16], s8[:, 0:8], 1.0, 8, [B_s8], [B_s8], eps=1e-24)
            P.op("dve", lambda e: e.tensor_tensor(out=W["kk"].rearrange("p (h k) -> p h k", k=64),
                                                  in0=W["kk"].rearrange("p (h k) -> p h k", k=64),
                                                  in1=bc(s8[:, 8:16].unsqueeze(2), [128, 8, 64]), op=ALU.mult),
                 r=[BW["kk"], B_s8], w=[BW["kk"]])
            P.op("dve", lambda e: e.scalar_tensor_tensor(out=W["kd"], in0=W["a"], scalar=-1.0, in1=kab, op0=ALU.add,
                                                         op1=ALU.mult), r=[BW["a"], B_c], w=[BW["kd"]])
            P.op("dve", lambda e: e.scalar_tensor_tensor(out=W["kd"], in0=W["kd"], scalar=1.0, in1=kr, op0=ALU.add,
                                                         op1=ALU.mult), r=[BW["kd"], B_zc], w=[BW["kd"]])
            TTo("pool", W["b"], W["kk"], W["a"], ALU.mult, [BW["kk"], BW["a"]], [BW["b"]])
            TTo("pool", W["t2"], r_, W["kd"], ALU.mult, [B_zc, BW["kd"]], [BW["t2"]])
            TTo("pool", W["t2"], W["t2"], rkb, ALU.mult, [BW["t2"], B_c], [BW["t2"]])
            P.op("dve", lambda e: e.tensor_reduce(out=ytile[:, 512:520], in_=W["t2"].rearrange("p (h k) -> p h k", k=64),
                                                  axis=AX.X, op=ALU.add), r=[BW["t2"]], w=[B_y])
            TTo("dve", WB["rbar"], r_, W["E1"], ALU.mult, [B_zc, BW["E1"]], [BWB["rbar"]])
            P.op("dve", lambda e: e.scalar_tensor_tensor(out=WB["abar"], in0=W["kk"], scalar=-1.0, in1=W["Ex"],
                                                         op0=ALU.mult, op1=ALU.mult), r=[BW["kk"], BW["Ex"]],
                 w=[BWB["abar"]])
            TTo("pool", WB["bbar"], W["b"], W["E2"], ALU.mult, [BW["b"], BW["E2"]], [BWB["bbar"]])
            TTo("dve", WB["kbar"], W["kd"], W["E2"], ALU.mult, [BW["kd"], BW["E2"]], [BWB["kbar"]])
            TTo("pool", WB["Bt"], W["b"], W["E3"], ALU.mult, [BW["b"], BW["E3"]], [BWB["Bt"]])
            TTo("pool", WB["Kt"], W["kd"], W["E3"], ALU.mult, [BW["kd"], BW["E3"]], [BWB["Kt"]])
            P.op("act", lambda e: e.copy(out=WB["vq"], in_=vr), r=[B_zc], w=[BWB["vq"]])

            for grp in range(2):
                hs = [4 * grp + i for i in range(4)]
                for h in hs:
                    tb = h % 2
                    hsl = slice(h * 64, (h + 1) * 64)
                    ttps = [pq[0][2 * tb], pq[0][2 * tb + 1]]
                    P.op("pe", [lambda e, i=i, nm=nm, tb=tb, hsl=hsl: e.transpose(out=PSB(0, 128, parts=64, off=tb * 512 + i * 128),
                                                                   in_=WB[nm][:, hsl], identity=ident_b)
                                for i, nm in enumerate(("abar", "rbar", "bbar", "kbar"))],
                         r=[BWB["abar"], BWB["rbar"], BWB["bbar"], BWB["kbar"], B_ident], w=ttps)
                    P.op("act", lambda e, h=h, tb=tb: e.copy(out=TT[h], in_=PSB(0, 512, parts=64, off=tb * 512)),
                         r=ttps, w=[B_TT[h]])
                    mb = 2 + (h % 2)
                    q3 = qslot()
                    P.op("pe", [lambda e, h=h, mb=mb: e.matmul(out=PS(mb, 256), lhsT=TT[h][:, 384:512], rhs=TT[h][:, 0:256],
                                                               start=True, stop=True),
                                lambda e, h=h, mb=mb: e.matmul(out=PS(mb, 256, off=256), lhsT=TT[h][:, 256:384],
                                                               rhs=TT[h][:, 0:256], start=True, stop=True),
                                lambda e, h=h, q3=q3: e.matmul(out=PQ(4, q3), lhsT=TT[h][:, 0:128], rhs=TT[h][:, 256:384],
                                                               start=True, stop=True)],
                         r=[B_TT[h]], w=pq[mb] + [pq[4][q3]])
                    P.op("dve", lambda e, h=h, mb=mb: e.tensor_tensor(out=MM[h], in0=PS(mb), in1=cm[:, d, 0:512],
                                                                      op=ALU.mult), r=pq[mb] + [B_c], w=[B_MM[h]])
                    P.op("dve", lambda e, h=h, q3=q3: e.tensor_tensor(out=PA[h][0], in0=PQ(4, q3), in1=cm[:, d, 512:640],
                                                                      op=ALU.mult), r=[pq[4][q3], B_c], w=[B_PA[h][0]])
                    q4 = qslot()
                    P.op("pe", lambda e, h=h, q4=q4, hsl=hsl: e.matmul(out=PQ(4, q4, 64), lhsT=MM[h][:, 0:128],
                                                                       rhs=WB["vq"][:, hsl], start=True, stop=True),
                         r=[B_MM[h], BWB["vq"]], w=[pq[4][q4]])
                    P.op("pool", lambda e, h=h, hsl=hsl: e.tensor_copy(out=Zf[h][:, 0:64], in_=WB["abar"][:, hsl]),
                         r=[BWB["abar"]], w=[B_Zf[h]])
                    P.op("pool", lambda e, h=h, hsl=hsl: e.tensor_copy(out=Zb[h][:, 0:64], in_=WB["abar"][:, hsl]),
                         r=[BWB["abar"]], w=[B_Zb[h]])
                    P.op("act", lambda e, h=h, q4=q4: e.copy(out=Zf[h][:, 64:128], in_=PQ(4, q4, 64)), r=[pq[4][q4]],
                         pw=[B_Zf[h]])
                    P.op("act", lambda e, h=h, q4=q4: e.copy(out=Zb[h][:, 64:128], in_=PQ(4, q4, 64)), r=[pq[4][q4]],
                         pw=[B_Zb[h]])
                cur = {h: 0 for h in hs}
                for j in range(7):
                    for i, h in enumerate(hs):
                        Bp = MM[h][:, 256:384] if j == 0 else PB[h][cur[h]]
                        Bb = B_MM[h] if j == 0 else B_PB[h][cur[h]]
                        P.op("pe", lambda e, h=h, i=i, Bp=Bp: e.matmul(out=PQ(7, i), lhsT=Bp, rhs=Zb[h], start=True, stop=True),
                             r=[Bb, B_Zb[h]], w=[pq[7][i]])
                    if j < 6:
                        for i, h in enumerate(hs):
                            Bp = MM[h][:, 256:384] if j == 0 else PB[h][cur[h]]
                            Bb = B_MM[h] if j == 0 else B_PB[h][cur[h]]
                            Ap = PA[h][cur[h]]
                            Ab = B_PA[h][cur[h]]
                            bk, qq = 5 + i // 2, (i % 2) * 2
                            P.op("pe", lambda e, Ap=Ap, Bp=Bp, bk=bk, qq=qq: e.matmul(out=PQ(bk, qq), lhsT=Ap, rhs=Bp,
                                                                                       start=True, stop=True),
                                 r=[Ab, Bb], w=[pq[bk][qq]])
                            if j < 5:
                                P.op("pe", lambda e, Ap=Ap, Bp=Bp, bk=bk, qq=qq: e.matmul(out=PQ(bk, qq + 1), lhsT=Bp, rhs=Ap,
                                                                                           start=True, stop=True),
                                     r=[Ab, Bb], w=[pq[bk][qq + 1]])
                    for i, h in enumerate(hs):
                        P.op("dve", lambda e, h=h, i=i: e.tensor_tensor(out=Zf[h], in0=PQ(7, i), in1=Zf[h], op=ALU.add),
                             r=[pq[7][i], B_Zf[h]], w=[B_Zf[h]])
                        P.op("pool", lambda e, h=h: e.tensor_copy(out=Zb[h], in_=Zf[h]), r=[B_Zf[h]], w=[B_Zb[h]])
                    if j < 6:
                        for i, h in enumerate(hs):
                            nx = 1 - cur[h]
                            bk, qq = 5 + i // 2, (i % 2) * 2
                            P.op("act", lambda e, h=h, nx=nx, bk=bk, qq=qq: e.copy(out=PB[h][nx], in_=PQ(bk, qq)),
                                 r=[pq[bk][qq]], w=[B_PB[h][nx]])
                            if j < 5:
                                P.op("act" if i % 2 else "dve",
                                     (lambda e, h=h, nx=nx, bk=bk, qq=qq: e.copy(out=PA[h][nx], in_=PQ(bk, qq + 1))) if i % 2 else
                                     (lambda e, h=h, nx=nx, bk=bk, qq=qq: e.tensor_scalar(out=PA[h][nx], in0=PQ(bk, qq + 1),
                                                                                            scalar1=1.0, scalar2=None,
                                                                                            op0=ALU.mult)),
                                     r=[pq[bk][qq + 1]], w=[B_PA[h][nx]])
                            cur[h] = nx
                for h in hs:
                    hsl = slice(h * 64, (h + 1) * 64)
                    q5 = qslot()
                    P.op("pe", lambda e, h=h, q5=q5: e.transpose(
                        out=psum[0:64, 4 * 512 + q5 * 128:4 * 512 + q5 * 128 + 64].bitcast(BF16),
                        in_=Zb[h][:, 0:64], identity=ident_b), r=[B_Zb[h], B_ident], w=[pq[4][q5]])
                    P.op("act", lambda e, h=h, q5=q5: e.copy(
                        out=WbT[h], in_=psum[0:64, 4 * 512 + q5 * 128:4 * 512 + q5 * 128 + 64].bitcast(BF16)),
                        r=[pq[4][q5]], w=[B_WbT[h]])
                    q6 = qslot()
                    P.op("pe", lambda e, h=h, q6=q6: e.matmul(out=PQ(4, q6, 64), lhsT=WbT[h], rhs=STb[s][d][h], start=True,
                                                              stop=True), r=[B_WbT[h], B_STb[s][d][h]], w=[pq[4][q6]])
                    P.op("dve", lambda e, h=h, q6=q6: e.tensor_tensor(out=Ub[h], in0=PQ(4, q6, 64), in1=Zf[h][:, 64:128],
                                                                      op=ALU.add), r=[pq[4][q6], B_Zf[h]], w=[B_Ub[h]])
                    yfirst = (h == 0)
                    P.op("pe", [lambda e, h=h, hsl=hsl: e.matmul(out=PS(1, 64, off=h * 64), lhsT=TT[h][:, 128:256],
                                                                 rhs=STb[s][d][h], start=True, stop=False),
                                lambda e, h=h, hsl=hsl: e.matmul(out=PS(1, 64, off=h * 64), lhsT=MM[h][:, 384:512], rhs=Ub[h],
                                                                 start=False, stop=False),
                                lambda e, h=h, hsl=hsl: e.matmul(out=PS(1, 64, off=h * 64), lhsT=MM[h][:, 128:256],
                                                                 rhs=WB["vq"][:, hsl], start=False, stop=True)],
                         r=[B_TT[h], B_STb[s][d][h], B_MM[h], B_Ub[h], BWB["vq"]],
                         w=pq[1] if yfirst else [], pw=[] if yfirst else pq[1])
                    q7 = qslot()
                    P.op("pe", [lambda e, h=h, q7=q7, hsl=hsl: e.matmul(out=PQ(4, q7, 64, parts=64), lhsT=WB["Bt"][:, hsl],
                                                                        rhs=Ub[h], start=True, stop=False),
                                lambda e, h=h, q7=q7, hsl=hsl: e.matmul(out=PQ(4, q7, 64, parts=64), lhsT=WB["Kt"][:, hsl],
                                                                        rhs=WB["vq"][:, hsl], start=False, stop=True)],
                         r=[BWB["Bt"], BWB["Kt"], B_Ub[h], BWB["vq"]], w=[pq[4][q7]])
                    P.op("dve", lambda e, h=h, q7=q7: e.scalar_tensor_tensor(
                        out=ST[s][d][h], in0=ST[s][d][h], scalar=gC[s][d][:, h:h + 1], in1=PQ(4, q7, 64, parts=64),
                        op0=ALU.mult, op1=ALU.add), r=[B_ST[s][d][h], B_gC[s][d], pq[4][q7]], w=[B_ST[s][d][h]])
                    P.op("pool", lambda e, h=h: e.tensor_copy(out=STb[s][d][h], in_=ST[s][d][h]), r=[B_ST[s][d][h]],
                         w=[B_STb[s][d][h]])
            P.op("act", lambda e: e.copy(out=ytile[:, 0:512], in_=PS(1)), r=pq[1], pw=[B_y])
            P.op("sp", lambda e: e.dma_start(out=yd_d[d][s, r0:r0 + 128, :], in_=ytile), r=[B_y], w=[yd_t[d][s][c]],
                 dma=True)

        for i in range(NT):
            for s in range(NS):
                process(s, 0, i)
                process(s, 1, NT - 1 - i)

        yf = zc[:, 0:520]
        yb = zp[:, 0:520]
        gvt = zn[:, 0:1024]
        for s in range(NS):
            for c in range(NT):
                r0 = c * 128
                P.op("sp", lambda e, s=s, r0=r0: e.dma_start(out=yf, in_=yd_d[0][s, r0:r0 + 128, :]), r=[yd_t[0][s][c]],
                     w=[B_zc], dma=True)
                P.op("sp", lambda e, s=s, r0=r0: e.dma_start(out=yb, in_=yd_d[1][s, r0:r0 + 128, :]), r=[yd_t[1][s][c]],
                     w=[B_zp], dma=True)
                P.op("sp", lambda e, s=s, r0=r0: e.dma_start(out=gvt, in_=gv_d[s, r0:r0 + 128, :]), r=[gv_t[s][c]],
                     w=[B_zn], dma=True)
                y3 = yf[:, 0:512].rearrange("p (h k) -> p h k", k=64)
                P.op("dve", lambda e: e.tensor_tensor(out=yf, in0=yf, in1=yb, op=ALU.add), r=[B_zc, B_zp], w=[B_zc])
                P.op("dve", lambda e: e.tensor_reduce(out=s8[:, 0:8], in_=y3, axis=AX.X, op=ALU.add), r=[B_zc], w=[B_s8])
                P.op("dve", lambda e: e.tensor_scalar(out=s8[:, 0:8], in0=s8[:, 0:8], scalar1=1.0 / 64.0, scalar2=None,
                                                      op0=ALU.mult), r=[B_s8], w=[B_s8])
                P.op("dve", lambda e: e.tensor_tensor(out=y3, in0=y3, in1=bc(s8[:, 0:8].unsqueeze(2), [128, 8, 64]),
                                                      op=ALU.subtract), r=[B_zc, B_s8], w=[B_zc])
                P.op("pool", lambda e: e.tensor_tensor(out=W["sq"], in0=yf[:, 0:512], in1=yf[:, 0:512], op=ALU.mult),
                     r=[B_zc], w=[BW["sq"]])
                P.op("dve", lambda e: e.tensor_reduce(out=s8[:, 0:8], in_=W["sq"].rearrange("p (h k) -> p h k", k=64),
                                                      axis=AX.X, op=ALU.add), r=[BW["sq"]], w=[B_s8])
                rstd(s8[:, 8:16], s8[:, 0:8], 1.0 / 64.0, 8, [B_s8], [B_s8], eps=GN_EPS)
                P.op("dve", lambda e: e.tensor_tensor(out=y3, in0=y3, in1=bc(s8[:, 8:16].unsqueeze(2), [128, 8, 64]),
                                                      op=ALU.mult), r=[B_zc, B_s8], w=[B_zc])
                P.op("dve", lambda e: e.tensor_tensor(out=yf[:, 0:512], in0=yf[:, 0:512], in1=lgb, op=ALU.mult),
                     r=[B_zc, B_c], w=[B_zc])
                P.op("dve", lambda e: e.tensor_tensor(out=yf[:, 0:512], in0=yf[:, 0:512], in1=lbb, op=ALU.add),
                     r=[B_zc, B_c], w=[B_zc])
                P.op("pool", lambda e: e.scalar_tensor_tensor(
                    out=W["t2"].rearrange("p (h k) -> p h k", k=64), in0=gvt[:, 512:1024].rearrange("p (h k) -> p h k", k=64),
                    scalar=0.5, in1=bc(yf[:, 512:520].unsqueeze(2), [128, 8, 64]), op0=ALU.mult, op1=ALU.mult)
                    if False else e.tensor_tensor(
                    out=W["t2"].rearrange("p (h k) -> p h k", k=64), in0=gvt[:, 512:1024].rearrange("p (h k) -> p h k", k=64),
                    in1=bc(yf[:, 512:520].unsqueeze(2), [128, 8, 64]), op=ALU.mult), r=[B_zc, B_zn], w=[BW["t2"]])
                P.op("dve", lambda e: e.scalar_tensor_tensor(out=yf[:, 0:512], in0=W["t2"], scalar=0.5, in1=yf[:, 0:512],
                                                             op0=ALU.mult, op1=ALU.add), r=[BW["t2"], B_zc], w=[B_zc])
                P.op("dve", lambda e: e.tensor_tensor(out=yf[:, 0:512], in0=yf[:, 0:512], in1=gvt[:, 0:512], op=ALU.mult),
                     r=[B_zc, B_zn], w=[B_zc])
                P.op("sp", lambda e, s=s, r0=r0: e.dma_start(out=rw_d[s, r0:r0 + 128, :], in_=yf[:, 0:512]), r=[B_zc],
                     w=[rw_t[s][c]], dma=True)
        P.barrier()

    @_phase(4)
    def _p4():
        ar.reset()
        w_out_b = ar.alloc([8, D], BF16)
        Ws_b = ar.alloc([8, 2048], BF16)
        gT2 = ar.alloc([4], F32)
        g2_b = ar.alloc([D], F32)
        NG = 8
        gsl = ar.alloc([NG, D], F32)
        B_g = [Buf(f"gs{k_}") for k_ in range(NG)]
        stage = [gsl[:, 0:2, :].rearrange("p a b -> p (a b)"), gsl[:, 2:4, :].rearrange("p a b -> p (a b)")]
        B_stage = [[B_g[0], B_g[1]], [B_g[2], B_g[3]]]
        B_w = Buf("w4")
        B_ws = Buf("ws")
        B_gT = Buf("gT4")
        B_g2 = Buf("g2")
        load_colT(gT2[:, 0:4], attn_out_g, 4, B_gT)
        P.op("sp", lambda e: e.dma_start(out=g2_b, in_=bc(norm2_g.unsqueeze(0), [128, D])), w=[B_g2], dma=True)
        for c in range(8):
            st, bs = stage[c % 2], B_stage[c % 2]
            P.op("sp", lambda e, st=st, c=c: e.dma_start(out=st[:, 0:D], in_=w_out[c * 128:(c + 1) * 128, :]), w=bs, dma=True)
            if c < 4:
                P.op("dve", lambda e, st=st, c=c: e.tensor_scalar(out=w_out_b[:, c, :], in0=st[:, 0:D], scalar1=gT2[:, c:c + 1],
                                                                  scalar2=None, op0=ALU.mult), r=bs + [B_gT], pw=[B_w])
            else:
                P.op("dve", lambda e, st=st, c=c: e.tensor_copy(out=w_out_b[:, c, :], in_=st[:, 0:D]), r=bs, pw=[B_w])
        skT = ar.alloc([2, 128], F32)
        B_sk = Buf("skT")
        wT4 = ar.alloc([4, 128], F32)
        B_wT4 = Buf("wT4")
        for hf in range(2):
            P.op("sp", lambda e, hf=hf: e.dma_start(out=stage[0][:, hf * 128:(hf + 1) * 128], in_=sub_keys[hf]),
                 w=B_stage[0] if hf == 0 else [], pw=[] if hf == 0 else B_stage[0], dma=True)
        P.op("pe", [lambda e, hf=hf: e.transpose(out=PS(0, 128, off=hf * 128), in_=stage[0][:, hf * 128:(hf + 1) * 128],
                                                 identity=ident_f) for hf in range(2)], r=B_stage[0] + [B_ident], w=[pbank[0]])
        P.op("act", lambda e: e.copy(out=skT.rearrange("p a b -> p (a b)"), in_=PS(0, 256)), r=[pbank[0]], w=[B_sk])
        for dc in range(8):
            st, bs = stage[(dc + 1) % 2], B_stage[(dc + 1) % 2]
            P.op("sp", lambda e, st=st, dc=dc: e.dma_start(out=st, in_=w_pq[dc * 128:(dc + 1) * 128, :]), w=bs, dma=True)
            for g4 in range(4):
                P.op("pe", [lambda e, st=st, jj=jj, g4=g4: e.transpose(
                    out=PS(1, 128, off=jj * 128), in_=st[:, (g4 * 4 + jj) * 128:(g4 * 4 + jj + 1) * 128], identity=ident_f)
                    for jj in range(4)], r=bs + [B_ident], w=[pbank[1]])
                P.op("act", lambda e: e.copy(out=wT4.rearrange("p a b -> p (a b)"), in_=PS(1)), r=[pbank[1]], w=[B_wT4])
                P.op("pe", [lambda e, jj=jj: e.matmul(out=PS(2, 128, off=jj * 128), lhsT=wT4[:, jj, :], rhs=skT[:, jj % 2, :],
                                                      start=True, stop=True) for jj in range(4)],
                     r=[B_wT4, B_sk], w=[pbank[2]])
                P.op("act", lambda e, dc=dc, g4=g4: e.copy(out=Ws_b[:, dc, g4 * 512:(g4 + 1) * 512], in_=PS(2)),
                     r=[pbank[2]], pw=[B_ws])

        xs = [ar.alloc([D], F32) for _ in range(2)]
        B_xs = [Buf("xs0"), Buf("xs1")]
        at = ar.alloc([D], F32)
        B_at = Buf("at")
        st4 = ar.alloc([8], F32)
        B_st = Buf("st4")
        catb = ar.alloc([D], BF16)
        B_catb = Buf("catb")
        catT = ar.alloc([8, 128], BF16)
        B_catT = Buf("catT")
        h_sb = [ar.alloc([D], F32) for _ in range(2)]
        B_h = [Buf("h0"), Buf("h1")]
        hn = ar.alloc([D], F32)
        B_hn = Buf("hn")
        s_sb = ar.alloc([16, 128], F32)
        B_s = Buf("s_sb")
        scr2 = ar.alloc([2048], F32)
        B_scr2 = Buf("scr2")
        cand = ar.alloc([8, 256], F32)
        B_cand = Buf("cand")
        eq = ar.alloc([8, 16, 16], F32)
        B_eq = Buf("eq")
        v16 = ar.alloc([16, 16], F32)
        i16 = ar.alloc([16, 16], U32)
        i16f = ar.alloc([16, 16], F32)
        B_v16, B_i16, B_i16f = Buf("v16"), Buf("i16"), Buf("i16f")
        b16 = ar.alloc([8, 16], F32)
        p16 = ar.alloc([8, 16], U32)
        pa = ar.alloc([8, 16], U32)
        pb_ = ar.alloc([8, 16], U32)
        paf = ar.alloc([8, 16], F32)
        pbf = ar.alloc([8, 16], F32)
        B_b16, B_p16, B_pab = Buf("b16"), Buf("p16"), Buf("pab")
        sel = ar.alloc([2, 8, 16], F32)
        B_sel = Buf("sel")
        idxf = ar.alloc([128], F32)
        idxu = ar.alloc([128], U32)
        B_idx = Buf("idx")
        gate = ar.alloc([8, 16], F32)
        B_gate = Buf("gate")
        gs8 = ar.alloc([16], F32)
        B_gs8 = Buf("gs8")
        actv = ar.alloc([128], F32)
        B_actv = Buf("actv")
        wgt = ar.alloc([128], F32)
        B_wgt = Buf("wgt")
        acc = ar.alloc([D], F32)
        B_acc = Buf("acc")
        iota16 = ar.alloc([16], F32)
        B_io = Buf("iota")
        P.op("pool", lambda e: e.iota(iota16, pattern=[[1, 16]], base=0, channel_multiplier=0,
                                      allow_small_or_imprecise_dtypes=True), w=[B_io])
        junk = scr2[:, 0:D]
        gk = [0]
        it = 0
        for s in range(NS):
            for t in range(NT):
                par = it % 2
                it += 1
                r0 = t * 128
                xsp, bxs, hp, bh = xs[par], B_xs[par], h_sb[par], B_h[par]
                P.op("sp", lambda e, xsp=xsp, s=s, r0=r0: e.dma_start(out=xsp, in_=x[s, r0:r0 + 128, :]),
                     w=[bxs], dma=True)
                P.op("sp", lambda e, s=s, r0=r0: e.dma_start(out=at[:, 0:512], in_=attn_d[s, r0:r0 + 128, :]),
                     r=[attn_t[s][t]], w=[B_at], dma=True)
                P.op("sp", lambda e, s=s, r0=r0: e.dma_start(out=at[:, 512:1024], in_=rw_d[s, r0:r0 + 128, :]),
                     r=[rw_t[s][t]], pw=[B_at], dma=True)
                P.op("act", lambda e: e.activation(out=junk[:, 0:512], in_=at[:, 0:512], func=AF.Square,
                                                   scale=float(512 ** -0.5), accum_out=st4[:, 0:1]),
                     r=[B_at], w=[B_scr2], pw=[B_st])
                rstd(st4[:, 1:2], st4[:, 0:1], 1.0, 1, [B_st], [B_st])
                P.op("dve", lambda e: e.tensor_scalar(out=catb[:, 0:512], in0=at[:, 0:512],
                                                      scalar1=st4[:, 1:2], scalar2=None, op0=ALU.mult),
                     r=[B_at, B_st], w=[B_catb])
                P.op("pool", lambda e: e.tensor_copy(out=catb[:, 512:1024], in_=at[:, 512:1024]),
                     r=[B_at], pw=[B_catb])
                P.op("pe", [lambda e, c=c: e.transpose(out=PSB(0, 128, off=c * 128), in_=catb[:, c * 128:(c + 1) * 128],
                                                       identity=ident_b) for c in range(8)],
                     r=[B_catb, B_ident], w=[pbank[0]])
                P.op("act", lambda e: e.copy(out=catT.rearrange("p a b -> p (a b)"), in_=PSB(0)), r=[pbank[0]], w=[B_catT])
                fns = []
                for j in range(2):
                    for c in range(8):
                        fns.append(lambda e, j=j, c=c: e.matmul(out=PS(1 + j), lhsT=catT[:, c, :],
                                                                rhs=w_out_b[:, c, j * 512:(j + 1) * 512],
                                                                start=(c == 0), stop=(c == 7)))
                P.op("pe", fns, r=[B_catT, B_w], w=[pbank[1], pbank[2]])
                for j in range(2):
                    P.op("dve", lambda e, j=j, hp=hp, xsp=xsp: e.tensor_tensor(
                        out=hp[:, j * 512:(j + 1) * 512], in0=PS(1 + j), in1=xsp[:, j * 512:(j + 1) * 512], op=ALU.add),
                        r=[pbank[1 + j], bxs], w=[bh] if j == 0 else [], pw=[] if j == 0 else [bh])
                P.op("act", lambda e, hp=hp: e.activation(out=junk, in_=hp, func=AF.Square, scale=1.0 / 32.0,
                                                          accum_out=st4[:, 2:3]), r=[bh], w=[B_scr2], pw=[B_st])
                rstd(st4[:, 3:4], st4[:, 2:3], 1.0, 1, [B_st], [B_st])
                P.op("dve", lambda e, hp=hp: e.scalar_tensor_tensor(out=hn, in0=hp, scalar=st4[:, 3:4], in1=g2_b,
                                                                    op0=ALU.mult, op1=ALU.mult),
                     r=[bh, B_st, B_g2], w=[B_hn])
                P.op("act", lambda e: e.copy(out=catb, in_=hn), r=[B_hn], w=[B_catb])
                P.op("pe", [lambda e, c=c: e.transpose(out=PSB(0, 128, off=c * 128), in_=catb[:, c * 128:(c + 1) * 128],
                                                       identity=ident_b) for c in range(8)],
                     r=[B_catb, B_ident], w=[pbank[0]])
                P.op("act", lambda e: e.copy(out=catT.rearrange("p a b -> p (a b)"), in_=PSB(0)), r=[pbank[0]], w=[B_catT])
                fns = []
                for j in range(4):
                    for c in range(8):
                        fns.append(lambda e, j=j, c=c: e.matmul(out=PS(3 + j), lhsT=catT[:, c, :],
                                                                rhs=Ws_b[:, c, j * 512:(j + 1) * 512],
                                                                start=(c == 0), stop=(c == 7)))
                P.op("pe", fns, r=[B_catT, B_ws], w=[pbank[3], pbank[4], pbank[5], pbank[6]])
                sf = s_sb.rearrange("p a b -> p (a b)")
                for j in range(4):
                    P.op("act", lambda e, j=j: e.copy(out=sf[:, j * 512:(j + 1) * 512], in_=PS(3 + j)), r=[pbank[3 + j]],
                         w=[B_s] if j == 0 else [], pw=[] if j == 0 else [B_s])
                s2 = scr2.rearrange("p (a b) -> p a b", b=128)
                for j in range(16):
                    fl = (j == 0)
                    P.op("dve", lambda e, j=j: e.max(out=v16[:, j, 0:8], in_=s_sb[:, j, :]), r=[B_s],
                         w=[B_v16] if fl else [], pw=[] if fl else [B_v16])
                    P.op("dve", lambda e, j=j: e.max_index(out=i16[:, j, 0:8], in_max=v16[:, j, 0:8], in_values=s_sb[:, j, :]),
                         r=[B_s, B_v16], w=[B_i16] if fl else [], pw=[] if fl else [B_i16])
                    P.op("dve", lambda e, j=j: e.match_replace(out=s2[:, j, :], in_to_replace=v16[:, j, 0:8],
                                                               in_values=s_sb[:, j, :], imm_value=-1e30),
                         r=[B_s, B_v16], w=[B_scr2] if fl else [], pw=[] if fl else [B_scr2])
                    P.op("dve", lambda e, j=j: e.max(out=v16[:, j, 8:16], in_=s2[:, j, :]), r=[B_scr2], pw=[B_v16])
                    P.op("dve", lambda e, j=j: e.max_index(out=i16[:, j, 8:16], in_max=v16[:, j, 8:16], in_values=s2[:, j, :]),
                         r=[B_scr2, B_v16], pw=[B_i16])
                P.op("dve", lambda e: e.tensor_copy(out=i16f, in_=i16), r=[B_i16], w=[B_i16f])
                v4 = v16.rearrange("p (h f) k -> p h f k", f=2)
                i4 = i16f.rearrange("p (h f) k -> p h f k", f=2)
                P.op("dve", lambda e: e.tensor_tensor(out=cand.rearrange("p h (a b) -> p h a b", b=16),
                                                      in0=bc(v4[:, :, 0, :].unsqueeze(3), [128, 8, 16, 16]),
                                                      in1=bc(v4[:, :, 1, :].unsqueeze(2), [128, 8, 16, 16]), op=ALU.add),
                     r=[B_v16], w=[B_cand])
                c2 = scr2.rearrange("p (a b) -> p a b", b=256)
                for h in range(8):
                    fl = (h == 0)
                    P.op("dve", lambda e, h=h: e.max(out=b16[:, h, 0:8], in_=cand[:, h, :]), r=[B_cand],
                         w=[B_b16] if fl else [], pw=[] if fl else [B_b16])
                    P.op("dve", lambda e, h=h: e.max_index(out=p16[:, h, 0:8], in_max=b16[:, h, 0:8], in_values=cand[:, h, :]),
                         r=[B_cand, B_b16], w=[B_p16] if fl else [], pw=[] if fl else [B_p16])
                    P.op("dve", lambda e, h=h: e.match_replace(out=c2[:, h, :], in_to_replace=b16[:, h, 0:8],
                                                               in_values=cand[:, h, :], imm_value=-1e30),
                         r=[B_cand, B_b16], w=[B_scr2] if fl else [], pw=[] if fl else [B_scr2])
                    P.op("dve", lambda e, h=h: e.max(out=b16[:, h, 8:16], in_=c2[:, h, :]), r=[B_scr2], pw=[B_b16])
                    P.op("dve", lambda e, h=h: e.max_index(out=p16[:, h, 8:16], in_max=b16[:, h, 8:16], in_values=c2[:, h, :]),
                         r=[B_scr2, B_b16], pw=[B_p16])
                P.op("dve", lambda e: e.tensor_single_scalar(out=pa, in_=p16, scalar=4, op=ALU.logical_shift_right),
                     r=[B_p16], w=[B_pab])
                P.op("dve", lambda e: e.tensor_single_scalar(out=pb_, in_=p16, scalar=15, op=ALU.bitwise_and),
                     r=[B_p16], pw=[B_pab])
                P.op("dve", lambda e: e.tensor_copy(out=paf, in_=pa), r=[B_pab], pw=[B_pab])
                P.op("dve", lambda e: e.tensor_copy(out=pbf, in_=pb_), r=[B_pab], pw=[B_pab])
                io4 = bc(iota16.unsqueeze(1).unsqueeze(1), [128, 8, 16, 16])
                for (k_, pf) in ((0, paf), (1, pbf)):
                    P.op("dve", lambda e, pf=pf: e.tensor_tensor(out=eq, in0=io4, in1=bc(pf.unsqueeze(3), [128, 8, 16, 16]),
                                                                 op=ALU.is_equal), r=[B_io, B_pab], w=[B_eq])
                    P.op("dve", lambda e, k_=k_: e.tensor_tensor(out=eq, in0=eq,
                                                                 in1=bc(i4[:, :, k_, :].unsqueeze(2), [128, 8, 16, 16]),
                                                                 op=ALU.mult), r=[B_eq, B_i16f], w=[B_eq])
                    P.op("dve", lambda e, k_=k_: e.tensor_reduce(out=sel[:, k_], in_=eq, axis=AX.X, op=ALU.add),
                         r=[B_eq], w=[B_sel] if k_ == 0 else [], pw=[] if k_ == 0 else [B_sel])
                P.op("dve", lambda e: e.scalar_tensor_tensor(out=idxf.rearrange("p (h k) -> p h k", k=16), in0=sel[:, 0],
                                                             scalar=128.0, in1=sel[:, 1], op0=ALU.mult, op1=ALU.add),
                     r=[B_sel], w=[B_idx])
                P.op("dve", lambda e: e.tensor_scalar(out=idxf, in0=idxf, scalar1=0.0, scalar2=float(NE - 1), op0=ALU.max,
                                                      op1=ALU.min), r=[B_idx], pw=[B_idx])
                P.op("dve", lambda e: e.tensor_copy(out=idxu, in_=idxf), r=[B_idx], pw=[B_idx])
                P.op("dve", lambda e: e.tensor_tensor(out=gate, in0=b16, in1=bc(b16[:, :, 0:1], [128, 8, 16]),
                                                      op=ALU.subtract), r=[B_b16], w=[B_gate])
                P.op("act", lambda e: e.activation(out=gate, in_=gate, func=AF.Exp), r=[B_gate], w=[B_gate])
                P.op("dve", lambda e: e.tensor_reduce(out=gs8[:, 0:8], in_=gate, axis=AX.X, op=ALU.add), r=[B_gate],
                     w=[B_gs8])
                P.op("dve", lambda e: e.reciprocal(out=gs8[:, 8:16], in_=gs8[:, 0:8]), r=[B_gs8], pw=[B_gs8])
                P.op("dve", lambda e: e.tensor_tensor(out=gate, in0=gate, in1=bc(gs8[:, 8:16].unsqueeze(2), [128, 8, 16]),
                                                      op=ALU.mult), r=[B_gate, B_gs8], w=[B_gate])
                for j in range(128):
                    k_ = gk[0] % NG
                    gk[0] += 1
                    P.op("pool", lambda e, j=j, k_=k_: e.indirect_dma_start(
                        out=gsl[:, k_, :], out_offset=None, in_=expert_u,
                        in_offset=bass.IndirectOffsetOnAxis(ap=idxu[:, j:j + 1], axis=0)),
                        r=[B_idx], w=[B_g[k_]], dma=True)
                    P.op("dve", lambda e, j=j, k_=k_: e.scalar_tensor_tensor(
                        out=junk, in0=gsl[:, k_, :], scalar=1.0, in1=hn, op0=ALU.mult, op1=ALU.mult,
                        accum_out=actv[:, j:j + 1]), r=[B_g[k_], B_hn], w=[B_scr2],
                        pw=[B_actv])
                P.op("act", lambda e: e.activation(out=wgt, in_=actv, func=AF.Gelu), r=[B_actv], w=[B_wgt])
                P.op("dve", lambda e: e.tensor_tensor(out=wgt, in0=wgt, in1=gate.rearrange("p h k -> p (h k)"), op=ALU.mult),
                     r=[B_wgt, B_gate], w=[B_wgt])
                for j in range(128):
                    k_ = gk[0] % NG
                    gk[0] += 1
                    P.op("pool", lambda e, j=j, k_=k_: e.indirect_dma_start(
                        out=gsl[:, k_, :], out_offset=None, in_=expert_v,
                        in_offset=bass.IndirectOffsetOnAxis(ap=idxu[:, j:j + 1], axis=0)),
                        r=[B_idx], w=[B_g[k_]], dma=True)
                    if j == 0:
                        P.op("dve", lambda e, j=j, k_=k_, hp=hp: e.scalar_tensor_tensor(
                            out=acc, in0=gsl[:, k_, :], scalar=wgt[:, j:j + 1], in1=hp, op0=ALU.mult, op1=ALU.add),
                            r=[B_g[k_], B_wgt, bh], w=[B_acc])
                    else:
                        P.op("dve", lambda e, j=j, k_=k_: e.scalar_tensor_tensor(
                            out=acc, in0=gsl[:, k_, :], scalar=wgt[:, j:j + 1], in1=acc, op0=ALU.mult, op1=ALU.add),
                            r=[B_g[k_], B_wgt, B_acc], w=[B_acc])
                o = P.op("sp", lambda e, s=s, r0=r0: e.dma_start(out=y[s, r0:r0 + 128, :], in_=acc),
                         r=[B_acc], dma=True)
                out_ops.append(o)
    P.barrier()
    P.finalize()
    es.close()
    return nc


_CACHE = {}


def _consts(SEQ):
    ident = np.eye(128, dtype=np.float32)
    half = 16
    inv = 1.0 / (10000.0 ** (np.arange(half, dtype=np.float32) / half))
    ang = np.arange(SEQ, dtype=np.float32)[:, None] * inv[None, :].astype(np.float32)
    rope = np.concatenate([np.cos(ang), np.sin(ang)], axis=1).astype(np.float32)
    ce = np.float32(np.exp(-0.5))
    idx = np.arange(128)
    cmask = np.zeros((128, 2, 896), np.float32)
    for d in range(2):
        strict = (idx[:, None] < idx[None, :]) if d == 0 else (idx[:, None] > idx[None, :])
        strict = strict.astype(np.float32)
        incl = strict + np.eye(128, dtype=np.float32)
        cmask[:, d, 0:128] = strict
        cmask[:, d, 128:256] = incl
        cmask[:, d, 256:384] = strict
        cmask[:, d, 384:512] = incl
        cmask[:, d, 512:640] = strict.T
        cmask[:, d, 640:768] = -ce * incl
        cmask[:, d, 768:896] = -ce * strict.T
    cmask = cmask.reshape(128, 1792)
    return dict(ident=ident, rope=rope, cmask=cmask)


WNAMES = ["norm1_g", "w_in", "q_lat_g", "w_uq", "kv_lat_g", "w_ukv", "q_norm_g", "k_norm_g", "attn_out_g",
          "mu_prev", "mu_next", "w0", "w_up", "a0", "a_up", "g_up", "k_k", "k_a", "r_k", "ln_x_g", "ln_x_b",
          "w_out", "norm2_g", "w_pq", "sub_keys", "expert_u", "expert_v"]


def kernel(**inputs):
    xp = np.asarray(inputs["x_prompt"], dtype=np.float32)
    xsm = np.asarray(inputs["x_sample"], dtype=np.float32)
    SEQ = xp.shape[1]
    seqs = [xp[i] for i in range(xp.shape[0])] + [xsm[i] for i in range(xsm.shape[0])]
    n = 8
    assign = [(c, 8 + c if 8 + c < len(seqs) else c) for c in range(n)]
    key = (SEQ, 2)
    if key not in _CACHE:
        _CACHE[key] = build(SEQ, 2)
    nc = _CACHE[key]
    base = {k: np.ascontiguousarray(np.asarray(inputs[k], dtype=np.float32)[0]) for k in WNAMES}
    base.update(_consts(SEQ))
    in_maps = []
    for c in range(n):
        m = dict(base)
        m["x"] = np.ascontiguousarray(np.stack([seqs[assign[c][0]], seqs[assign[c][1]]]))
        in_maps.append(m)
    res = run_bass_kernel_spmd(nc, in_maps, core_ids=list(range(n)))
    outs = [None] * len(seqs)
    for c in range(n):
        yc = res.results[c]["y"]
        outs[assign[c][0]] = yc[0]
        if 8 + c < len(seqs):
            outs[8 + c] = yc[1]
    nb = xp.shape[0]
    return (np.stack(outs[:nb]).astype(np.float32), np.stack(outs[nb:]).astype(np.float32))
```

```python
import numpy as np
import concourse.bass as bass
import concourse.mybir as mybir
from concourse.bass_utils import run_bass_kernel_spmd
from contextlib import ExitStack

F32 = mybir.dt.float32
BF16 = mybir.dt.bfloat16
U32 = mybir.dt.uint32
AF = mybir.ActivationFunctionType
ALU = mybir.AluOpType
AX = mybir.AxisListType

D = 1024
IN_COLS = 2464
OFF_KV = 384
OFF_KR = 640
OFF_RWKV = 672
RWKV_IN = 1792
NE = 16384
EPS = 1e-6
GN_EPS = 64e-5
KDMA = 6


class Buf:
    __slots__ = ("name", "writers", "readers", "war", "full", "excl")

    def __init__(self, name="", excl=False):
        self.name = name
        self.excl = excl
        self.writers = []
        self.readers = []
        self.war = []
        self.full = None


class Op:
    __slots__ = ("eng", "fns", "deps", "signal", "dma", "sem", "val")


class Prog:
    ENGS = ("pe", "dve", "act", "pool", "sp")

    def __init__(self, nc, es):
        self.nc = nc
        self.es = es
        self.streams = {e: [] for e in self.ENGS}
        self.allops = []
        self.dma_since = []
        self.nsem = 0

    def op(self, eng, fns, r=(), w=(), pw=(), dma=False):
        import os
        lim = int(os.environ.get("K_MAXOPS", "0"))
        self.nrec = getattr(self, "nrec", 0) + 1
        if lim and self.nrec > lim:
            o = Op()
            o.deps = set()
            o.signal = False
            o.dma = dma
            o.sem = None
            o.val = 0
            o.fns = []
            o.eng = eng
            return o
        if os.environ.get("K_TRACE"):
            import inspect
            fr = inspect.stack()[1]
            print("OP", self.nrec, eng, fr.lineno)
        o = Op()
        o.eng = eng
        o.fns = list(fns) if isinstance(fns, (list, tuple)) else [fns]
        o.dma = dma
        o.signal = dma
        o.sem = None
        o.val = 0
        deps = set()
        for b in r:
            deps.update(b.writers)
            if b.excl:
                deps.update(b.readers)
        for b in w:
            b.war = b.readers + b.writers
            deps.update(b.war)
        for b in pw:
            if b.readers:
                b.war = b.readers + b.writers
                b.writers = []
                b.readers = []
                b.full = None
            deps.update(b.war)
            if b.full is not None:
                deps.add(b.full)
        for b in w:
            b.writers = [o]
            b.readers = []
            b.full = o
        for b in pw:
            b.writers.append(o)
        for b in r:
            if (b not in w) and (b not in pw):
                b.readers.append(o)
        deps.discard(o)
        o.deps = deps
        self.streams[eng].append(o)
        self.allops.append(o)
        if dma:
            self.dma_since.append(o)
        return o

    def barrier(self):
        deps = set(self.dma_since)
        self.dma_since = []
        for e in self.ENGS:
            for o in reversed(self.streams[e]):
                if o.fns and not o.dma:
                    deps.add(o)
                    break
        for e in self.ENGS:
            o = Op()
            o.eng = e
            o.fns = []
            o.dma = False
            o.signal = False
            o.sem = None
            o.val = 0
            o.deps = set(deps)
            self.streams[e].append(o)
            self.allops.append(o)

    def newsem(self):
        self.nsem += 1
        return self.es.enter_context(self.nc.semaphore(f"s{self.nsem}"))

    def finalize(self):
        nc = self.nc
        for o in self.allops:
            for d in o.deps:
                d.signal = True
        slots = {e: [dict(sem=None, cnt=0, last=None) for _ in range(KDMA)] for e in self.ENGS}
        rr = {e: 0 for e in self.ENGS}
        for o in self.allops:
            if o.dma:
                sl = slots[o.eng][rr[o.eng] % KDMA]
                rr[o.eng] += 1
                if sl["sem"] is None or sl["cnt"] + 16 > 65000:
                    sl["sem"] = self.newsem()
                    sl["cnt"] = 0
                if sl["last"] is not None:
                    o.deps.add(sl["last"])
                sl["cnt"] += 16
                o.sem = sl["sem"]
                o.val = sl["cnt"]
                sl["last"] = o
        for e in self.ENGS:
            sem = None
            cnt = 0
            for o in self.streams[e]:
                if o.dma or not o.signal:
                    continue
                if sem is None or cnt >= 60000:
                    sem = self.newsem()
                    cnt = 0
                cnt += 1
                o.sem = sem
                o.val = cnt
        streams = self.streams

        def run(engname, eobj):
            seen = {}
            for o in streams[engname]:
                need = {}
                for d in o.deps:
                    k = id(d.sem)
                    if d.val > seen.get(k, 0) and d.val > need.get(k, (0, None))[0]:
                        need[k] = (d.val, d.sem)
                for k, (v, s) in need.items():
                    eobj.wait_ge(s, v)
                    seen[k] = v
                ins = None
                for f in o.fns:
                    ins = f(eobj)
                if o.signal:
                    ins.then_inc(o.sem, 16 if o.dma else 1)

        with nc.Block() as block:
            block.tensor(lambda e: run("pe", e))
            block.vector(lambda e: run("dve", e))
            block.scalar(lambda e: run("act", e))
            block.gpsimd(lambda e: run("pool", e))
            block.sync(lambda e: run("sp", e))


class Arena:
    def __init__(self, nc, es, nwords):
        self.t = es.enter_context(nc.sbuf_tensor("arena", [128, nwords], F32))
        self.n = nwords
        self.base = 0
        self.p = 0

    def mark(self):
        self.base = self.p

    def reset(self):
        self.p = self.base

    def alloc(self, shape, dt, parts=128):
        n = 1
        for s_ in shape:
            n *= s_
        words = (n * (2 if dt == BF16 else 4) + 3) // 4
        words = (words + 7) // 8 * 8
        assert self.p + words <= self.n, f"arena overflow {self.p + words} > {self.n}"
        ap = self.t[:, self.p:self.p + words]
        self.p += words
        if dt != F32:
            ap = ap.bitcast(dt)
        ap = ap[:, 0:n]
        if len(shape) == 2:
            ap = ap.rearrange("p (a b) -> p a b", b=shape[1])
        elif len(shape) == 3:
            ap = ap.rearrange("p (a b c) -> p a b c", b=shape[1], c=shape[2])
        if parts != 128:
            ap = ap[0:parts]
        return ap


def bc(ap, shape):
    return ap.to_broadcast(list(shape))


def build(SEQ=8192, NS=2, dbg=False, phases=(1, 2, 3, 4)):
    NT = SEQ // 128
    nc = bass.Bass("TRN2", target_bir_lowering=False)
    es = ExitStack()

    def din(name, shape, dt=F32):
        return nc.dram_tensor(name, list(shape), dt, kind="ExternalInput").ap()

    def dscr(name, shape, dt=F32):
        kind = "ExternalOutput" if dbg else "Internal"
        return nc.dram_tensor(name, list(shape), dt, kind=kind).ap()

    x = din("x", [NS, SEQ, D])
    norm1_g = din("norm1_g", [D])
    w_in = din("w_in", [D, IN_COLS])
    q_lat_g = din("q_lat_g", [384])
    w_uq = din("w_uq", [384, 768])
    kv_lat_g = din("kv_lat_g", [256])
    w_ukv = din("w_ukv", [256, 1024])
    q_norm_g = din("q_norm_g", [96])
    k_norm_g = din("k_norm_g", [96])
    attn_out_g = din("attn_out_g", [512])
    mu_prev = din("mu_prev", [RWKV_IN])
    mu_next = din("mu_next", [RWKV_IN])
    w0 = din("w0", [2, 512])
    w_up = din("w_up", [2, 64, 512])
    a0 = din("a0", [2, 512])
    a_up = din("a_up", [2, 64, 512])
    g_up = din("g_up", [128, 512])
    k_k = din("k_k", [512])
    k_a = din("k_a", [512])
    r_k = din("r_k", [512])
    ln_x_g = din("ln_x_g", [512])
    ln_x_b = din("ln_x_b", [512])
    w_out = din("w_out", [D, D])
    norm2_g = din("norm2_g", [D])
    w_pq = din("w_pq", [D, 2048])
    sub_keys = din("sub_keys", [2, 128, 128])
    expert_u = din("expert_u", [NE, D])
    expert_v = din("expert_v", [NE, D])
    ident_d = din("ident", [128, 128])
    rope_d = din("rope", [SEQ, 32])
    cmask_d = din("cmask", [128, 1792])

    y = nc.dram_tensor("y", [NS, SEQ, D], F32, kind="ExternalOutput").ap()
    qT_d = dscr("qT_s", [NS, 8, 96, SEQ], BF16)
    kT_d = dscr("kT_s", [NS, 8, 96, SEQ], BF16)
    v_d = dscr("v_s", [NS, SEQ, 520], BF16)
    z_d = dscr("z_s", [NS, SEQ + 2, RWKV_IN])
    attn_d = dscr("attn_s", [NS, SEQ, 512])
    rw_d = dscr("rw_s", [NS, SEQ, 512])

    P = Prog(nc, es)
    ar = Arena(nc, es, 45000)
    psum = es.enter_context(nc.psum_tensor("psum", [128, 4096], F32))
    pbank = [Buf(f"pb{i}", excl=True) for i in range(8)]

    def PS(b, n=512, parts=128, off=0):
        a = psum[:, b * 512 + off:b * 512 + off + n]
        return a if parts == 128 else a[0:parts]

    def PSB(b, n=1024, parts=128, off=0):
        a = psum[:, b * 512:(b + 1) * 512].bitcast(BF16)[:, off:off + n]
        return a if parts == 128 else a[0:parts]

    ident_f = ar.alloc([128], F32)
    ident_b = ar.alloc([128], BF16)
    B_ident = Buf("ident")
    P.op("sp", lambda e: e.dma_start(out=ident_f, in_=ident_d), w=[B_ident], dma=True)
    P.op("dve", lambda e: e.tensor_copy(out=ident_b, in_=ident_f), r=[B_ident], pw=[B_ident])
    mhalf = ar.alloc([16], F32)
    B_mh = Buf("mhalf")
    P.op("pool", lambda e: e.memset(mhalf, -0.5), w=[B_mh])
    ar.mark()

    def rstd(out, in_, mul, n, r, pw, eps=EPS):
        P.op("pool", lambda e: e.tensor_scalar(out=out, in0=in_, scalar1=float(mul), scalar2=float(eps), op0=ALU.mult,
                                               op1=ALU.add), r=r, pw=pw)
        P.op("pool", lambda e: e.tensor_tensor(out=out, in0=out, in1=mhalf[:, 0:n], op=ALU.pow), r=r + [B_mh], pw=pw)

    def _phase(n):
        def deco(f):
            if n in phases:
                f()
            return f
        return deco

    zt = [[Buf(f"z{s}_{t}") for t in range(NT + 2)] for s in range(NS)]
    qkv_t = [[Buf(f"qkv{s}_{t}") for t in range(NT)] for s in range(NS)]
    attn_t = [[Buf(f"at{s}_{t}") for t in range(NT)] for s in range(NS)]
    rw_t = [[Buf(f"rw{s}_{t}") for t in range(NT)] for s in range(NS)]
    out_ops = []

    def load_colT(dst, src_vec, nchunk, buf):
        P.op("sp", lambda e: e.dma_start(out=dst, in_=src_vec.rearrange("(c p) -> p c", p=128),
                                         allow_slow_non_contiguous=True), w=[buf], dma=True)

    @_phase(1)
    def _p1():
        ar.reset()
        w_in_b = ar.alloc([8, IN_COLS], BF16)
        w_uq_b = ar.alloc([3, 768], BF16)
        w_ukv_b = ar.alloc([2, 1024], BF16)
        gT = ar.alloc([16], F32)
        gq_b = ar.alloc([96], F32)
        gk_b = ar.alloc([96], F32)
        stage = [ar.alloc([IN_COLS], F32) for _ in range(2)]
        B_w = Buf("w1")
        B_gT = Buf("gT")
        B_gqk = Buf("gqk")
        B_stage = [Buf("st0"), Buf("st1")]
        load_colT(gT[:, 0:8], norm1_g, 8, B_gT)
        P.op("sp", lambda e: e.dma_start(out=gT[:, 8:11], in_=q_lat_g.rearrange("(c p) -> p c", p=128),
                                         allow_slow_non_contiguous=True), pw=[B_gT], dma=True)
        P.op("sp", lambda e: e.dma_start(out=gT[:, 11:13], in_=kv_lat_g.rearrange("(c p) -> p c", p=128),
                                         allow_slow_non_contiguous=True), pw=[B_gT], dma=True)
        P.op("sp", lambda e: e.dma_start(out=gq_b, in_=bc(q_norm_g.unsqueeze(0), [128, 96])), w=[B_gqk], dma=True)
        P.op("sp", lambda e: e.dma_start(out=gk_b, in_=bc(k_norm_g.unsqueeze(0), [128, 96])), pw=[B_gqk], dma=True)
        P.op("dve", lambda e: e.tensor_scalar(out=gq_b, in0=gq_b, scalar1=float(96 ** -0.5), scalar2=None,
                                              op0=ALU.mult), r=[B_gqk], pw=[B_gqk])
        k = 0
        jobs = [(w_in[c * 128:(c + 1) * 128, :], w_in_b[:, c, :], IN_COLS, c) for c in range(8)]
        jobs += [(w_uq[c * 128:(c + 1) * 128, :], w_uq_b[:, c, :], 768, 8 + c) for c in range(3)]
        jobs += [(w_ukv[c * 128:(c + 1) * 128, :], w_ukv_b[:, c, :], 1024, 11 + c) for c in range(2)]
        for (src, dst, n, gc) in jobs:
            st = stage[k % 2]
            bs = B_stage[k % 2]
            P.op("sp", lambda e, st=st, src=src, n=n: e.dma_start(out=st[:, 0:n], in_=src), w=[bs], dma=True)
            P.op("dve", lambda e, st=st, dst=dst, n=n, gc=gc: e.tensor_scalar(
                out=dst, in0=st[:, 0:n], scalar1=gT[:, gc:gc + 1], scalar2=None, op0=ALU.mult),
                r=[bs, B_gT], pw=[B_w])
            k += 1

        xs = [ar.alloc([D], F32) for _ in range(2)]
        B_xs = [Buf("xs0"), Buf("xs1")]
        junk = ar.alloc([D], F32)
        B_junk = Buf("junk")
        xb = ar.alloc([D], BF16)
        B_xb = Buf("xb")
        xT = ar.alloc([8, 128], BF16)
        B_xT = Buf("xT")
        st4 = ar.alloc([8], F32)
        B_st = Buf("st4")
        proj = ar.alloc([IN_COLS], F32)
        B_proj = Buf("proj")
        latb = ar.alloc([640], BF16)
        B_latb = Buf("latb")
        latT = ar.alloc([5, 128], BF16)
        B_latT = Buf("latT")
        q_sb = ar.alloc([8, 96], F32)
        k_sb = ar.alloc([8, 96], F32)
        B_q = Buf("q")
        B_k = Buf("k")
        sq = ar.alloc([8, 96], F32)
        B_sq = Buf("sq")
        sq2 = ar.alloc([8, 96], F32)
        B_sq2 = Buf("sq2")
        hst = ar.alloc([32], F32)
        B_hq = Buf("hq")
        B_hk = Buf("hk")
        vb = ar.alloc([8, 65], BF16)
        B_vb = Buf("vb")
        qb = ar.alloc([8, 96], BF16)
        kb = ar.alloc([8, 96], BF16)
        B_qb = Buf("qb")
        B_kb = Buf("kb")
        rt = ar.alloc([8, 6, 16], F32)
        B_rtq = Buf("rtq")
        B_rtk = Buf("rtk")
        cs = [ar.alloc([32], F32) for _ in range(2)]
        B_cs = [Buf("cs0"), Buf("cs1")]
        qT_sb = ar.alloc([8, 128], BF16)
        kT_sb = ar.alloc([8, 128], BF16)
        B_qT = Buf("qTsb")
        B_kT = Buf("kTsb")
        zero_t = ar.alloc([RWKV_IN], F32)
        B_zero = Buf("zero")

        P.op("dve", lambda e: e.memset(vb, 1.0), w=[B_vb])
        P.op("dve", lambda e: e.memset(zero_t, 0.0), w=[B_zero])
        for s in range(NS):
            P.op("sp", lambda e, s=s: e.dma_start(out=z_d[s, 0:1, :], in_=zero_t[0:1, :]), r=[B_zero],
                 w=[zt[s][0]], dma=True)
            P.op("sp", lambda e, s=s: e.dma_start(out=z_d[s, SEQ + 1:SEQ + 2, :], in_=zero_t[0:1, :]), r=[B_zero],
                 w=[zt[s][NT + 1]], dma=True)

        it = 0
        for s in range(NS):
            for t in range(NT):
                par = it % 2
                it += 1
                xsp, bxs = xs[par], B_xs[par]
                csp, bcs = cs[par], B_cs[par]
                r0 = t * 128
                P.op("sp", lambda e, xsp=xsp, s=s, r0=r0: e.dma_start(out=xsp, in_=x[s, r0:r0 + 128, :]),
                     w=[bxs], dma=True)
                P.op("sp", lambda e, csp=csp, r0=r0: e.dma_start(out=csp, in_=rope_d[r0:r0 + 128, :]),
                     w=[bcs], dma=True)
                P.op("act", lambda e, xsp=xsp: e.activation(out=junk, in_=xsp, func=AF.Square, scale=1.0 / 32.0,
                                                            accum_out=st4[:, 0:1]),
                     r=[bxs], w=[B_junk], pw=[B_st])
                P.op("dve", lambda e, xsp=xsp: e.tensor_copy(out=xb, in_=xsp), r=[bxs], w=[B_xb])
                rstd(st4[:, 1:2], st4[:, 0:1], 1.0, 1, [B_st], [B_st])
                P.op("pe", [lambda e, c=c: e.transpose(out=PSB(0, 128, off=c * 128), in_=xb[:, c * 128:(c + 1) * 128],
                                                       identity=ident_b) for c in range(8)],
                     r=[B_xb, B_ident], w=[pbank[0]])
                P.op("act", lambda e: e.copy(out=xT.rearrange("p a b -> p (a b)"), in_=PSB(0)), r=[pbank[0]], w=[B_xT])
                fns = []
                for j in range(5):
                    a, b_ = j * 512, min((j + 1) * 512, IN_COLS)
                    for c in range(8):
                        fns.append(lambda e, j=j, a=a, b_=b_, c=c: e.matmul(
                            out=PS(1 + j, b_ - a), lhsT=xT[:, c, :], rhs=w_in_b[:, c, a:b_],
                            start=(c == 0), stop=(c == 7)))
                P.op("pe", fns, r=[B_xT, B_w], w=[pbank[1], pbank[2], pbank[3], pbank[4], pbank[5]])
                for j in range(5):
                    a, b_ = j * 512, min((j + 1) * 512, IN_COLS)
                    P.op("act", lambda e, j=j, a=a, b_=b_: e.activation(
                        out=proj[:, a:b_], in_=PS(1 + j, b_ - a), func=AF.Copy, scale=st4[:, 1:2]),
                        r=[pbank[1 + j], B_st], pw=[B_proj])
                P.op("sp", lambda e, s=s, r0=r0: e.dma_start(out=z_d[s, 1 + r0:1 + r0 + 128, :],
                                                             in_=proj[:, OFF_RWKV:IN_COLS]),
                     r=[B_proj], w=[zt[s][t + 1]], dma=True)
                P.op("act", lambda e: e.activation(out=junk[:, 0:384], in_=proj[:, 0:384], func=AF.Square,
                                                   scale=float(384 ** -0.5), accum_out=st4[:, 2:3]),
                     r=[B_proj], w=[B_junk], pw=[B_st])
                P.op("act", lambda e: e.activation(out=junk[:, 0:256], in_=proj[:, 384:640], func=AF.Square,
                                                   scale=float(256 ** -0.5), accum_out=st4[:, 3:4]),
                     r=[B_proj], w=[B_junk], pw=[B_st])
                rstd(st4[:, 4:6], st4[:, 2:4], 1.0, 2, [B_st], [B_st])
                P.op("dve", lambda e: e.tensor_scalar(out=latb[:, 0:384], in0=proj[:, 0:384], scalar1=st4[:, 4:5],
                                                      scalar2=None, op0=ALU.mult), r=[B_proj, B_st], w=[B_latb])
                P.op("dve", lambda e: e.tensor_scalar(out=latb[:, 384:640], in0=proj[:, 384:640], scalar1=st4[:, 5:6],
                                                      scalar2=None, op0=ALU.mult), r=[B_proj, B_st], pw=[B_latb])
                P.op("pe", [lambda e, c=c: e.transpose(out=PSB(0, 128, off=c * 128), in_=latb[:, c * 128:(c + 1) * 128],
                                                       identity=ident_b) for c in range(5)],
                     r=[B_latb, B_ident], w=[pbank[0]])
                P.op("act", lambda e: e.copy(out=latT.rearrange("p a b -> p (a b)"), in_=PSB(0, 640)),
                     r=[pbank[0]], w=[B_latT])
                fns = []
                for (bk, a, b_) in ((6, 0, 512), (7, 512, 768)):
                    for c in range(3):
                        fns.append(lambda e, bk=bk, a=a, b_=b_, c=c: e.matmul(
                            out=PS(bk, b_ - a), lhsT=latT[:, c, :], rhs=w_uq_b[:, c, a:b_],
                            start=(c == 0), stop=(c == 2)))
                P.op("pe", fns, r=[B_latT, B_w], w=[pbank[6], pbank[7]])
                fns = []
                for (bk, a, b_) in ((1, 0, 512), (2, 512, 1024)):
                    for c in range(2):
                        fns.append(lambda e, bk=bk, a=a, b_=b_, c=c: e.matmul(
                            out=PS(bk, 512), lhsT=latT[:, 3 + c, :], rhs=w_ukv_b[:, c, a:b_],
                            start=(c == 0), stop=(c == 1)))
                P.op("pe", fns, r=[B_latT, B_w], w=[pbank[1], pbank[2]])
                qf = q_sb.rearrange("p a b -> p (a b)")
                P.op("act", lambda e: e.copy(out=qf[:, 0:512], in_=PS(6)), r=[pbank[6]], w=[B_q])
                P.op("act", lambda e: e.copy(out=qf[:, 512:768], in_=PS(7, 256)), r=[pbank[7]], pw=[B_q])
                for hh in range(2):
                    kvv = PS(1 + hh).rearrange("p (h d) -> p h d", d=128)
                    import os as _os
                    P.op(_os.environ.get("K_E67", "act"), lambda e, hh=hh, kvv=kvv: (e.tensor_copy if _os.environ.get("K_E67", "act") == "dve" else e.copy)(out=k_sb[:, hh * 4:(hh + 1) * 4, 0:64],
                                                                        in_=kvv[:, :, 0:64]),
                         r=[pbank[1 + hh]], pw=[B_k] if hh else [], w=[] if hh else [B_k])
                    P.op("act", lambda e, hh=hh, kvv=kvv: e.copy(out=vb[:, hh * 4:(hh + 1) * 4, 0:64],
                                                                 in_=kvv[:, :, 64:128]),
                         r=[pbank[1 + hh]], pw=[B_vb])
                P.op("dve", lambda e: e.tensor_copy(out=k_sb[:, :, 64:96],
                                                    in_=bc(proj[:, OFF_KR:OFF_RWKV].unsqueeze(1), [128, 8, 32])),
                     r=[B_proj], pw=[B_k])
                P.op("sp", lambda e, s=s, r0=r0: e.dma_start(out=v_d[s, r0:r0 + 128, :],
                                                             in_=vb.rearrange("p a b -> p (a b)")),
                     r=[B_vb], pw=[qkv_t[s][t]], dma=True)
                for (tsb, Bt, sqt, Bsq, ho, Bh, gb, ob, Bo, ro, Brt, eng) in (
                        (q_sb, B_q, sq, B_sq, 0, B_hq, gq_b, qb, B_qb, 0, B_rtq, "dve"),
                        (k_sb, B_k, sq2, B_sq2, 16, B_hk, gk_b, kb, B_kb, 3, B_rtk, "pool")):
                    TT = lambda e, **kw: e.tensor_tensor(**kw)
                    P.op(eng, lambda e, tsb=tsb, sqt=sqt: e.tensor_tensor(out=sqt, in0=tsb, in1=tsb, op=ALU.mult),
                         r=[Bt], w=[Bsq])
                    P.op("dve", lambda e, sqt=sqt, ho=ho: e.tensor_reduce(out=hst[:, ho:ho + 8], in_=sqt, axis=AX.X,
                                                                         op=ALU.add), r=[Bsq], w=[Bh])
                    rstd(hst[:, ho + 8:ho + 16], hst[:, ho:ho + 8], 1.0 / 96.0, 8, [Bh], [Bh])
                    P.op(eng, lambda e, tsb=tsb, ho=ho: e.tensor_tensor(
                        out=tsb, in0=tsb, in1=bc(hst[:, ho + 8:ho + 16].unsqueeze(2), [128, 8, 96]), op=ALU.mult),
                        r=[Bt, Bh], w=[Bt])
                    P.op(eng, lambda e, tsb=tsb, gb=gb: e.tensor_tensor(
                        out=tsb, in0=tsb, in1=bc(gb.unsqueeze(1), [128, 8, 96]), op=ALU.mult),
                        r=[Bt, B_gqk], w=[Bt])
                    cosb = bc(csp[:, 0:16].unsqueeze(1), [128, 8, 16])
                    sinb = bc(csp[:, 16:32].unsqueeze(1), [128, 8, 16])
                    x1 = tsb[:, :, 64:80]
                    x2 = tsb[:, :, 80:96]
                    P.op(eng, lambda e, x1=x1, cosb=cosb, ro=ro: e.tensor_tensor(out=rt[:, :, ro, :], in0=x1, in1=cosb,
                                                                               op=ALU.mult), r=[Bt, bcs], w=[Brt])
                    P.op(eng, lambda e, x2=x2, sinb=sinb, ro=ro: e.tensor_tensor(out=rt[:, :, ro + 1, :], in0=x2,
                                                                                in1=sinb, op=ALU.mult),
                         r=[Bt, bcs], pw=[Brt])
                    P.op(eng, lambda e, ob=ob, ro=ro: e.tensor_tensor(out=ob[:, :, 64:80], in0=rt[:, :, ro, :],
                                                                     in1=rt[:, :, ro + 1, :], op=ALU.subtract),
                         r=[Brt], w=[Bo])
                    P.op(eng, lambda e, x1=x1, sinb=sinb, ro=ro: e.tensor_tensor(out=rt[:, :, ro, :], in0=x1, in1=sinb,
                                                                                op=ALU.mult), r=[Bt, bcs], w=[Brt])
                    P.op(eng, lambda e, x2=x2, cosb=cosb, ro=ro: e.tensor_tensor(out=rt[:, :, ro + 1, :], in0=x2,
                                                                                in1=cosb, op=ALU.mult),
                         r=[Bt, bcs], pw=[Brt])
                    P.op(eng, lambda e, ob=ob, ro=ro: e.tensor_tensor(out=ob[:, :, 80:96], in0=rt[:, :, ro, :],
                                                                     in1=rt[:, :, ro + 1, :], op=ALU.add),
                         r=[Brt], pw=[Bo])
                    P.op(eng, lambda e, ob=ob, tsb=tsb: e.tensor_copy(out=ob[:, :, 0:64], in_=tsb[:, :, 0:64]),
                         r=[Bt], pw=[Bo])
                for (ob, Bo, pb, dst_sb, Bd, dst_d) in ((qb, B_qb, 6, qT_sb, B_qT, qT_d), (kb, B_kb, 7, kT_sb, B_kT, kT_d)):
                    P.op("pe", [lambda e, h=h, ob=ob, pb=pb: e.transpose(out=PSB(pb, 128, parts=96, off=h * 128),
                                                                       in_=ob[:, h, :], identity=ident_b)
                                for h in range(8)], r=[Bo, B_ident], w=[pbank[pb]])
                    P.op("act", lambda e, pb=pb, dst_sb=dst_sb: e.copy(out=dst_sb[0:96].rearrange("p a b -> p (a b)"),
                                                                      in_=PSB(pb, 1024, parts=96)),
                         r=[pbank[pb]], w=[Bd])
                    P.op("sp", lambda e, dst_sb=dst_sb, dst_d=dst_d, s=s, r0=r0: e.dma_start(
                        out=dst_d[s, :, :, r0:r0 + 128].rearrange("h d t -> d h t"), in_=dst_sb[0:96]),
                        r=[Bd], pw=[qkv_t[s][t]], dma=True)
        P.barrier()

    @_phase(2)
    def _p2():
        ar.reset()
        QG = min(512, SEQ)
        NQG = SEQ // QG
        kT_h = [ar.alloc([SEQ], BF16, parts=96) for _ in range(2)]
        qT_h = [ar.alloc([SEQ], BF16, parts=96) for _ in range(2)]
        B_kq = [Buf("kq0"), Buf("kq1")]
        v_all = ar.alloc([NT, 520], BF16)
        B_vall = Buf("vall")
        pT = [ar.alloc([QG], BF16) for _ in range(3)]
        B_pT = [Buf(f"pT{i}") for i in range(3)]
        oT = ar.alloc([QG], F32, parts=65)
        B_oT = Buf("oT")
        rc = ar.alloc([4], F32)
        B_rc = Buf("rc")
        ao = [ar.alloc([4, 64], F32) for _ in range(2)]
        B_ao = [Buf("ao0"), Buf("ao1")]
        hi = 0
        gi = 0
        si = 0
        for s in range(NS):
            P.op("sp", lambda e, s=s: e.dma_start(out=v_all, in_=v_d[s].rearrange("(c p) f -> p c f", p=128)),
                 r=qkv_t[s], w=[B_vall], dma=True)
            for h in range(8):
                par = hi % 2
                hi += 1
                P.op("sp", lambda e, s=s, h=h, par=par: e.dma_start(out=kT_h[par], in_=kT_d[s, h]),
                     r=qkv_t[s], w=[B_kq[par]], dma=True)
                P.op("sp", lambda e, s=s, h=h, par=par: e.dma_start(out=qT_h[par], in_=qT_d[s, h]),
                     r=qkv_t[s], pw=[B_kq[par]], dma=True)
                for qg in range(NQG):
                    ob = 3 + (gi % 2)
                    gi += 1
                    q_ap = qT_h[par][:, qg * QG:(qg + 1) * QG]
                    steps = []
                    for kc in range(NT):
                        sb = si % 3
                        si += 1
                        steps.append((kc, sb))

                    def emit_S(kc, sb, par=par, q_ap=q_ap):
                        P.op("pe", lambda e, kc=kc, sb=sb: e.matmul(out=PS(sb, QG), lhsT=kT_h[par][:, kc * 128:(kc + 1) * 128],
                                                                     rhs=q_ap, start=True, stop=True),
                             r=[B_kq[par]], w=[pbank[sb]])

                    def emit_E(kc, sb):
                        P.op("act", lambda e, sb=sb: e.activation(out=pT[sb], in_=PS(sb, QG), func=AF.Exp),
                             r=[pbank[sb]], w=[B_pT[sb]])

                    def emit_PV(kc, sb, ob=ob, h=h):
                        P.op("pe", lambda e, kc=kc, sb=sb: e.matmul(out=PS(ob, QG, parts=65),
                                                                     lhsT=v_all[:, kc, h * 65:(h + 1) * 65], rhs=pT[sb],
                                                                     start=(kc == 0), stop=(kc == NT - 1)),
                             r=[B_pT[sb], B_vall], w=[pbank[ob]] if kc == 0 else [], pw=[] if kc == 0 else [pbank[ob]])

                    emit_S(*steps[0])
                    for i_, (kc, sb) in enumerate(steps):
                        if i_ + 1 < len(steps):
                            emit_S(*steps[i_ + 1])
                        emit_E(kc, sb)
                        emit_PV(kc, sb)
                    P.op("act", lambda e, ob=ob: e.copy(out=oT, in_=PS(ob, QG, parts=65)), r=[pbank[ob]], w=[B_oT])
                    nj = QG // 128
                    P.op("pe", [lambda e, j=j: e.transpose(out=PS(5, 65, off=j * 128), in_=oT[:, j * 128:(j + 1) * 128],
                                                           identity=ident_f[0:65, 0:65]) for j in range(nj)],
                         r=[B_oT, B_ident], w=[pbank[5]])
                    o5 = PS(5).rearrange("p (j d) -> p j d", d=128)
                    P.op("dve", lambda e, o5=o5, nj=nj: e.reciprocal(out=rc[:, 0:nj], in_=o5[:, 0:nj, 64]),
                         r=[pbank[5]], w=[B_rc])
                    ap_ = gi % 2
                    P.op("dve", lambda e, o5=o5, nj=nj, ap_=ap_: e.tensor_tensor(
                        out=ao[ap_][:, 0:nj, :], in0=o5[:, 0:nj, 0:64],
                        in1=bc(rc[:, 0:nj].unsqueeze(2), [128, nj, 64]), op=ALU.mult),
                        r=[pbank[5], B_rc], w=[B_ao[ap_]])
                    t0 = qg * nj
                    P.op("sp", lambda e, s=s, h=h, qg=qg, nj=nj, ap_=ap_: e.dma_start(
                        out=attn_d[s, qg * QG:(qg + 1) * QG, h * 64:(h + 1) * 64].rearrange("(j p) d -> p j d", p=128),
                        in_=ao[ap_][:, 0:nj, :]),
                        r=[B_ao[ap_]], pw=[attn_t[s][t0 + j] for j in range(nj)], dma=True)
        P.barrier()

    @_phase(3)
    def _p3():
        ar.reset()
        CE = float(np.exp(-0.5))
        mup = ar.alloc([RWKV_IN], F32)
        mun = ar.alloc([RWKV_IN], F32)
        kkb = ar.alloc([512], F32)
        kab = ar.alloc([512], F32)
        rkb = ar.alloc([512], F32)
        lgb = ar.alloc([512], F32)
        lbb = ar.alloc([512], F32)
        w0b = ar.alloc([2, 512], F32)
        a0b = ar.alloc([2, 512], F32)
        wup = ar.alloc([2, 512], BF16)
        aup = ar.alloc([2, 512], BF16)
        gup = ar.alloc([512], BF16)
        cm = ar.alloc([2, 896], F32)
        negc = ar.alloc([2], F32)
        B_c = Buf("p3c")
        first = [True]

        def cload(dst, src):
            P.op("sp", lambda e: e.dma_start(out=dst, in_=src), w=[B_c] if first[0] else [], pw=[] if first[0] else [B_c],
                 dma=True)
            first[0] = False

        cload(mup, bc(mu_prev.unsqueeze(0), [128, RWKV_IN]))
        cload(mun, bc(mu_next.unsqueeze(0), [128, RWKV_IN]))
        for (dst, src) in ((kkb, k_k), (kab, k_a), (rkb, r_k), (lgb, ln_x_g), (lbb, ln_x_b)):
            cload(dst, bc(src.unsqueeze(0), [128, 512]))
        for d in range(2):
            cload(w0b[:, d, :], bc(w0[d:d + 1, :], [128, 512]))
            cload(a0b[:, d, :], bc(a0[d:d + 1, :], [128, 512]))
        cload(cm.rearrange("p a b -> p (a b)"), cmask_d)
        stg = ar.alloc([1024], F32)
        B_stg = Buf("stg")
        for (dst, src) in ((wup, w_up), (aup, a_up)):
            P.op("sp", lambda e, src=src: e.dma_start(out=stg[0:64].rearrange("p (a b) -> p a b", b=512),
                                                      in_=src.rearrange("d j c -> j d c")), w=[B_stg], dma=True)
            P.op("dve", lambda e, dst=dst: e.tensor_copy(out=dst[0:64].rearrange("p a b -> p (a b)"), in_=stg[0:64]),
                 r=[B_stg], pw=[B_c])
        P.op("sp", lambda e: e.dma_start(out=stg[:, 0:512], in_=g_up), w=[B_stg], dma=True)
        P.op("dve", lambda e: e.tensor_copy(out=gup, in_=stg[:, 0:512]), r=[B_stg], pw=[B_c])
        P.op("dve", lambda e: e.memset(negc, -CE), pw=[B_c])

        yd_d = [dscr(f"yd{d}_s", [NS, SEQ, 520]) for d in range(2)]
        gv_d = dscr("gv_s", [NS, SEQ, 1024])
        yd_t = [[[Buf(f"yd{d}_{s}_{t}") for t in range(NT)] for s in range(NS)] for d in range(2)]
        gv_t = [[Buf(f"gv{s}_{t}") for t in range(NT)] for s in range(NS)]

        _pb3 = [Buf(f"p3b{b}", excl=True) for b in range(8)]
        pq = [[_pb3[b]] * 4 for b in range(8)]

        def PQ(b, q, n=128, parts=128):
            a = psum[:, b * 512 + q * 128:b * 512 + q * 128 + n]
            return a if parts == 128 else a[0:parts]

        zc = ar.alloc([RWKV_IN], F32)
        zp = ar.alloc([RWKV_IN], F32)
        zn = ar.alloc([RWKV_IN], F32)
        B_zc, B_zp, B_zn = Buf("zc"), Buf("zp"), Buf("zn")
        lat = ar.alloc([256], BF16)
        latT = ar.alloc([384], BF16)
        B_lat, B_latT = Buf("lat"), Buf("latT")
        names = ["tmp", "sg", "a", "E1", "E2", "E3", "Ex", "kk", "sq", "kd", "b", "t2", "g"]
        W = {n_: ar.alloc([512], F32) for n_ in names}
        BW = {n_: Buf("w_" + n_) for n_ in names}
        bnames = ["rbar", "abar", "bbar", "kbar", "Bt", "Kt", "vq"]
        WB = {n_: ar.alloc([512], BF16) for n_ in bnames}
        BWB = {n_: Buf("wb_" + n_) for n_ in bnames}
        ytile = ar.alloc([520], F32)
        B_y = Buf("ytile")
        s8 = ar.alloc([16], F32)
        B_s8 = Buf("s8")
        gC = [[ar.alloc([8], F32, parts=64) for d in range(2)] for s in range(NS)]
        B_gC = [[Buf(f"gC{s}{d}") for d in range(2)] for s in range(NS)]
        ST = [[[ar.alloc([64], F32, parts=64) for h in range(8)] for d in range(2)] for s in range(NS)]
        STb = [[[ar.alloc([64], BF16, parts=64) for h in range(8)] for d in range(2)] for s in range(NS)]
        B_ST = [[[Buf(f"ST{s}{d}{h}") for h in range(8)] for d in range(2)] for s in range(NS)]
        B_STb = [[[Buf(f"STb{s}{d}{h}") for h in range(8)] for d in range(2)] for s in range(NS)]
        TT = [ar.alloc([512], BF16, parts=64) for h in range(8)]
        MM = [ar.alloc([512], BF16) for h in range(8)]
        Zf = [ar.alloc([128], F32) for h in range(8)]
        Zb = [ar.alloc([128], BF16) for h in range(8)]
        WbT = [ar.alloc([128], BF16, parts=64) for h in range(8)]
        Ub = [ar.alloc([64], BF16) for h in range(8)]
        PA = [[ar.alloc([128], BF16) for k_ in range(2)] for h in range(8)]
        PB = [[ar.alloc([128], BF16) for k_ in range(2)] for h in range(8)]
        B_TT = [Buf(f"TT{h}") for h in range(8)]
        B_MM = [Buf(f"MM{h}") for h in range(8)]
        B_Zf = [Buf(f"Zf{h}") for h in range(8)]
        B_Zb = [Buf(f"Zb{h}") for h in range(8)]
        B_WbT = [Buf(f"WbT{h}") for h in range(8)]
        B_Ub = [Buf(f"Ub{h}") for h in range(8)]
        B_PA = [[Buf(f"PA{h}{k_}") for k_ in range(2)] for h in range(8)]
        B_PB = [[Buf(f"PB{h}{k_}") for k_ in range(2)] for h in range(8)]

        for s in range(NS):
            for d in range(2):
                for h in range(8):
                    P.op("pool", lambda e, s=s, d=d, h=h: e.memset(ST[s][d][h], 0.0), w=[B_ST[s][d][h]])
                    P.op("pool", lambda e, s=s, d=d, h=h: e.memset(STb[s][d][h], 0.0), w=[B_STb[s][d][h]])

        qrot = [0]

        def qslot():
            q = qrot[0] % 4
            qrot[0] += 1
            return q

        def process(s, d, c):
            r0 = c * 128
            P.op("sp", lambda e: e.dma_start(out=zc, in_=z_d[s, 1 + r0:1 + r0 + 128, :]), w=[B_zc], dma=True)
            P.op("sp", lambda e: e.dma_start(out=zp, in_=z_d[s, r0:r0 + 128, :]), w=[B_zp], dma=True)
            P.op("sp", lambda e: e.dma_start(out=zn, in_=z_d[s, 2 + r0:2 + r0 + 128, :]), w=[B_zn], dma=True)
            TTo = lambda eng, o, a_, b_, op, r, w: P.op(eng, lambda e: e.tensor_tensor(out=o, in0=a_, in1=b_, op=op), r=r, w=w)
            TTo("dve", zp, zp, zc, ALU.subtract, [B_zp, B_zc], [B_zp])
            TTo("dve", zp, zp, mup, ALU.mult, [B_zp, B_c], [B_zp])
            TTo("pool", zn, zn, zc, ALU.subtract, [B_zn, B_zc], [B_zn])
            TTo("pool", zn, zn, mun, ALU.mult, [B_zn, B_c], [B_zn])
            TTo("dve", zc, zc, zp, ALU.add, [B_zc, B_zp], [B_zc])
            TTo("dve", zc, zc, zn, ALU.add, [B_zc, B_zn], [B_zc])
            r_ = zc[:, 0:512]
            kr = zc[:, 512:1024]
            vr = zc[:, 1024:1536]
            P.op("act", lambda e: e.activation(out=lat[:, 0:64], in_=zc[:, 1536:1600], func=AF.Tanh), r=[B_zc], w=[B_lat])
            P.op("act", lambda e: e.activation(out=lat[:, 128:256], in_=zc[:, 1664:1792], func=AF.Sigmoid), r=[B_zc],
                 pw=[B_lat])
            P.op("dve", lambda e: e.tensor_copy(out=lat[:, 64:128], in_=zc[:, 1600:1664]), r=[B_zc], pw=[B_lat])
            P.op("pe", [lambda e: e.transpose(out=PSB(0, 128, parts=64, off=0), in_=lat[:, 0:64], identity=ident_b),
                        lambda e: e.transpose(out=PSB(0, 128, parts=64, off=128), in_=lat[:, 64:128], identity=ident_b),
                        lambda e: e.transpose(out=PSB(0, 128, off=256), in_=lat[:, 128:256], identity=ident_b)],
                 r=[B_lat, B_ident], w=pq[0])
            P.op("act", lambda e: e.copy(out=latT[0:64, 0:256], in_=PSB(0, 256, parts=64)), r=pq[0], w=[B_latT])
            P.op("act", lambda e: e.copy(out=latT[:, 256:384], in_=PSB(0, 128, off=256)), r=pq[0], pw=[B_latT])
            P.op("pe", lambda e: e.matmul(out=PS(2), lhsT=latT[0:64, 0:128], rhs=wup[0:64, d, :], start=True, stop=True),
                 r=[B_latT, B_c], w=pq[2])
            P.op("pe", lambda e: e.matmul(out=PS(3), lhsT=latT[0:64, 128:256], rhs=aup[0:64, d, :], start=True, stop=True),
                 r=[B_latT, B_c], w=pq[3])
            P.op("dve", lambda e: e.tensor_tensor(out=W["tmp"], in0=PS(2), in1=w0b[:, d, :], op=ALU.add),
                 r=pq[2] + [B_c], w=[BW["tmp"]])
            P.op("act", lambda e: e.activation(out=W["sg"], in_=W["tmp"], func=AF.Sigmoid), r=[BW["tmp"]], w=[BW["sg"]])
            P.op("dve", lambda e: e.tensor_tensor(out=W["tmp"], in0=PS(3), in1=a0b[:, d, :], op=ALU.add),
                 r=pq[3] + [B_c], w=[BW["tmp"]])
            P.op("act", lambda e: e.activation(out=W["a"], in_=W["tmp"], func=AF.Sigmoid), r=[BW["tmp"]], w=[BW["a"]])
            if d == 0:
                P.op("pe", lambda e: e.matmul(out=PS(1), lhsT=latT[:, 256:384], rhs=gup, start=True, stop=True),
                     r=[B_latT, B_c], w=pq[1])
                P.op("act", lambda e: e.copy(out=W["g"], in_=PS(1)), r=pq[1], w=[BW["g"]])
                P.op("sp", lambda e: e.dma_start(out=gv_d[s, r0:r0 + 128, 0:512], in_=W["g"]), r=[BW["g"]],
                     pw=[gv_t[s][c]], dma=True)
                P.op("sp", lambda e: e.dma_start(out=gv_d[s, r0:r0 + 128, 512:1024], in_=vr), r=[B_zc],
                     pw=[gv_t[s][c]], dma=True)
            P.op("pe", lambda e: e.matmul(out=PS(2), lhsT=cm[:, d, 640:768], rhs=W["sg"], start=True, stop=True),
                 r=[BW["sg"], B_c], w=pq[2])
            P.op("pe", lambda e: e.matmul(out=PS(3), lhsT=cm[:, d, 768:896], rhs=W["sg"], start=True, stop=True),
                 r=[BW["sg"], B_c], w=pq[3])
            gq = qslot()
            P.op("pe", [lambda e, h=h: e.matmul(out=PQ(4, gq, 1, parts=64)[:, 0:1] if False else psum[0:64, 4 * 512 + gq * 128 + h:4 * 512 + gq * 128 + h + 1],
                                                  lhsT=W["sg"][:, h * 64:(h + 1) * 64], rhs=negc[:, 0:1], start=True, stop=True)
                        for h in range(8)], r=[BW["sg"], B_c], w=[pq[4][gq]])
            P.op("act", lambda e: e.activation(out=gC[s][d], in_=PQ(4, gq, 8, parts=64), func=AF.Exp), r=[pq[4][gq]],
                 w=[B_gC[s][d]])
            P.op("act", lambda e: e.activation(out=W["E1"], in_=PS(2), func=AF.Exp), r=pq[2], w=[BW["E1"]])
            P.op("act", lambda e: e.activation(out=W["E2"], in_=PS(2), func=AF.Exp, scale=-1.0), r=pq[2], w=[BW["E2"]])
            P.op("act", lambda e: e.activation(out=W["E3"], in_=PS(3), func=AF.Exp), r=pq[3], w=[BW["E3"]])
            P.op("dve", lambda e: e.scalar_tensor_tensor(out=W["tmp"], in0=W["sg"], scalar=CE, in1=PS(2), op0=ALU.mult,
                                                         op1=ALU.add), r=[BW["sg"]] + pq[2], w=[BW["tmp"]])
            P.op("act", lambda e: e.activation(out=W["Ex"], in_=W["tmp"], func=AF.Exp), r=[BW["tmp"]], w=[BW["Ex"]])
            TTo("dve", W["kk"], kr, kkb, ALU.mult, [B_zc, B_c], [BW["kk"]])
            TTo("pool", W["sq"], W["kk"], W["kk"], ALU.mult, [BW["kk"]], [BW["sq"]])
            P.op("dve", lambda e: e.tensor_reduce(out=s8[:, 0:8], in_=W["sq"].rearrange("p (h k) -> p h k", k=64),
                                                  axis=AX.X, op=ALU.add), r=[BW["sq"]], w=[B_s8])
            rstd(s8[:, 8:16], s8[:, 0:8], 1.0, 8, [B_s8], [B_s8], eps=1e-24)
            P.op("dve", lambda e: e.tensor_tensor(out=W["kk"].rearrange("p (h k) -> p h k", k=64),
                                                  in0=W["kk"].rearrange("p (h k) -> p h k", k=64),
                                                  in1=bc(s8[:, 8:16].unsqueeze(2), [128, 8, 64]), op=ALU.mult),
                 r=[BW["kk"], B_s8], w=[BW["kk"]])
            P.op("dve", lambda e: e.scalar_tensor_tensor(out=W["kd"], in0=W["a"], scalar=-1.0, in1=kab, op0=ALU.add,
                                                         op1=ALU.mult), r=[BW["a"], B_c], w=[BW["kd"]])
            P.op("dve", lambda e: e.scalar_tensor_tensor(out=W["kd"], in0=W["kd"], scalar=1.0, in1=kr, op0=ALU.add,
                                                         op1=ALU.mult), r=[BW["kd"], B_zc], w=[BW["kd"]])
            TTo("pool", W["b"], W["kk"], W["a"], ALU.mult, [BW["kk"], BW["a"]], [BW["b"]])
            TTo("pool", W["t2"], r_, W["kd"], ALU.mult, [B_zc, BW["kd"]], [BW["t2"]])
            TTo("pool", W["t2"], W["t2"], rkb, ALU.mult, [BW["t2"], B_c], [BW["t2"]])
            P.op("dve", lambda e: e.tensor_reduce(out=ytile[:, 512:520], in_=W["t2"].rearrange("p (h k) -> p h k", k=64),
                                                  axis=AX.X, op=ALU.add), r=[BW["t2"]], w=[B_y])
            TTo("dve", WB["rbar"], r_, W["E1"], ALU.mult, [B_zc, BW["E1"]], [BWB["rbar"]])
            P.op("dve", lambda e: e.scalar_tensor_tensor(out=WB["abar"], in0=W["kk"], scalar=-1.0, in1=W["Ex"],
                                                         op0=ALU.mult, op1=ALU.mult), r=[BW["kk"], BW["Ex"]],
                 w=[BWB["abar"]])
            TTo("pool", WB["bbar"], W["b"], W["E2"], ALU.mult, [BW["b"], BW["E2"]], [BWB["bbar"]])
            TTo("dve", WB["kbar"], W["kd"], W["E2"], ALU.mult, [BW["kd"], BW["E2"]], [BWB["kbar"]])
            TTo("pool", WB["Bt"], W["b"], W["E3"], ALU.mult, [BW["b"], BW["E3"]], [BWB["Bt"]])
            TTo("pool", WB["Kt"], W["kd"], W["E3"], ALU.mult, [BW["kd"], BW["E3"]], [BWB["Kt"]])
            P.op("act", lambda e: e.copy(out=WB["vq"], in_=vr), r=[B_zc], w=[BWB["vq"]])

            for grp in range(2):
                hs = [4 * grp + i for i in range(4)]
                for i, h in enumerate(hs):
                    hsl = slice(h * 64, (h + 1) * 64)
                    tbk, tb = i // 2, i % 2
                    P.op("pe", [lambda e, ii=ii, nm=nm, tbk=tbk, tb=tb, hsl=hsl: e.transpose(
                        out=PSB(tbk, 128, parts=64, off=tb * 512 + ii * 128), in_=WB[nm][:, hsl], identity=ident_b)
                        for ii, nm in enumerate(("abar", "rbar", "bbar", "kbar"))],
                         r=[BWB["abar"], BWB["rbar"], BWB["bbar"], BWB["kbar"], B_ident],
                         w=pq[tbk] if tb == 0 else [], pw=[] if tb == 0 else pq[tbk])
                for i, h in enumerate(hs):
                    tbk, tb = i // 2, i % 2
                    P.op("act", lambda e, h=h, tbk=tbk, tb=tb: e.copy(out=TT[h], in_=PSB(tbk, 512, parts=64, off=tb * 512)),
                         r=pq[tbk], w=[B_TT[h]])
                for i, h in enumerate(hs):
                    P.op("pe", lambda e, h=h, i=i: e.matmul(out=PQ(4, i), lhsT=TT[h][:, 0:128], rhs=TT[h][:, 256:384],
                                                            start=True, stop=True),
                         r=[B_TT[h]], w=pq[4] if i == 0 else [], pw=[] if i == 0 else pq[4])
                for i, h in enumerate(hs):
                    mb = 2 + (i % 2)
                    P.op("pe", [lambda e, h=h, mb=mb: e.matmul(out=PS(mb, 256), lhsT=TT[h][:, 384:512], rhs=TT[h][:, 0:256],
                                                               start=True, stop=True),
                                lambda e, h=h, mb=mb: e.matmul(out=PS(mb, 256, off=256), lhsT=TT[h][:, 256:384],
                                                               rhs=TT[h][:, 0:256], start=True, stop=True)],
                         r=[B_TT[h]], w=pq[mb])
                    P.op("dve", lambda e, h=h, mb=mb: e.tensor_tensor(out=MM[h], in0=PS(mb), in1=cm[:, d, 0:512],
                                                                      op=ALU.mult), r=pq[mb] + [B_c], w=[B_MM[h]])
                    if i == 1:
                        for i2, h2 in enumerate(hs):
                            P.op("dve", lambda e, h2=h2, i2=i2: e.tensor_tensor(out=PA[h2][0], in0=PQ(4, i2),
                                                                                in1=cm[:, d, 512:640], op=ALU.mult),
                                 r=pq[4] + [B_c], w=[B_PA[h2][0]])
                for i, h in enumerate(hs):
                    hsl = slice(h * 64, (h + 1) * 64)
                    P.op("pe", lambda e, h=h, i=i, hsl=hsl: e.matmul(out=PQ(7, i, 64), lhsT=MM[h][:, 0:128],
                                                                     rhs=WB["vq"][:, hsl], start=True, stop=True),
                         r=[B_MM[h], BWB["vq"]], w=pq[7] if i == 0 else [], pw=[] if i == 0 else pq[7])
                    P.op("pool", lambda e, h=h, hsl=hsl: e.tensor_copy(out=Zb[h][:, 0:64], in_=WB["abar"][:, hsl]),
                         r=[BWB["abar"]], w=[B_Zb[h]])
                for i, h in enumerate(hs):
                    P.op("act", lambda e, h=h, i=i: e.copy(out=Zb[h][:, 64:128], in_=PQ(7, i, 64)), r=pq[7],
                         pw=[B_Zb[h]])
            if True:
                hs = list(range(8))
                cur = {h: 0 for h in hs}
                for j in range(7):
                    for i, h in enumerate(hs):
                        Bp = MM[h][:, 256:384] if j == 0 else PB[h][cur[h]]
                        Bb = B_MM[h] if j == 0 else B_PB[h][cur[h]]
                        P.op("pe", lambda e, h=h, i=i, Bp=Bp: e.matmul(out=PQ(i // 4, i % 4), lhsT=Bp, rhs=Zb[h], start=True, stop=True),
                             r=[Bb, B_Zb[h]], w=[pq[i // 4][i % 4]])
                    if j < 6:
                        for i, h in enumerate(hs):
                            Bp = MM[h][:, 256:384] if j == 0 else PB[h][cur[h]]
                            Bb = B_MM[h] if j == 0 else B_PB[h][cur[h]]
                            Ap = PA[h][cur[h]]
                            Ab = B_PA[h][cur[h]]
                            bk, qq = 3 + i // 2, (i % 2) * 2
                            P.op("pe", lambda e, Ap=Ap, Bp=Bp, bk=bk, qq=qq: e.matmul(out=PQ(bk, qq), lhsT=Ap, rhs=Bp,
                                                                                       start=True, stop=True),
                                 r=[Ab, Bb], w=[pq[bk][qq]])
                            if j < 5:
                                P.op("pe", lambda e, Ap=Ap, Bp=Bp, bk=bk, qq=qq: e.matmul(out=PQ(bk, qq + 1), lhsT=Bp, rhs=Ap,
                                                                                           start=True, stop=True),
                                     r=[Ab, Bb], w=[pq[bk][qq + 1]])
                    for i, h in enumerate(hs):
                        P.op("dve", lambda e, h=h, i=i: e.tensor_tensor(out=Zb[h], in0=PQ(i // 4, i % 4), in1=Zb[h], op=ALU.add),
                             r=[pq[i // 4][i % 4], B_Zb[h]], w=[B_Zb[h]])
                    if j < 6:
                        for i, h in enumerate(hs):
                            nx = 1 - cur[h]
                            bk, qq = 3 + i // 2, (i % 2) * 2
                            P.op("act", lambda e, h=h, nx=nx, bk=bk, qq=qq: e.copy(out=PB[h][nx], in_=PQ(bk, qq)),
                                 r=[pq[bk][qq]], w=[B_PB[h][nx]])
                            if j < 5:
                                P.op("act" if i % 2 else "dve",
                                     (lambda e, h=h, nx=nx, bk=bk, qq=qq: e.copy(out=PA[h][nx], in_=PQ(bk, qq + 1))) if i % 2 else
                                     (lambda e, h=h, nx=nx, bk=bk, qq=qq: e.tensor_scalar(out=PA[h][nx], in0=PQ(bk, qq + 1),
                                                                                            scalar1=1.0, scalar2=None,
                                                                                            op0=ALU.mult)),
                                     r=[pq[bk][qq + 1]], w=[B_PA[h][nx]])
                            cur[h] = nx
                for i, h in enumerate(hs):
                    P.op("pe", lambda e, h=h, i=i: e.transpose(out=PSB(0, 128, parts=64, off=i * 128), in_=Zb[h][:, 0:64],
                                                               identity=ident_b),
                         r=[B_Zb[h], B_ident], w=pq[0] if i == 0 else [], pw=[] if i == 0 else pq[0])
                for i, h in enumerate(hs):
                    P.op("act", lambda e, h=h, i=i: e.copy(out=WbT[h], in_=PSB(0, 128, parts=64, off=i * 128)),
                         r=pq[0], w=[B_WbT[h]])
                for i, h in enumerate(hs):
                    P.op("pe", lambda e, h=h, i=i: e.matmul(out=PS(1, 64, off=i * 64), lhsT=WbT[h], rhs=STb[s][d][h], start=True,
                                                            stop=True),
                         r=[B_WbT[h], B_STb[s][d][h]], w=pq[1] if i == 0 else [], pw=[] if i == 0 else pq[1])
                for i, h in enumerate(hs):
                    P.op("dve", lambda e, h=h, i=i: e.tensor_tensor(out=Ub[h], in0=PS(1, 64, off=i * 64), in1=Zb[h][:, 64:128],
                                                                    op=ALU.add), r=pq[1] + [B_Zb[h]], w=[B_Ub[h]])
                for i, h in enumerate(hs):
                    hsl = slice(h * 64, (h + 1) * 64)
                    P.op("pe", [lambda e, h=h, i=i, hsl=hsl: e.matmul(out=PS(2, 64, off=i * 64), lhsT=TT[h][:, 128:256],
                                                                      rhs=STb[s][d][h], start=True, stop=False),
                                lambda e, h=h, i=i, hsl=hsl: e.matmul(out=PS(2, 64, off=i * 64), lhsT=MM[h][:, 384:512],
                                                                      rhs=Ub[h], start=False, stop=False),
                                lambda e, h=h, i=i, hsl=hsl: e.matmul(out=PS(2, 64, off=i * 64), lhsT=MM[h][:, 128:256],
                                                                      rhs=WB["vq"][:, hsl], start=False, stop=True)],
                         r=[B_TT[h], B_STb[s][d][h], B_MM[h], B_Ub[h], BWB["vq"]],
                         w=pq[2] if i == 0 else [], pw=[] if i == 0 else pq[2])
                    P.op("pe", [lambda e, h=h, i=i, hsl=hsl: e.matmul(out=PS(4, 64, parts=64, off=i * 64), lhsT=WB["Bt"][:, hsl],
                                                                      rhs=Ub[h], start=True, stop=False),
                                lambda e, h=h, i=i, hsl=hsl: e.matmul(out=PS(4, 64, parts=64, off=i * 64), lhsT=WB["Kt"][:, hsl],
                                                                      rhs=WB["vq"][:, hsl], start=False, stop=True)],
                         r=[BWB["Bt"], BWB["Kt"], B_Ub[h], BWB["vq"]], w=pq[4] if i == 0 else [], pw=[] if i == 0 else pq[4])
                for i, h in enumerate(hs):
                    P.op("dve", lambda e, h=h, i=i: e.scalar_tensor_tensor(
                        out=ST[s][d][h], in0=ST[s][d][h], scalar=gC[s][d][:, h:h + 1], in1=PS(4, 64, parts=64, off=i * 64),
                        op0=ALU.mult, op1=ALU.add), r=[B_ST[s][d][h], B_gC[s][d]] + pq[4], w=[B_ST[s][d][h]])
                    P.op("pool", lambda e, h=h: e.tensor_copy(out=STb[s][d][h], in_=ST[s][d][h]), r=[B_ST[s][d][h]],
                         w=[B_STb[s][d][h]])
                P.op("act", lambda e: e.copy(out=ytile[:, 0:512], in_=PS(2)), r=pq[2], pw=[B_y])
            P.op("sp", lambda e: e.dma_start(out=yd_d[d][s, r0:r0 + 128, :], in_=ytile), r=[B_y], w=[yd_t[d][s][c]],
                 dma=True)

        for i in range(NT):
            for s in range(NS):
                process(s, 0, i)
                process(s, 1, NT - 1 - i)

        yf = zc[:, 0:520]
        yb = zp[:, 0:520]
        gvt = zn[:, 0:1024]
        for s in range(NS):
            for c in range(NT):
                r0 = c * 128
                P.op("sp", lambda e, s=s, r0=r0: e.dma_start(out=yf, in_=yd_d[0][s, r0:r0 + 128, :]), r=[yd_t[0][s][c]],
                     w=[B_zc], dma=True)
                P.op("sp", lambda e, s=s, r0=r0: e.dma_start(out=yb, in_=yd_d[1][s, r0:r0 + 128, :]), r=[yd_t[1][s][c]],
                     w=[B_zp], dma=True)
                P.op("sp", lambda e, s=s, r0=r0: e.dma_start(out=gvt, in_=gv_d[s, r0:r0 + 128, :]), r=[gv_t[s][c]],
                     w=[B_zn], dma=True)
                y3 = yf[:, 0:512].rearrange("p (h k) -> p h k", k=64)
                P.op("dve", lambda e: e.tensor_tensor(out=yf, in0=yf, in1=yb, op=ALU.add), r=[B_zc, B_zp], w=[B_zc])
                P.op("dve", lambda e: e.tensor_reduce(out=s8[:, 0:8], in_=y3, axis=AX.X, op=ALU.add), r=[B_zc], w=[B_s8])
                P.op("dve", lambda e: e.tensor_scalar(out=s8[:, 0:8], in0=s8[:, 0:8], scalar1=1.0 / 64.0, scalar2=None,
                                                      op0=ALU.mult), r=[B_s8], w=[B_s8])
                P.op("dve", lambda e: e.tensor_tensor(out=y3, in0=y3, in1=bc(s8[:, 0:8].unsqueeze(2), [128, 8, 64]),
                                                      op=ALU.subtract), r=[B_zc, B_s8], w=[B_zc])
                P.op("pool", lambda e: e.tensor_tensor(out=W["sq"], in0=yf[:, 0:512], in1=yf[:, 0:512], op=ALU.mult),
                     r=[B_zc], w=[BW["sq"]])
                P.op("dve", lambda e: e.tensor_reduce(out=s8[:, 0:8], in_=W["sq"].rearrange("p (h k) -> p h k", k=64),
                                                      axis=AX.X, op=ALU.add), r=[BW["sq"]], w=[B_s8])
                rstd(s8[:, 8:16], s8[:, 0:8], 1.0 / 64.0, 8, [B_s8], [B_s8], eps=GN_EPS)
                P.op("dve", lambda e: e.tensor_tensor(out=y3, in0=y3, in1=bc(s8[:, 8:16].unsqueeze(2), [128, 8, 64]),
                                                      op=ALU.mult), r=[B_zc, B_s8], w=[B_zc])
                P.op("dve", lambda e: e.tensor_tensor(out=yf[:, 0:512], in0=yf[:, 0:512], in1=lgb, op=ALU.mult),
                     r=[B_zc, B_c], w=[B_zc])
                P.op("dve", lambda e: e.tensor_tensor(out=yf[:, 0:512], in0=yf[:, 0:512], in1=lbb, op=ALU.add),
                     r=[B_zc, B_c], w=[B_zc])
                P.op("pool", lambda e: e.scalar_tensor_tensor(
                    out=W["t2"].rearrange("p (h k) -> p h k", k=64), in0=gvt[:, 512:1024].rearrange("p (h k) -> p h k", k=64),
                    scalar=0.5, in1=bc(yf[:, 512:520].unsqueeze(2), [128, 8, 64]), op0=ALU.mult, op1=ALU.mult)
                    if False else e.tensor_tensor(
                    out=W["t2"].rearrange("p (h k) -> p h k", k=64), in0=gvt[:, 512:1024].rearrange("p (h k) -> p h k", k=64),
                    in1=bc(yf[:, 512:520].unsqueeze(2), [128, 8, 64]), op=ALU.mult), r=[B_zc, B_zn], w=[BW["t2"]])
                P.op("dve", lambda e: e.scalar_tensor_tensor(out=yf[:, 0:512], in0=W["t2"], scalar=0.5, in1=yf[:, 0:512],
                                                             op0=ALU.mult, op1=ALU.add), r=[BW["t2"], B_zc], w=[B_zc])
                P.op("dve", lambda e: e.tensor_tensor(out=yf[:, 0:512], in0=yf[:, 0:512], in1=gvt[:, 0:512], op=ALU.mult),
                     r=[B_zc, B_zn], w=[B_zc])
                P.op("sp", lambda e, s=s, r0=r0: e.dma_start(out=rw_d[s, r0:r0 + 128, :], in_=yf[:, 0:512]), r=[B_zc],
                     w=[rw_t[s][c]], dma=True)
        P.barrier()

    @_phase(4)
    def _p4():
        ar.reset()
        w_out_b = ar.alloc([8, D], BF16)
        Ws_b = ar.alloc([8, 2048], BF16)
        gT2 = ar.alloc([4], F32)
        g2_b = ar.alloc([D], F32)
        NG = 8
        stg_ = ar.alloc([4, D], F32)
        gsl = ar.alloc([NG, D], BF16)
        B_g = [Buf(f"gs{k_}") for k_ in range(NG)]
        stage = [stg_[:, 0:2, :].rearrange("p a b -> p (a b)"), stg_[:, 2:4, :].rearrange("p a b -> p (a b)")]
        B_stage = [[Buf("stg0")], [Buf("stg1")]]
        ubf_d = nc.dram_tensor("ubf_s", [NE, D], BF16, kind="Internal").ap()
        vbf_d = nc.dram_tensor("vbf_s", [NE, D], BF16, kind="Internal").ap()
        B_ubf, B_vbf = Buf("ubf"), Buf("vbf")
        for (src_, dst_, bb_) in ((expert_u, ubf_d, B_ubf), (expert_v, vbf_d, B_vbf)):
            for q_ in range(8):
                P.op("pool", lambda e, src_=src_, dst_=dst_, q_=q_: e.dma_start(
                    out=dst_[q_ * 2048:(q_ + 1) * 2048, :], in_=src_[q_ * 2048:(q_ + 1) * 2048, :]), pw=[bb_], dma=True)
        B_w = Buf("w4")
        B_ws = Buf("ws")
        B_gT = Buf("gT4")
        B_g2 = Buf("g2")
        load_colT(gT2[:, 0:4], attn_out_g, 4, B_gT)
        P.op("sp", lambda e: e.dma_start(out=g2_b, in_=bc(norm2_g.unsqueeze(0), [128, D])), w=[B_g2], dma=True)
        for c in range(8):
            st, bs = stage[c % 2], B_stage[c % 2]
            P.op("sp", lambda e, st=st, c=c: e.dma_start(out=st[:, 0:D], in_=w_out[c * 128:(c + 1) * 128, :]), w=bs, dma=True)
            if c < 4:
                P.op("dve", lambda e, st=st, c=c: e.tensor_scalar(out=w_out_b[:, c, :], in0=st[:, 0:D], scalar1=gT2[:, c:c + 1],
                                                                  scalar2=None, op0=ALU.mult), r=bs + [B_gT], pw=[B_w])
            else:
                P.op("dve", lambda e, st=st, c=c: e.tensor_copy(out=w_out_b[:, c, :], in_=st[:, 0:D]), r=bs, pw=[B_w])
        skT = ar.alloc([2, 128], F32)
        B_sk = Buf("skT")
        wT4 = ar.alloc([4, 128], F32)
        B_wT4 = Buf("wT4")
        for hf in range(2):
            P.op("sp", lambda e, hf=hf: e.dma_start(out=stage[0][:, hf * 128:(hf + 1) * 128], in_=sub_keys[hf]),
                 w=B_stage[0] if hf == 0 else [], pw=[] if hf == 0 else B_stage[0], dma=True)
        P.op("pe", [lambda e, hf=hf: e.transpose(out=PS(0, 128, off=hf * 128), in_=stage[0][:, hf * 128:(hf + 1) * 128],
                                                 identity=ident_f) for hf in range(2)], r=B_stage[0] + [B_ident], w=[pbank[0]])
        P.op("act", lambda e: e.copy(out=skT.rearrange("p a b -> p (a b)"), in_=PS(0, 256)), r=[pbank[0]], w=[B_sk])
        for dc in range(8):
            st, bs = stage[(dc + 1) % 2], B_stage[(dc + 1) % 2]
            P.op("sp", lambda e, st=st, dc=dc: e.dma_start(out=st, in_=w_pq[dc * 128:(dc + 1) * 128, :]), w=bs, dma=True)
            for g4 in range(4):
                P.op("pe", [lambda e, st=st, jj=jj, g4=g4: e.transpose(
                    out=PS(1, 128, off=jj * 128), in_=st[:, (g4 * 4 + jj) * 128:(g4 * 4 + jj + 1) * 128], identity=ident_f)
                    for jj in range(4)], r=bs + [B_ident], w=[pbank[1]])
                P.op("act", lambda e: e.copy(out=wT4.rearrange("p a b -> p (a b)"), in_=PS(1)), r=[pbank[1]], w=[B_wT4])
                P.op("pe", [lambda e, jj=jj: e.matmul(out=PS(2, 128, off=jj * 128), lhsT=wT4[:, jj, :], rhs=skT[:, jj % 2, :],
                                                      start=True, stop=True) for jj in range(4)],
                     r=[B_wT4, B_sk], w=[pbank[2]])
                P.op("act", lambda e, dc=dc, g4=g4: e.copy(out=Ws_b[:, dc, g4 * 512:(g4 + 1) * 512], in_=PS(2)),
                     r=[pbank[2]], pw=[B_ws])

        xs = [ar.alloc([D], F32) for _ in range(2)]
        B_xs = [Buf("xs0"), Buf("xs1")]
        at = ar.alloc([D], F32)
        B_at = Buf("at")
        st4 = ar.alloc([8], F32)
        B_st = Buf("st4")
        catb = ar.alloc([D], BF16)
        B_catb = Buf("catb")
        catT = ar.alloc([8, 128], BF16)
        B_catT = Buf("catT")
        h_sb = [ar.alloc([D], F32) for _ in range(2)]
        B_h = [Buf("h0"), Buf("h1")]
        hn = ar.alloc([D], F32)
        B_hn = Buf("hn")
        hnb = [ar.alloc([D], BF16) for _ in range(2)]
        B_hnb = [Buf("hnb0"), Buf("hnb1")]
        s_sb = ar.alloc([16, 128], F32)
        B_s = Buf("s_sb")
        scr2 = ar.alloc([2048], F32)
        B_scr2 = Buf("scr2")
        cand = ar.alloc([8, 256], F32)
        B_cand = Buf("cand")
        eq = ar.alloc([8, 16, 16], F32)
        B_eq = Buf("eq")
        v16 = ar.alloc([16, 16], F32)
        i16 = ar.alloc([16, 16], U32)
        i16f = ar.alloc([16, 16], F32)
        B_v16, B_i16, B_i16f = Buf("v16"), Buf("i16"), Buf("i16f")
        b16 = ar.alloc([8, 16], F32)
        p16 = ar.alloc([8, 16], U32)
        pa = ar.alloc([8, 16], U32)
        pb_ = ar.alloc([8, 16], U32)
        paf = ar.alloc([8, 16], F32)
        pbf = ar.alloc([8, 16], F32)
        B_b16, B_p16, B_pab = Buf("b16"), Buf("p16"), Buf("pab")
        sel = ar.alloc([2, 8, 16], F32)
        B_sel = Buf("sel")
        idxf = ar.alloc([128], F32)
        B_idxf = Buf("idxf")
        idxu = [ar.alloc([128], U32) for _ in range(2)]
        B_idx = [Buf("idx0"), Buf("idx1")]
        gate = [ar.alloc([8, 16], F32) for _ in range(2)]
        B_gate = [Buf("gate0"), Buf("gate1")]
        gs8 = ar.alloc([16], F32)
        B_gs8 = Buf("gs8")
        actv = ar.alloc([128], F32)
        B_actv = Buf("actv")
        wgt = ar.alloc([128], F32)
        B_wgt = Buf("wgt")
        acc = ar.alloc([D], F32)
        B_acc = Buf("acc")
        junkb = ar.alloc([D], BF16)
        B_junkb = Buf("junkb")
        NDG = 4
        dg = [ar.alloc([128], BF16) for _ in range(NDG)]
        B_dg = [Buf(f"dg{i_}") for i_ in range(NDG)]
        iota16 = ar.alloc([16], F32)
        B_io = Buf("iota")
        P.op("pool", lambda e: e.iota(iota16, pattern=[[1, 16]], base=0, channel_multiplier=0,
                                      allow_small_or_imprecise_dtypes=True), w=[B_io])
        junk = scr2[:, 0:D]
        gk = [0]

        def front(s, t, par):
            r0 = t * 128
            xsp, bxs, hp, bh = xs[par], B_xs[par], h_sb[par], B_h[par]
            P.op("sp", lambda e: e.dma_start(out=xsp, in_=x[s, r0:r0 + 128, :]), w=[bxs], dma=True)
            P.op("sp", lambda e: e.dma_start(out=at[:, 0:512], in_=attn_d[s, r0:r0 + 128, :]),
                 r=[attn_t[s][t]], w=[B_at], dma=True)
            P.op("sp", lambda e: e.dma_start(out=at[:, 512:1024], in_=rw_d[s, r0:r0 + 128, :]),
                 r=[rw_t[s][t]], pw=[B_at], dma=True)
            P.op("act", lambda e: e.activation(out=junk[:, 0:512], in_=at[:, 0:512], func=AF.Square,
                                               scale=float(512 ** -0.5), accum_out=st4[:, 0:1]),
                 r=[B_at], w=[B_scr2], pw=[B_st])
            rstd(st4[:, 1:2], st4[:, 0:1], 1.0, 1, [B_st], [B_st])
            P.op("dve", lambda e: e.tensor_scalar(out=catb[:, 0:512], in0=at[:, 0:512],
                                                  scalar1=st4[:, 1:2], scalar2=None, op0=ALU.mult),
                 r=[B_at, B_st], w=[B_catb])
            P.op("act", lambda e: e.copy(out=catb[:, 512:1024], in_=at[:, 512:1024]), r=[B_at], pw=[B_catb])
            P.op("pe", [lambda e, c=c: e.transpose(out=PSB(0, 128, off=c * 128), in_=catb[:, c * 128:(c + 1) * 128],
                                                   identity=ident_b) for c in range(8)],
                 r=[B_catb, B_ident], w=[pbank[0]])
            P.op("act", lambda e: e.copy(out=catT.rearrange("p a b -> p (a b)"), in_=PSB(0)), r=[pbank[0]], w=[B_catT])
            fns = []
            for j in range(2):
                for c in range(8):
                    fns.append(lambda e, j=j, c=c: e.matmul(out=PS(1 + j), lhsT=catT[:, c, :],
                                                            rhs=w_out_b[:, c, j * 512:(j + 1) * 512],
                                                            start=(c == 0), stop=(c == 7)))
            P.op("pe", fns, r=[B_catT, B_w], w=[pbank[1], pbank[2]])
            for j in range(2):
                P.op("dve", lambda e, j=j: e.tensor_tensor(
                    out=hp[:, j * 512:(j + 1) * 512], in0=PS(1 + j), in1=xsp[:, j * 512:(j + 1) * 512], op=ALU.add),
                    r=[pbank[1 + j], bxs], w=[bh] if j == 0 else [], pw=[] if j == 0 else [bh])
            yield
            P.op("act", lambda e: e.activation(out=junk, in_=hp, func=AF.Square, scale=1.0 / 32.0,
                                               accum_out=st4[:, 2:3]), r=[bh], w=[B_scr2], pw=[B_st])
            rstd(st4[:, 3:4], st4[:, 2:3], 1.0, 1, [B_st], [B_st])
            P.op("dve", lambda e: e.scalar_tensor_tensor(out=hn, in0=hp, scalar=st4[:, 3:4], in1=g2_b,
                                                         op0=ALU.mult, op1=ALU.mult),
                 r=[bh, B_st, B_g2], w=[B_hn])
            P.op("act", lambda e: e.copy(out=catb, in_=hn), r=[B_hn], w=[B_catb])
            P.op("act", lambda e: e.copy(out=hnb[par], in_=hn), r=[B_hn], w=[B_hnb[par]])
            P.op("pe", [lambda e, c=c: e.transpose(out=PSB(0, 128, off=c * 128), in_=catb[:, c * 128:(c + 1) * 128],
                                                   identity=ident_b) for c in range(8)],
                 r=[B_catb, B_ident], w=[pbank[0]])
            P.op("act", lambda e: e.copy(out=catT.rearrange("p a b -> p (a b)"), in_=PSB(0)), r=[pbank[0]], w=[B_catT])
            sf = s_sb.rearrange("p a b -> p (a b)")
            for half in range(2):
                fns = []
                for jj in range(2):
                    j = half * 2 + jj
                    for c in range(8):
                        fns.append(lambda e, j=j, jj=jj, c=c: e.matmul(out=PS(3 + jj), lhsT=catT[:, c, :],
                                                                      rhs=Ws_b[:, c, j * 512:(j + 1) * 512],
                                                                      start=(c == 0), stop=(c == 7)))
                P.op("pe", fns, r=[B_catT, B_ws], w=[pbank[3], pbank[4]])
                for jj in range(2):
                    j = half * 2 + jj
                    P.op("act", lambda e, j=j, jj=jj: e.copy(out=sf[:, j * 512:(j + 1) * 512], in_=PS(3 + jj)),
                         r=[pbank[3 + jj]], w=[B_s] if j == 0 else [], pw=[] if j == 0 else [B_s])
            yield
            s2 = scr2.rearrange("p (a b) -> p a b", b=128)
            for j in range(16):
                fl = (j == 0)
                P.op("dve", lambda e, j=j: e.max(out=v16[:, j, 0:8], in_=s_sb[:, j, :]), r=[B_s],
                     w=[B_v16] if fl else [], pw=[] if fl else [B_v16])
                P.op("dve", lambda e, j=j: e.max_index(out=i16[:, j, 0:8], in_max=v16[:, j, 0:8], in_values=s_sb[:, j, :]),
                     r=[B_s, B_v16], w=[B_i16] if fl else [], pw=[] if fl else [B_i16])
                P.op("dve", lambda e, j=j: e.match_replace(out=s2[:, j, :], in_to_replace=v16[:, j, 0:8],
                                                           in_values=s_sb[:, j, :], imm_value=-1e30),
                     r=[B_s, B_v16], w=[B_scr2] if fl else [], pw=[] if fl else [B_scr2])
                P.op("dve", lambda e, j=j: e.max(out=v16[:, j, 8:16], in_=s2[:, j, :]), r=[B_scr2], pw=[B_v16])
                P.op("dve", lambda e, j=j: e.max_index(out=i16[:, j, 8:16], in_max=v16[:, j, 8:16], in_values=s2[:, j, :]),
                     r=[B_scr2, B_v16], pw=[B_i16])
                if j % 4 == 3:
                    yield
            P.op("dve", lambda e: e.tensor_copy(out=i16f, in_=i16), r=[B_i16], w=[B_i16f])
            v4 = v16.rearrange("p (h f) k -> p h f k", f=2)
            i4 = i16f.rearrange("p (h f) k -> p h f k", f=2)
            P.op("dve", lambda e: e.tensor_tensor(out=cand.rearrange("p h (a b) -> p h a b", b=16),
                                                  in0=bc(v4[:, :, 0, :].unsqueeze(3), [128, 8, 16, 16]),
                                                  in1=bc(v4[:, :, 1, :].unsqueeze(2), [128, 8, 16, 16]), op=ALU.add),
                 r=[B_v16], w=[B_cand])
            c2 = scr2.rearrange("p (a b) -> p a b", b=256)
            for h in range(8):
                fl = (h == 0)
                P.op("dve", lambda e, h=h: e.max(out=b16[:, h, 0:8], in_=cand[:, h, :]), r=[B_cand],
                     w=[B_b16] if fl else [], pw=[] if fl else [B_b16])
                P.op("dve", lambda e, h=h: e.max_index(out=p16[:, h, 0:8], in_max=b16[:, h, 0:8], in_values=cand[:, h, :]),
                     r=[B_cand, B_b16], w=[B_p16] if fl else [], pw=[] if fl else [B_p16])
                P.op("dve", lambda e, h=h: e.match_replace(out=c2[:, h, :], in_to_replace=b16[:, h, 0:8],
                                                           in_values=cand[:, h, :], imm_value=-1e30),
                     r=[B_cand, B_b16], w=[B_scr2] if fl else [], pw=[] if fl else [B_scr2])
                P.op("dve", lambda e, h=h: e.max(out=b16[:, h, 8:16], in_=c2[:, h, :]), r=[B_scr2], pw=[B_b16])
                P.op("dve", lambda e, h=h: e.max_index(out=p16[:, h, 8:16], in_max=b16[:, h, 8:16], in_values=c2[:, h, :]),
                     r=[B_scr2, B_b16], pw=[B_p16])
                if h % 4 == 3:
                    yield
            P.op("dve", lambda e: e.tensor_single_scalar(out=pa, in_=p16, scalar=4, op=ALU.logical_shift_right),
                 r=[B_p16], w=[B_pab])
            P.op("dve", lambda e: e.tensor_single_scalar(out=pb_, in_=p16, scalar=15, op=ALU.bitwise_and),
                 r=[B_p16], pw=[B_pab])
            P.op("dve", lambda e: e.tensor_copy(out=paf, in_=pa), r=[B_pab], pw=[B_pab])
            P.op("dve", lambda e: e.tensor_copy(out=pbf, in_=pb_), r=[B_pab], pw=[B_pab])
            io4 = bc(iota16.unsqueeze(1).unsqueeze(1), [128, 8, 16, 16])
            for (k_, pf) in ((0, paf), (1, pbf)):
                P.op("dve", lambda e, pf=pf: e.tensor_tensor(out=eq, in0=io4, in1=bc(pf.unsqueeze(3), [128, 8, 16, 16]),
                                                             op=ALU.is_equal), r=[B_io, B_pab], w=[B_eq])
                P.op("dve", lambda e, k_=k_: e.tensor_tensor(out=eq, in0=eq,
                                                             in1=bc(i4[:, :, k_, :].unsqueeze(2), [128, 8, 16, 16]),
                                                             op=ALU.mult), r=[B_eq, B_i16f], w=[B_eq])
                P.op("dve", lambda e, k_=k_: e.tensor_reduce(out=sel[:, k_], in_=eq, axis=AX.X, op=ALU.add),
                     r=[B_eq], w=[B_sel] if k_ == 0 else [], pw=[] if k_ == 0 else [B_sel])
                yield
            P.op("dve", lambda e: e.scalar_tensor_tensor(out=idxf.rearrange("p (h k) -> p h k", k=16), in0=sel[:, 0],
                                                         scalar=128.0, in1=sel[:, 1], op0=ALU.mult, op1=ALU.add),
                 r=[B_sel], w=[B_idxf])
            P.op("dve", lambda e: e.tensor_scalar(out=idxf, in0=idxf, scalar1=0.0, scalar2=float(NE - 1), op0=ALU.max,
                                                  op1=ALU.min), r=[B_idxf], w=[B_idxf])
            P.op("dve", lambda e: e.tensor_copy(out=idxu[par], in_=idxf), r=[B_idxf], w=[B_idx[par]])
            gt = gate[par]
            P.op("dve", lambda e: e.tensor_tensor(out=gt, in0=b16, in1=bc(b16[:, :, 0:1], [128, 8, 16]),
                                                  op=ALU.subtract), r=[B_b16], w=[B_gate[par]])
            P.op("act", lambda e: e.activation(out=gt, in_=gt, func=AF.Exp), r=[B_gate[par]], w=[B_gate[par]])
            P.op("dve", lambda e: e.tensor_reduce(out=gs8[:, 0:8], in_=gt, axis=AX.X, op=ALU.add), r=[B_gate[par]],
                 w=[B_gs8])
            P.op("dve", lambda e: e.reciprocal(out=gs8[:, 8:16], in_=gs8[:, 0:8]), r=[B_gs8], pw=[B_gs8])
            P.op("dve", lambda e: e.tensor_tensor(out=gt, in0=gt, in1=bc(gs8[:, 8:16].unsqueeze(2), [128, 8, 16]),
                                                  op=ALU.mult), r=[B_gate[par], B_gs8], w=[B_gate[par]])
            yield

        def back(s, t, par, nxt):
            r0 = t * 128
            hp, bh = h_sb[par], B_h[par]

            def adv():
                if nxt is not None:
                    next(nxt, None)

            for j in range(128):
                k_ = gk[0] % NG
                gk[0] += 1
                P.op("pool", lambda e, j=j, k_=k_: e.indirect_dma_start(
                    out=gsl[:, k_, :], out_offset=None, in_=ubf_d,
                    in_offset=bass.IndirectOffsetOnAxis(ap=idxu[par][:, j:j + 1], axis=0)),
                    r=[B_idx[par], B_ubf], w=[B_g[k_]], dma=True)
                P.op("dve", lambda e, j=j, k_=k_: e.scalar_tensor_tensor(
                    out=junkb, in0=gsl[:, k_, :], scalar=1.0, in1=hnb[par], op0=ALU.mult, op1=ALU.mult,
                    accum_out=actv[:, j:j + 1]), r=[B_g[k_], B_hnb[par]], w=[B_junkb],
                    pw=[B_actv])
                if j % 16 == 15:
                    adv()
            P.op("act", lambda e: e.activation(out=wgt, in_=actv, func=AF.Gelu), r=[B_actv], w=[B_wgt])
            P.op("dve", lambda e: e.tensor_tensor(out=wgt, in0=wgt, in1=gate[par].rearrange("p h k -> p (h k)"), op=ALU.mult),
                 r=[B_wgt, B_gate[par]], w=[B_wgt])
            for j in range(128):
                k_ = gk[0] % NG
                gk[0] += 1
                kd_ = j % NDG
                P.op("pool", lambda e, j=j, k_=k_: e.indirect_dma_start(
                    out=gsl[:, k_, :], out_offset=None, in_=vbf_d,
                    in_offset=bass.IndirectOffsetOnAxis(ap=idxu[par][:, j:j + 1], axis=0)),
                    r=[B_idx[par], B_vbf], w=[B_g[k_]], dma=True)
                P.op("dve", lambda e, j=j, kd_=kd_: e.tensor_scalar(out=dg[kd_], in0=ident_b, scalar1=wgt[:, j:j + 1],
                                                                    scalar2=None, op0=ALU.mult),
                     r=[B_ident, B_wgt], w=[B_dg[kd_]])
                P.op("pe", [lambda e, j=j, k_=k_, kd_=kd_, hh=hh: e.matmul(
                    out=PS(5 + hh), lhsT=dg[kd_], rhs=gsl[:, k_, hh * 512:(hh + 1) * 512], start=(j == 0), stop=(j == 127))
                    for hh in range(2)], r=[B_g[k_], B_dg[kd_]],
                    w=[pbank[5], pbank[6]] if j == 0 else [], pw=[] if j == 0 else [pbank[5], pbank[6]])
                if j % 16 == 15:
                    adv()
            for hh in range(2):
                P.op("dve", lambda e, hh=hh: e.tensor_tensor(out=acc[:, hh * 512:(hh + 1) * 512], in0=PS(5 + hh),
                                                             in1=hp[:, hh * 512:(hh + 1) * 512], op=ALU.add),
                     r=[pbank[5 + hh], bh], w=[B_acc] if hh == 0 else [], pw=[] if hh == 0 else [B_acc])
            o = P.op("sp", lambda e: e.dma_start(out=y[s, r0:r0 + 128, :], in_=acc), r=[B_acc], dma=True)
            out_ops.append(o)

        tiles = [(s, t) for s in range(NS) for t in range(NT)]
        g0 = front(tiles[0][0], tiles[0][1], 0)
        for _ in g0:
            pass
        for i_, (s, t) in enumerate(tiles):
            nxt = front(tiles[i_ + 1][0], tiles[i_ + 1][1], (i_ + 1) % 2) if i_ + 1 < len(tiles) else None
            back(s, t, i_ % 2, nxt)
            if nxt is not None:
                for _ in nxt:
                    pass
    P.barrier()
    P.finalize()
    es.close()
    return nc


_CACHE = {}


def _consts(SEQ):
    ident = np.eye(128, dtype=np.float32)
    half = 16
    inv = 1.0 / (10000.0 ** (np.arange(half, dtype=np.float32) / half))
    ang = np.arange(SEQ, dtype=np.float32)[:, None] * inv[None, :].astype(np.float32)
    rope = np.concatenate([np.cos(ang), np.sin(ang)], axis=1).astype(np.float32)
    ce = np.float32(np.exp(-0.5))
    idx = np.arange(128)
    cmask = np.zeros((128, 2, 896), np.float32)
    for d in range(2):
        strict = (idx[:, None] < idx[None, :]) if d == 0 else (idx[:, None] > idx[None, :])
        strict = strict.astype(np.float32)
        incl = strict + np.eye(128, dtype=np.float32)
        cmask[:, d, 0:128] = strict
        cmask[:, d, 128:256] = incl
        cmask[:, d, 256:384] = strict
        cmask[:, d, 384:512] = incl
        cmask[:, d, 512:640] = strict.T
        cmask[:, d, 640:768] = -ce * incl
        cmask[:, d, 768:896] = -ce * strict.T
    cmask = cmask.reshape(128, 1792)
    return dict(ident=ident, rope=rope, cmask=cmask)


WNAMES = ["norm1_g", "w_in", "q_lat_g", "w_uq", "kv_lat_g", "w_ukv", "q_norm_g", "k_norm_g", "attn_out_g",
          "mu_prev", "mu_next", "w0", "w_up", "a0", "a_up", "g_up", "k_k", "k_a", "r_k", "ln_x_g", "ln_x_b",
          "w_out", "norm2_g", "w_pq", "sub_keys", "expert_u", "expert_v"]


def kernel(**inputs):
    xp = np.asarray(inputs["x_prompt"], dtype=np.float32)
    xsm = np.asarray(inputs["x_sample"], dtype=np.float32)
    SEQ = xp.shape[1]
    seqs = [xp[i] for i in range(xp.shape[0])] + [xsm[i] for i in range(xsm.shape[0])]
    n = 8
    assign = [(c, 8 + c if 8 + c < len(seqs) else c) for c in range(n)]
    key = (SEQ, 2)
    if key not in _CACHE:
        _CACHE[key] = build(SEQ, 2)
    nc = _CACHE[key]
    base = {k: np.ascontiguousarray(np.asarray(inputs[k], dtype=np.float32)[0]) for k in WNAMES}
    base.update(_consts(SEQ))
    in_maps = []
    for c in range(n):
        m = dict(base)
        m["x"] = np.ascontiguousarray(np.stack([seqs[assign[c][0]], seqs[assign[c][1]]]))
        in_maps.append(m)
    res = run_bass_kernel_spmd(nc, in_maps, core_ids=list(range(n)))
    outs = [None] * len(seqs)
    for c in range(n):
        yc = res.results[c]["y"]
        outs[assign[c][0]] = yc[0]
        if 8 + c < len(seqs):
            outs[8 + c] = yc[1]
    nb = xp.shape[0]
    return (np.stack(outs[:nb]).astype(np.float32), np.stack(outs[nb:]).astype(np.float32))
```

```python
import numpy as np
import concourse.bass as bass
import concourse.mybir as mybir
from concourse.bass_utils import run_bass_kernel_spmd
from contextlib import ExitStack

F32 = mybir.dt.float32
BF16 = mybir.dt.bfloat16
U32 = mybir.dt.uint32
AF = mybir.ActivationFunctionType
ALU = mybir.AluOpType
AX = mybir.AxisListType

D = 1024
IN_COLS = 2464
OFF_KV = 384
OFF_KR = 640
OFF_RWKV = 672
RWKV_IN = 1792
NE = 16384
EPS = 1e-6
GN_EPS = 64e-5
KDMA = 6


class Buf:
    __slots__ = ("name", "writers", "readers", "war", "full", "excl")

    def __init__(self, name="", excl=False):
        self.name = name
        self.excl = excl
        self.writers = []
        self.readers = []
        self.war = []
        self.full = None


class Op:
    __slots__ = ("eng", "fns", "deps", "signal", "dma", "sem", "val")


class Prog:
    ENGS = ("pe", "dve", "act", "pool", "sp")

    def __init__(self, nc, es):
        self.nc = nc
        self.es = es
        self.streams = {e: [] for e in self.ENGS}
        self.allops = []
        self.dma_since = []
        self.nsem = 0

    def op(self, eng, fns, r=(), w=(), pw=(), dma=False):
        import os
        lim = int(os.environ.get("K_MAXOPS", "0"))
        self.nrec = getattr(self, "nrec", 0) + 1
        if lim and self.nrec > lim:
            o = Op()
            o.deps = set()
            o.signal = False
            o.dma = dma
            o.sem = None
            o.val = 0
            o.fns = []
            o.eng = eng
            return o
        if os.environ.get("K_TRACE"):
            import inspect
            fr = inspect.stack()[1]
            print("OP", self.nrec, eng, fr.lineno)
        o = Op()
        o.eng = eng
        o.fns = list(fns) if isinstance(fns, (list, tuple)) else [fns]
        o.dma = dma
        o.signal = dma
        o.sem = None
        o.val = 0
        deps = set()
        for b in r:
            deps.update(b.writers)
            if b.excl:
                deps.update(b.readers)
        for b in w:
            b.war = b.readers + b.writers
            deps.update(b.war)
        for b in pw:
            if b.readers:
                b.war = b.readers + b.writers
                b.writers = []
                b.readers = []
                b.full = None
            deps.update(b.war)
            if b.full is not None:
                deps.add(b.full)
        for b in w:
            b.writers = [o]
            b.readers = []
            b.full = o
        for b in pw:
            b.writers.append(o)
        for b in r:
            if (b not in w) and (b not in pw):
                b.readers.append(o)
        deps.discard(o)
        o.deps = deps
        self.streams[eng].append(o)
        self.allops.append(o)
        if dma:
            self.dma_since.append(o)
        return o

    def barrier(self):
        deps = set(self.dma_since)
        self.dma_since = []
        for e in self.ENGS:
            for o in reversed(self.streams[e]):
                if o.fns and not o.dma:
                    deps.add(o)
                    break
        for e in self.ENGS:
            o = Op()
            o.eng = e
            o.fns = []
            o.dma = False
            o.signal = False
            o.sem = None
            o.val = 0
            o.deps = set(deps)
            self.streams[e].append(o)
            self.allops.append(o)

    def newsem(self):
        self.nsem += 1
        return self.es.enter_context(self.nc.semaphore(f"s{self.nsem}"))

    def finalize(self):
        nc = self.nc
        for o in self.allops:
            for d in o.deps:
                d.signal = True
        slots = {e: [dict(sem=None, cnt=0, last=None) for _ in range(KDMA)] for e in self.ENGS}
        rr = {e: 0 for e in self.ENGS}
        for o in self.allops:
            if o.dma:
                sl = slots[o.eng][rr[o.eng] % KDMA]
                rr[o.eng] += 1
                if sl["sem"] is None or sl["cnt"] + 16 > 65000:
                    sl["sem"] = self.newsem()
                    sl["cnt"] = 0
                if sl["last"] is not None:
                    o.deps.add(sl["last"])
                sl["cnt"] += 16
                o.sem = sl["sem"]
                o.val = sl["cnt"]
                sl["last"] = o
        for e in self.ENGS:
            sem = None
            cnt = 0
            for o in self.streams[e]:
                if o.dma or not o.signal:
                    continue
                if sem is None or cnt >= 60000:
                    sem = self.newsem()
                    cnt = 0
                cnt += 1
                o.sem = sem
                o.val = cnt
        streams = self.streams

        def run(engname, eobj):
            seen = {}
            for o in streams[engname]:
                need = {}
                for d in o.deps:
                    k = id(d.sem)
                    if d.val > seen.get(k, 0) and d.val > need.get(k, (0, None))[0]:
                        need[k] = (d.val, d.sem)
                for k, (v, s) in need.items():
                    eobj.wait_ge(s, v)
                    seen[k] = v
                ins = None
                for f in o.fns:
                    ins = f(eobj)
                if o.signal:
                    ins.then_inc(o.sem, 16 if o.dma else 1)

        with nc.Block() as block:
            block.tensor(lambda e: run("pe", e))
            block.vector(lambda e: run("dve", e))
            block.scalar(lambda e: run("act", e))
            block.gpsimd(lambda e: run("pool", e))
            block.sync(lambda e: run("sp", e))


class Arena:
    def __init__(self, nc, es, nwords):
        self.t = es.enter_context(nc.sbuf_tensor("arena", [128, nwords], F32))
        self.n = nwords
        self.base = 0
        self.p = 0

    def mark(self):
        self.base = self.p

    def reset(self):
        self.p = self.base

    def alloc(self, shape, dt, parts=128):
        n = 1
        for s_ in shape:
            n *= s_
        words = (n * (2 if dt == BF16 else 4) + 3) // 4
        words = (words + 7) // 8 * 8
        assert self.p + words <= self.n, f"arena overflow {self.p + words} > {self.n}"
        ap = self.t[:, self.p:self.p + words]
        self.p += words
        if dt != F32:
            ap = ap.bitcast(dt)
        ap = ap[:, 0:n]
        if len(shape) == 2:
            ap = ap.rearrange("p (a b) -> p a b", b=shape[1])
        elif len(shape) == 3:
            ap = ap.rearrange("p (a b c) -> p a b c", b=shape[1], c=shape[2])
        if parts != 128:
            ap = ap[0:parts]
        return ap


def bc(ap, shape):
    return ap.to_broadcast(list(shape))


def build(SEQ=8192, NS=2, dbg=False, phases=(1, 2, 3, 4)):
    NT = SEQ // 128
    nc = bass.Bass("TRN2", target_bir_lowering=False)
    es = ExitStack()

    def din(name, shape, dt=F32):
        return nc.dram_tensor(name, list(shape), dt, kind="ExternalInput").ap()

    def dscr(name, shape, dt=F32):
        kind = "ExternalOutput" if dbg else "Internal"
        return nc.dram_tensor(name, list(shape), dt, kind=kind).ap()

    x = din("x", [NS, SEQ, D])
    norm1_g = din("norm1_g", [D])
    w_in = din("w_in", [D, IN_COLS])
    q_lat_g = din("q_lat_g", [384])
    w_uq = din("w_uq", [384, 768])
    kv_lat_g = din("kv_lat_g", [256])
    w_ukv = din("w_ukv", [256, 1024])
    q_norm_g = din("q_norm_g", [96])
    k_norm_g = din("k_norm_g", [96])
    attn_out_g = din("attn_out_g", [512])
    mu_prev = din("mu_prev", [RWKV_IN])
    mu_next = din("mu_next", [RWKV_IN])
    w0 = din("w0", [2, 512])
    w_up = din("w_up", [2, 64, 512])
    a0 = din("a0", [2, 512])
    a_up = din("a_up", [2, 64, 512])
    g_up = din("g_up", [128, 512])
    k_k = din("k_k", [512])
    k_a = din("k_a", [512])
    r_k = din("r_k", [512])
    ln_x_g = din("ln_x_g", [512])
    ln_x_b = din("ln_x_b", [512])
    w_out = din("w_out", [D, D])
    norm2_g = din("norm2_g", [D])
    w_pq = din("w_pq", [D, 2048])
    sub_keys = din("sub_keys", [2, 128, 128])
    expert_u = din("expert_u", [NE, D])
    expert_v = din("expert_v", [NE, D])
    ident_d = din("ident", [128, 128])
    rope_d = din("rope", [SEQ, 32])
    cmask_d = din("cmask", [128, 1792])

    y = nc.dram_tensor("y", [NS, SEQ, D], F32, kind="ExternalOutput").ap()
    qT_d = dscr("qT_s", [NS, 8, 96, SEQ], BF16)
    kT_d = dscr("kT_s", [NS, 8, 96, SEQ], BF16)
    v_d = dscr("v_s", [NS, SEQ, 520], BF16)
    z_d = dscr("z_s", [NS, SEQ + 2, RWKV_IN])
    attn_d = dscr("attn_s", [NS, SEQ, 512])
    rw_d = dscr("rw_s", [NS, SEQ, 512])

    P = Prog(nc, es)
    ar = Arena(nc, es, 45000)
    psum = es.enter_context(nc.psum_tensor("psum", [128, 4096], F32))
    pbank = [Buf(f"pb{i}", excl=True) for i in range(8)]

    def PS(b, n=512, parts=128, off=0):
        a = psum[:, b * 512 + off:b * 512 + off + n]
        return a if parts == 128 else a[0:parts]

    def PSB(b, n=1024, parts=128, off=0):
        a = psum[:, b * 512:(b + 1) * 512].bitcast(BF16)[:, off:off + n]
        return a if parts == 128 else a[0:parts]

    ident_f = ar.alloc([128], F32)
    ident_b = ar.alloc([128], BF16)
    B_ident = Buf("ident")
    P.op("sp", lambda e: e.dma_start(out=ident_f, in_=ident_d), w=[B_ident], dma=True)
    P.op("dve", lambda e: e.tensor_copy(out=ident_b, in_=ident_f), r=[B_ident], pw=[B_ident])
    mhalf = ar.alloc([16], F32)
    B_mh = Buf("mhalf")
    P.op("pool", lambda e: e.memset(mhalf, -0.5), w=[B_mh])
    ar.mark()

    def rstd(out, in_, mul, n, r, pw, eps=EPS):
        P.op("pool", lambda e: e.tensor_scalar(out=out, in0=in_, scalar1=float(mul), scalar2=float(eps), op0=ALU.mult,
                                               op1=ALU.add), r=r, pw=pw)
        P.op("pool", lambda e: e.tensor_tensor(out=out, in0=out, in1=mhalf[:, 0:n], op=ALU.pow), r=r + [B_mh], pw=pw)

    def _phase(n):
        def deco(f):
            if n in phases:
                f()
            return f
        return deco

    zt = [[Buf(f"z{s}_{t}") for t in range(NT + 2)] for s in range(NS)]
    qkv_t = [[Buf(f"qkv{s}_{t}") for t in range(NT)] for s in range(NS)]
    attn_t = [[Buf(f"at{s}_{t}") for t in range(NT)] for s in range(NS)]
    rw_t = [[Buf(f"rw{s}_{t}") for t in range(NT)] for s in range(NS)]
    out_ops = []

    def load_colT(dst, src_vec, nchunk, buf):
        P.op("sp", lambda e: e.dma_start(out=dst, in_=src_vec.rearrange("(c p) -> p c", p=128),
                                         allow_slow_non_contiguous=True), w=[buf], dma=True)

    @_phase(1)
    def _p1():
        ar.reset()
        w_in_b = ar.alloc([8, IN_COLS], BF16)
        w_uq_b = ar.alloc([3, 768], BF16)
        w_ukv_b = ar.alloc([2, 1024], BF16)
        gT = ar.alloc([16], F32)
        gq_b = ar.alloc([96], F32)
        gk_b = ar.alloc([96], F32)
        stage = [ar.alloc([IN_COLS], F32) for _ in range(2)]
        B_w = Buf("w1")
        B_gT = Buf("gT")
        B_gqk = Buf("gqk")
        B_stage = [Buf("st0"), Buf("st1")]
        load_colT(gT[:, 0:8], norm1_g, 8, B_gT)
        P.op("sp", lambda e: e.dma_start(out=gT[:, 8:11], in_=q_lat_g.rearrange("(c p) -> p c", p=128),
                                         allow_slow_non_contiguous=True), pw=[B_gT], dma=True)
        P.op("sp", lambda e: e.dma_start(out=gT[:, 11:13], in_=kv_lat_g.rearrange("(c p) -> p c", p=128),
                                         allow_slow_non_contiguous=True), pw=[B_gT], dma=True)
        P.op("sp", lambda e: e.dma_start(out=gq_b, in_=bc(q_norm_g.unsqueeze(0), [128, 96])), w=[B_gqk], dma=True)
        P.op("sp", lambda e: e.dma_start(out=gk_b, in_=bc(k_norm_g.unsqueeze(0), [128, 96])), pw=[B_gqk], dma=True)
        P.op("dve", lambda e: e.tensor_scalar(out=gq_b, in0=gq_b, scalar1=float(96 ** -0.5), scalar2=None,
                                              op0=ALU.mult), r=[B_gqk], pw=[B_gqk])
        k = 0
        jobs = [(w_in[c * 128:(c + 1) * 128, :], w_in_b[:, c, :], IN_COLS, c) for c in range(8)]
        jobs += [(w_uq[c * 128:(c + 1) * 128, :], w_uq_b[:, c, :], 768, 8 + c) for c in range(3)]
        jobs += [(w_ukv[c * 128:(c + 1) * 128, :], w_ukv_b[:, c, :], 1024, 11 + c) for c in range(2)]
        for (src, dst, n, gc) in jobs:
            st = stage[k % 2]
            bs = B_stage[k % 2]
            P.op("sp", lambda e, st=st, src=src, n=n: e.dma_start(out=st[:, 0:n], in_=src), w=[bs], dma=True)
            P.op("dve", lambda e, st=st, dst=dst, n=n, gc=gc: e.tensor_scalar(
                out=dst, in0=st[:, 0:n], scalar1=gT[:, gc:gc + 1], scalar2=None, op0=ALU.mult),
                r=[bs, B_gT], pw=[B_w])
            k += 1

        xs = [ar.alloc([D], F32) for _ in range(2)]
        B_xs = [Buf("xs0"), Buf("xs1")]
        junk = ar.alloc([D], F32)
        B_junk = Buf("junk")
        xb = ar.alloc([D], BF16)
        B_xb = Buf("xb")
        xT = ar.alloc([8, 128], BF16)
        B_xT = Buf("xT")
        st4 = ar.alloc([8], F32)
        B_st = Buf("st4")
        proj = ar.alloc([IN_COLS], F32)
        B_proj = Buf("proj")
        latb = ar.alloc([640], BF16)
        B_latb = Buf("latb")
        latT = ar.alloc([5, 128], BF16)
        B_latT = Buf("latT")
        q_sb = ar.alloc([8, 96], F32)
        k_sb = ar.alloc([8, 96], F32)
        B_q = Buf("q")
        B_k = Buf("k")
        sq = ar.alloc([8, 96], F32)
        B_sq = Buf("sq")
        sq2 = ar.alloc([8, 96], F32)
        B_sq2 = Buf("sq2")
        hst = ar.alloc([32], F32)
        B_hq = Buf("hq")
        B_hk = Buf("hk")
        vb = ar.alloc([8, 65], BF16)
        B_vb = Buf("vb")
        qb = ar.alloc([8, 96], BF16)
        kb = ar.alloc([8, 96], BF16)
        B_qb = Buf("qb")
        B_kb = Buf("kb")
        rt = ar.alloc([8, 6, 16], F32)
        B_rtq = Buf("rtq")
        B_rtk = Buf("rtk")
        cs = [ar.alloc([32], F32) for _ in range(2)]
        B_cs = [Buf("cs0"), Buf("cs1")]
        qT_sb = ar.alloc([8, 128], BF16)
        kT_sb = ar.alloc([8, 128], BF16)
        B_qT = Buf("qTsb")
        B_kT = Buf("kTsb")
        zero_t = ar.alloc([RWKV_IN], F32)
        B_zero = Buf("zero")

        P.op("dve", lambda e: e.memset(vb, 1.0), w=[B_vb])
        P.op("dve", lambda e: e.memset(zero_t, 0.0), w=[B_zero])
        for s in range(NS):
            P.op("sp", lambda e, s=s: e.dma_start(out=z_d[s, 0:1, :], in_=zero_t[0:1, :]), r=[B_zero],
                 w=[zt[s][0]], dma=True)
            P.op("sp", lambda e, s=s: e.dma_start(out=z_d[s, SEQ + 1:SEQ + 2, :], in_=zero_t[0:1, :]), r=[B_zero],
                 w=[zt[s][NT + 1]], dma=True)

        it = 0
        for s in range(NS):
            for t in range(NT):
                par = it % 2
                it += 1
                xsp, bxs = xs[par], B_xs[par]
                csp, bcs = cs[par], B_cs[par]
                r0 = t * 128
                P.op("sp", lambda e, xsp=xsp, s=s, r0=r0: e.dma_start(out=xsp, in_=x[s, r0:r0 + 128, :]),
                     w=[bxs], dma=True)
                P.op("sp", lambda e, csp=csp, r0=r0: e.dma_start(out=csp, in_=rope_d[r0:r0 + 128, :]),
                     w=[bcs], dma=True)
                P.op("act", lambda e, xsp=xsp: e.activation(out=junk, in_=xsp, func=AF.Square, scale=1.0 / 32.0,
                                                            accum_out=st4[:, 0:1]),
                     r=[bxs], w=[B_junk], pw=[B_st])
                P.op("dve", lambda e, xsp=xsp: e.tensor_copy(out=xb, in_=xsp), r=[bxs], w=[B_xb])
                rstd(st4[:, 1:2], st4[:, 0:1], 1.0, 1, [B_st], [B_st])
                P.op("pe", [lambda e, c=c: e.transpose(out=PSB(0, 128, off=c * 128), in_=xb[:, c * 128:(c + 1) * 128],
                                                       identity=ident_b) for c in range(8)],
                     r=[B_xb, B_ident], w=[pbank[0]])
                P.op("act", lambda e: e.copy(out=xT.rearrange("p a b -> p (a b)"), in_=PSB(0)), r=[pbank[0]], w=[B_xT])
                fns = []
                for j in range(5):
                    a, b_ = j * 512, min((j + 1) * 512, IN_COLS)
                    for c in range(8):
                        fns.append(lambda e, j=j, a=a, b_=b_, c=c: e.matmul(
                            out=PS(1 + j, b_ - a), lhsT=xT[:, c, :], rhs=w_in_b[:, c, a:b_],
                            start=(c == 0), stop=(c == 7)))
                P.op("pe", fns, r=[B_xT, B_w], w=[pbank[1], pbank[2], pbank[3], pbank[4], pbank[5]])
                for j in range(5):
                    a, b_ = j * 512, min((j + 1) * 512, IN_COLS)
                    P.op("act", lambda e, j=j, a=a, b_=b_: e.activation(
                        out=proj[:, a:b_], in_=PS(1 + j, b_ - a), func=AF.Copy, scale=st4[:, 1:2]),
                        r=[pbank[1 + j], B_st], pw=[B_proj])
                P.op("sp", lambda e, s=s, r0=r0: e.dma_start(out=z_d[s, 1 + r0:1 + r0 + 128, :],
                                                             in_=proj[:, OFF_RWKV:IN_COLS]),
                     r=[B_proj], w=[zt[s][t + 1]], dma=True)
                P.op("act", lambda e: e.activation(out=junk[:, 0:384], in_=proj[:, 0:384], func=AF.Square,
                                                   scale=float(384 ** -0.5), accum_out=st4[:, 2:3]),
                     r=[B_proj], w=[B_junk], pw=[B_st])
                P.op("act", lambda e: e.activation(out=junk[:, 0:256], in_=proj[:, 384:640], func=AF.Square,
                                                   scale=float(256 ** -0.5), accum_out=st4[:, 3:4]),
                     r=[B_proj], w=[B_junk], pw=[B_st])
                rstd(st4[:, 4:6], st4[:, 2:4], 1.0, 2, [B_st], [B_st])
                P.op("dve", lambda e: e.tensor_scalar(out=latb[:, 0:384], in0=proj[:, 0:384], scalar1=st4[:, 4:5],
                                                      scalar2=None, op0=ALU.mult), r=[B_proj, B_st], w=[B_latb])
                P.op("dve", lambda e: e.tensor_scalar(out=latb[:, 384:640], in0=proj[:, 384:640], scalar1=st4[:, 5:6],
                                                      scalar2=None, op0=ALU.mult), r=[B_proj, B_st], pw=[B_latb])
                P.op("pe", [lambda e, c=c: e.transpose(out=PSB(0, 128, off=c * 128), in_=latb[:, c * 128:(c + 1) * 128],
                                                       identity=ident_b) for c in range(5)],
                     r=[B_latb, B_ident], w=[pbank[0]])
                P.op("act", lambda e: e.copy(out=latT.rearrange("p a b -> p (a b)"), in_=PSB(0, 640)),
                     r=[pbank[0]], w=[B_latT])
                fns = []
                for (bk, a, b_) in ((6, 0, 512), (7, 512, 768)):
                    for c in range(3):
                        fns.append(lambda e, bk=bk, a=a, b_=b_, c=c: e.matmul(
                            out=PS(bk, b_ - a), lhsT=latT[:, c, :], rhs=w_uq_b[:, c, a:b_],
                            start=(c == 0), stop=(c == 2)))
                P.op("pe", fns, r=[B_latT, B_w], w=[pbank[6], pbank[7]])
                fns = []
                for (bk, a, b_) in ((1, 0, 512), (2, 512, 1024)):
                    for c in range(2):
                        fns.append(lambda e, bk=bk, a=a, b_=b_, c=c: e.matmul(
                            out=PS(bk, 512), lhsT=latT[:, 3 + c, :], rhs=w_ukv_b[:, c, a:b_],
                            start=(c == 0), stop=(c == 1)))
                P.op("pe", fns, r=[B_latT, B_w], w=[pbank[1], pbank[2]])
                qf = q_sb.rearrange("p a b -> p (a b)")
                P.op("act", lambda e: e.copy(out=qf[:, 0:512], in_=PS(6)), r=[pbank[6]], w=[B_q])
                P.op("act", lambda e: e.copy(out=qf[:, 512:768], in_=PS(7, 256)), r=[pbank[7]], pw=[B_q])
                for hh in range(2):
                    kvv = PS(1 + hh).rearrange("p (h d) -> p h d", d=128)
                    import os as _os
                    P.op(_os.environ.get("K_E67", "act"), lambda e, hh=hh, kvv=kvv: (e.tensor_copy if _os.environ.get("K_E67", "act") == "dve" else e.copy)(out=k_sb[:, hh * 4:(hh + 1) * 4, 0:64],
                                                                        in_=kvv[:, :, 0:64]),
                         r=[pbank[1 + hh]], pw=[B_k] if hh else [], w=[] if hh else [B_k])
                    P.op("act", lambda e, hh=hh, kvv=kvv: e.copy(out=vb[:, hh * 4:(hh + 1) * 4, 0:64],
                                                                 in_=kvv[:, :, 64:128]),
                         r=[pbank[1 + hh]], pw=[B_vb])
                P.op("dve", lambda e: e.tensor_copy(out=k_sb[:, :, 64:96],
                                                    in_=bc(proj[:, OFF_KR:OFF_RWKV].unsqueeze(1), [128, 8, 32])),
                     r=[B_proj], pw=[B_k])
                P.op("sp", lambda e, s=s, r0=r0: e.dma_start(out=v_d[s, r0:r0 + 128, :],
                                                             in_=vb.rearrange("p a b -> p (a b)")),
                     r=[B_vb], pw=[qkv_t[s][t]], dma=True)
                for (tsb, Bt, sqt, Bsq, ho, Bh, gb, ob, Bo, ro, Brt, eng) in (
                        (q_sb, B_q, sq, B_sq, 0, B_hq, gq_b, qb, B_qb, 0, B_rtq, "dve"),
                        (k_sb, B_k, sq2, B_sq2, 16, B_hk, gk_b, kb, B_kb, 3, B_rtk, "pool")):
                    TT = lambda e, **kw: e.tensor_tensor(**kw)
                    P.op(eng, lambda e, tsb=tsb, sqt=sqt: e.tensor_tensor(out=sqt, in0=tsb, in1=tsb, op=ALU.mult),
                         r=[Bt], w=[Bsq])
                    P.op("dve", lambda e, sqt=sqt, ho=ho: e.tensor_reduce(out=hst[:, ho:ho + 8], in_=sqt, axis=AX.X,
                                                                         op=ALU.add), r=[Bsq], w=[Bh])
                    rstd(hst[:, ho + 8:ho + 16], hst[:, ho:ho + 8], 1.0 / 96.0, 8, [Bh], [Bh])
                    P.op(eng, lambda e, tsb=tsb, ho=ho: e.tensor_tensor(
                        out=tsb, in0=tsb, in1=bc(hst[:, ho + 8:ho + 16].unsqueeze(2), [128, 8, 96]), op=ALU.mult),
                        r=[Bt, Bh], w=[Bt])
                    P.op(eng, lambda e, tsb=tsb, gb=gb: e.tensor_tensor(
                        out=tsb, in0=tsb, in1=bc(gb.unsqueeze(1), [128, 8, 96]), op=ALU.mult),
                        r=[Bt, B_gqk], w=[Bt])
                    cosb = bc(csp[:, 0:16].unsqueeze(1), [128, 8, 16])
                    sinb = bc(csp[:, 16:32].unsqueeze(1), [128, 8, 16])
                    x1 = tsb[:, :, 64:80]
                    x2 = tsb[:, :, 80:96]
                    P.op(eng, lambda e, x1=x1, cosb=cosb, ro=ro: e.tensor_tensor(out=rt[:, :, ro, :], in0=x1, in1=cosb,
                                                                               op=ALU.mult), r=[Bt, bcs], w=[Brt])
                    P.op(eng, lambda e, x2=x2, sinb=sinb, ro=ro: e.tensor_tensor(out=rt[:, :, ro + 1, :], in0=x2,
                                                                                in1=sinb, op=ALU.mult),
                         r=[Bt, bcs], pw=[Brt])
                    P.op(eng, lambda e, ob=ob, ro=ro: e.tensor_tensor(out=ob[:, :, 64:80], in0=rt[:, :, ro, :],
                                                                     in1=rt[:, :, ro + 1, :], op=ALU.subtract),
                         r=[Brt], w=[Bo])
                    P.op(eng, lambda e, x1=x1, sinb=sinb, ro=ro: e.tensor_tensor(out=rt[:, :, ro, :], in0=x1, in1=sinb,
                                                                                op=ALU.mult), r=[Bt, bcs], w=[Brt])
                    P.op(eng, lambda e, x2=x2, cosb=cosb, ro=ro: e.tensor_tensor(out=rt[:, :, ro + 1, :], in0=x2,
                                                                                in1=cosb, op=ALU.mult),
                         r=[Bt, bcs], pw=[Brt])
                    P.op(eng, lambda e, ob=ob, ro=ro: e.tensor_tensor(out=ob[:, :, 80:96], in0=rt[:, :, ro, :],
                                                                     in1=rt[:, :, ro + 1, :], op=ALU.add),
                         r=[Brt], pw=[Bo])
                    P.op(eng, lambda e, ob=ob, tsb=tsb: e.tensor_copy(out=ob[:, :, 0:64], in_=tsb[:, :, 0:64]),
                         r=[Bt], pw=[Bo])
                for (ob, Bo, pb, dst_sb, Bd, dst_d) in ((qb, B_qb, 6, qT_sb, B_qT, qT_d), (kb, B_kb, 7, kT_sb, B_kT, kT_d)):
                    P.op("pe", [lambda e, h=h, ob=ob, pb=pb: e.transpose(out=PSB(pb, 128, parts=96, off=h * 128),
                                                                       in_=ob[:, h, :], identity=ident_b)
                                for h in range(8)], r=[Bo, B_ident], w=[pbank[pb]])
                    P.op("act", lambda e, pb=pb, dst_sb=dst_sb: e.copy(out=dst_sb[0:96].rearrange("p a b -> p (a b)"),
                                                                      in_=PSB(pb, 1024, parts=96)),
                         r=[pbank[pb]], w=[Bd])
                    P.op("sp", lambda e, dst_sb=dst_sb, dst_d=dst_d, s=s, r0=r0: e.dma_start(
                        out=dst_d[s, :, :, r0:r0 + 128].rearrange("h d t -> d h t"), in_=dst_sb[0:96]),
                        r=[Bd], pw=[qkv_t[s][t]], dma=True)
        P.barrier()

    @_phase(2)
    def _p2():
        ar.reset()
        QG = min(512, SEQ)
        NQG = SEQ // QG
        kT_h = [ar.alloc([SEQ], BF16, parts=96) for _ in range(2)]
        qT_h = [ar.alloc([SEQ], BF16, parts=96) for _ in range(2)]
        B_kq = [Buf("kq0"), Buf("kq1")]
        v_all = ar.alloc([NT, 520], BF16)
        B_vall = Buf("vall")
        pT = [ar.alloc([QG], BF16) for _ in range(3)]
        B_pT = [Buf(f"pT{i}") for i in range(3)]
        oT = ar.alloc([QG], F32, parts=65)
        B_oT = Buf("oT")
        rc = ar.alloc([4], F32)
        B_rc = Buf("rc")
        ao = [ar.alloc([4, 64], F32) for _ in range(2)]
        B_ao = [Buf("ao0"), Buf("ao1")]
        hi = 0
        gi = 0
        si = 0
        for s in range(NS):
            P.op("sp", lambda e, s=s: e.dma_start(out=v_all, in_=v_d[s].rearrange("(c p) f -> p c f", p=128)),
                 r=qkv_t[s], w=[B_vall], dma=True)
            for h in range(8):
                par = hi % 2
                hi += 1
                P.op("sp", lambda e, s=s, h=h, par=par: e.dma_start(out=kT_h[par], in_=kT_d[s, h]),
                     r=qkv_t[s], w=[B_kq[par]], dma=True)
                P.op("sp", lambda e, s=s, h=h, par=par: e.dma_start(out=qT_h[par], in_=qT_d[s, h]),
                     r=qkv_t[s], pw=[B_kq[par]], dma=True)
                for qg in range(NQG):
                    ob = 3 + (gi % 2)
                    gi += 1
                    q_ap = qT_h[par][:, qg * QG:(qg + 1) * QG]
                    steps = []
                    for kc in range(NT):
                        sb = si % 3
                        si += 1
                        steps.append((kc, sb))

                    def emit_S(kc, sb, par=par, q_ap=q_ap):
                        P.op("pe", lambda e, kc=kc, sb=sb: e.matmul(out=PS(sb, QG), lhsT=kT_h[par][:, kc * 128:(kc + 1) * 128],
                                                                     rhs=q_ap, start=True, stop=True),
                             r=[B_kq[par]], w=[pbank[sb]])

                    def emit_E(kc, sb):
                        P.op("act", lambda e, sb=sb: e.activation(out=pT[sb], in_=PS(sb, QG), func=AF.Exp),
                             r=[pbank[sb]], w=[B_pT[sb]])

                    def emit_PV(kc, sb, ob=ob, h=h):
                        P.op("pe", lambda e, kc=kc, sb=sb: e.matmul(out=PS(ob, QG, parts=65),
                                                                     lhsT=v_all[:, kc, h * 65:(h + 1) * 65], rhs=pT[sb],
                                                                     start=(kc == 0), stop=(kc == NT - 1)),
                             r=[B_pT[sb], B_vall], w=[pbank[ob]] if kc == 0 else [], pw=[] if kc == 0 else [pbank[ob]])

                    emit_S(*steps[0])
                    for i_, (kc, sb) in enumerate(steps):
                        if i_ + 1 < len(steps):
                            emit_S(*steps[i_ + 1])
                        emit_E(kc, sb)
                        emit_PV(kc, sb)
                    P.op("act", lambda e, ob=ob: e.copy(out=oT, in_=PS(ob, QG, parts=65)), r=[pbank[ob]], w=[B_oT])
                    nj = QG // 128
                    P.op("pe", [lambda e, j=j: e.transpose(out=PS(5, 65, off=j * 128), in_=oT[:, j * 128:(j + 1) * 128],
                                                           identity=ident_f[0:65, 0:65]) for j in range(nj)],
                         r=[B_oT, B_ident], w=[pbank[5]])
                    o5 = PS(5).rearrange("p (j d) -> p j d", d=128)
                    P.op("dve", lambda e, o5=o5, nj=nj: e.reciprocal(out=rc[:, 0:nj], in_=o5[:, 0:nj, 64]),
                         r=[pbank[5]], w=[B_rc])
                    ap_ = gi % 2
                    P.op("dve", lambda e, o5=o5, nj=nj, ap_=ap_: e.tensor_tensor(
                        out=ao[ap_][:, 0:nj, :], in0=o5[:, 0:nj, 0:64],
                        in1=bc(rc[:, 0:nj].unsqueeze(2), [128, nj, 64]), op=ALU.mult),
                        r=[pbank[5], B_rc], w=[B_ao[ap_]])
                    t0 = qg * nj
                    P.op("sp", lambda e, s=s, h=h, qg=qg, nj=nj, ap_=ap_: e.dma_start(
                        out=attn_d[s, qg * QG:(qg + 1) * QG, h * 64:(h + 1) * 64].rearrange("(j p) d -> p j d", p=128),
                        in_=ao[ap_][:, 0:nj, :]),
                        r=[B_ao[ap_]], pw=[attn_t[s][t0 + j] for j in range(nj)], dma=True)
        P.barrier()

    @_phase(3)
    def _p3():
        ar.reset()
        CE = float(np.exp(-0.5))
        mup = ar.alloc([RWKV_IN], F32)
        mun = ar.alloc([RWKV_IN], F32)
        kkb = ar.alloc([512], F32)
        kab = ar.alloc([512], F32)
        rkb = ar.alloc([512], F32)
        lgb = ar.alloc([512], F32)
        lbb = ar.alloc([512], F32)
        w0b = ar.alloc([2, 512], F32)
        a0b = ar.alloc([2, 512], F32)
        wup = ar.alloc([2, 512], BF16)
        aup = ar.alloc([2, 512], BF16)
        gup = ar.alloc([512], BF16)
        cm = ar.alloc([2, 896], F32)
        negc = ar.alloc([2], F32)
        B_c = Buf("p3c")
        first = [True]

        def cload(dst, src):
            P.op("sp", lambda e: e.dma_start(out=dst, in_=src), w=[B_c] if first[0] else [], pw=[] if first[0] else [B_c],
                 dma=True)
            first[0] = False

        cload(mup, bc(mu_prev.unsqueeze(0), [128, RWKV_IN]))
        cload(mun, bc(mu_next.unsqueeze(0), [128, RWKV_IN]))
        for (dst, src) in ((kkb, k_k), (kab, k_a), (rkb, r_k), (lgb, ln_x_g), (lbb, ln_x_b)):
            cload(dst, bc(src.unsqueeze(0), [128, 512]))
        for d in range(2):
            cload(w0b[:, d, :], bc(w0[d:d + 1, :], [128, 512]))
            cload(a0b[:, d, :], bc(a0[d:d + 1, :], [128, 512]))
        cload(cm.rearrange("p a b -> p (a b)"), cmask_d)
        stg = ar.alloc([1024], F32)
        B_stg = Buf("stg")
        for (dst, src) in ((wup, w_up), (aup, a_up)):
            P.op("sp", lambda e, src=src: e.dma_start(out=stg[0:64].rearrange("p (a b) -> p a b", b=512),
                                                      in_=src.rearrange("d j c -> j d c")), w=[B_stg], dma=True)
            P.op("dve", lambda e, dst=dst: e.tensor_copy(out=dst[0:64].rearrange("p a b -> p (a b)"), in_=stg[0:64]),
                 r=[B_stg], pw=[B_c])
        P.op("sp", lambda e: e.dma_start(out=stg[:, 0:512], in_=g_up), w=[B_stg], dma=True)
        P.op("dve", lambda e: e.tensor_copy(out=gup, in_=stg[:, 0:512]), r=[B_stg], pw=[B_c])
        P.op("dve", lambda e: e.memset(negc, -CE), pw=[B_c])

        yd_d = [dscr(f"yd{d}_s", [NS, SEQ, 520]) for d in range(2)]
        gv_d = dscr("gv_s", [NS, SEQ, 1024])
        yd_t = [[[Buf(f"yd{d}_{s}_{t}") for t in range(NT)] for s in range(NS)] for d in range(2)]
        gv_t = [[Buf(f"gv{s}_{t}") for t in range(NT)] for s in range(NS)]

        _pb3 = [Buf(f"p3b{b}", excl=True) for b in range(8)]
        pq = [[_pb3[b]] * 4 for b in range(8)]

        def PQ(b, q, n=128, parts=128):
            a = psum[:, b * 512 + q * 128:b * 512 + q * 128 + n]
            return a if parts == 128 else a[0:parts]

        zc = ar.alloc([RWKV_IN], F32)
        zp = ar.alloc([RWKV_IN], F32)
        zn = ar.alloc([RWKV_IN], F32)
        B_zc, B_zp, B_zn = Buf("zc"), Buf("zp"), Buf("zn")
        lat = ar.alloc([256], BF16)
        latT = ar.alloc([384], BF16)
        B_lat, B_latT = Buf("lat"), Buf("latT")
        names = ["tmp", "sg", "a", "E1", "E2", "E3", "Ex", "kk", "sq", "kd", "b", "t2", "g"]
        W = {n_: ar.alloc([512], F32) for n_ in names}
        BW = {n_: Buf("w_" + n_) for n_ in names}
        bnames = ["rbar", "abar", "bbar", "kbar", "Bt", "Kt", "vq"]
        WB = {n_: ar.alloc([512], BF16) for n_ in bnames}
        BWB = {n_: Buf("wb_" + n_) for n_ in bnames}
        ytile = ar.alloc([520], F32)
        B_y = Buf("ytile")
        s8 = ar.alloc([16], F32)
        B_s8 = Buf("s8")
        gC = [[ar.alloc([8], F32, parts=64) for d in range(2)] for s in range(NS)]
        B_gC = [[Buf(f"gC{s}{d}") for d in range(2)] for s in range(NS)]
        ST = [[[ar.alloc([64], F32, parts=64) for h in range(8)] for d in range(2)] for s in range(NS)]
        STb = [[[ar.alloc([64], BF16, parts=64) for h in range(8)] for d in range(2)] for s in range(NS)]
        B_ST = [[[Buf(f"ST{s}{d}{h}") for h in range(8)] for d in range(2)] for s in range(NS)]
        B_STb = [[[Buf(f"STb{s}{d}{h}") for h in range(8)] for d in range(2)] for s in range(NS)]
        TT = [ar.alloc([512], BF16, parts=64) for h in range(8)]
        MM = [ar.alloc([512], BF16) for h in range(8)]
        Zf = [ar.alloc([128], F32) for h in range(8)]
        Zb = [ar.alloc([128], BF16) for h in range(8)]
        WbT = [ar.alloc([128], BF16, parts=64) for h in range(8)]
        Ub = [ar.alloc([64], BF16) for h in range(8)]
        PA = [[ar.alloc([128], BF16) for k_ in range(2)] for h in range(8)]
        PB = [[ar.alloc([128], BF16) for k_ in range(2)] for h in range(8)]
        B_TT = [Buf(f"TT{h}") for h in range(8)]
        B_MM = [Buf(f"MM{h}") for h in range(8)]
        B_Zf = [Buf(f"Zf{h}") for h in range(8)]
        B_Zb = [Buf(f"Zb{h}") for h in range(8)]
        B_WbT = [Buf(f"WbT{h}") for h in range(8)]
        B_Ub = [Buf(f"Ub{h}") for h in range(8)]
        B_PA = [[Buf(f"PA{h}{k_}") for k_ in range(2)] for h in range(8)]
        B_PB = [[Buf(f"PB{h}{k_}") for k_ in range(2)] for h in range(8)]

        for s in range(NS):
            for d in range(2):
                for h in range(8):
                    P.op("pool", lambda e, s=s, d=d, h=h: e.memset(ST[s][d][h], 0.0), w=[B_ST[s][d][h]])
                    P.op("pool", lambda e, s=s, d=d, h=h: e.memset(STb[s][d][h], 0.0), w=[B_STb[s][d][h]])

        qrot = [0]

        def qslot():
            q = qrot[0] % 4
            qrot[0] += 1
            return q

        def process(s, d, c):
            r0 = c * 128
            P.op("sp", lambda e: e.dma_start(out=zc, in_=z_d[s, 1 + r0:1 + r0 + 128, :]), w=[B_zc], dma=True)
            P.op("sp", lambda e: e.dma_start(out=zp, in_=z_d[s, r0:r0 + 128, :]), w=[B_zp], dma=True)
            P.op("sp", lambda e: e.dma_start(out=zn, in_=z_d[s, 2 + r0:2 + r0 + 128, :]), w=[B_zn], dma=True)
            TTo = lambda eng, o, a_, b_, op, r, w: P.op(eng, lambda e: e.tensor_tensor(out=o, in0=a_, in1=b_, op=op), r=r, w=w)
            TTo("dve", zp, zp, zc, ALU.subtract, [B_zp, B_zc], [B_zp])
            TTo("dve", zp, zp, mup, ALU.mult, [B_zp, B_c], [B_zp])
            TTo("pool", zn, zn, zc, ALU.subtract, [B_zn, B_zc], [B_zn])
            TTo("pool", zn, zn, mun, ALU.mult, [B_zn, B_c], [B_zn])
            TTo("dve", zc, zc, zp, ALU.add, [B_zc, B_zp], [B_zc])
            TTo("dve", zc, zc, zn, ALU.add, [B_zc, B_zn], [B_zc])
            r_ = zc[:, 0:512]
            kr = zc[:, 512:1024]
            vr = zc[:, 1024:1536]
            P.op("act", lambda e: e.activation(out=lat[:, 0:64], in_=zc[:, 1536:1600], func=AF.Tanh), r=[B_zc], w=[B_lat])
            P.op("act", lambda e: e.activation(out=lat[:, 128:256], in_=zc[:, 1664:1792], func=AF.Sigmoid), r=[B_zc],
                 pw=[B_lat])
            P.op("dve", lambda e: e.tensor_copy(out=lat[:, 64:128], in_=zc[:, 1600:1664]), r=[B_zc], pw=[B_lat])
            P.op("pe", [lambda e: e.transpose(out=PSB(0, 128, parts=64, off=0), in_=lat[:, 0:64], identity=ident_b),
                        lambda e: e.transpose(out=PSB(0, 128, parts=64, off=128), in_=lat[:, 64:128], identity=ident_b),
                        lambda e: e.transpose(out=PSB(0, 128, off=256), in_=lat[:, 128:256], identity=ident_b)],
                 r=[B_lat, B_ident], w=pq[0])
            P.op("act", lambda e: e.copy(out=latT[0:64, 0:256], in_=PSB(0, 256, parts=64)), r=pq[0], w=[B_latT])
            P.op("act", lambda e: e.copy(out=latT[:, 256:384], in_=PSB(0, 128, off=256)), r=pq[0], pw=[B_latT])
            P.op("pe", lambda e: e.matmul(out=PS(2), lhsT=latT[0:64, 0:128], rhs=wup[0:64, d, :], start=True, stop=True),
                 r=[B_latT, B_c], w=pq[2])
            P.op("pe", lambda e: e.matmul(out=PS(3), lhsT=latT[0:64, 128:256], rhs=aup[0:64, d, :], start=True, stop=True),
                 r=[B_latT, B_c], w=pq[3])
            P.op("dve", lambda e: e.tensor_tensor(out=W["tmp"], in0=PS(2), in1=w0b[:, d, :], op=ALU.add),
                 r=pq[2] + [B_c], w=[BW["tmp"]])
            P.op("act", lambda e: e.activation(out=W["sg"], in_=W["tmp"], func=AF.Sigmoid), r=[BW["tmp"]], w=[BW["sg"]])
            P.op("dve", lambda e: e.tensor_tensor(out=W["tmp"], in0=PS(3), in1=a0b[:, d, :], op=ALU.add),
                 r=pq[3] + [B_c], w=[BW["tmp"]])
            P.op("act", lambda e: e.activation(out=W["a"], in_=W["tmp"], func=AF.Sigmoid), r=[BW["tmp"]], w=[BW["a"]])
            if d == 0:
                P.op("pe", lambda e: e.matmul(out=PS(1), lhsT=latT[:, 256:384], rhs=gup, start=True, stop=True),
                     r=[B_latT, B_c], w=pq[1])
                P.op("act", lambda e: e.copy(out=W["g"], in_=PS(1)), r=pq[1], w=[BW["g"]])
                P.op("sp", lambda e: e.dma_start(out=gv_d[s, r0:r0 + 128, 0:512], in_=W["g"]), r=[BW["g"]],
                     pw=[gv_t[s][c]], dma=True)
                P.op("sp", lambda e: e.dma_start(out=gv_d[s, r0:r0 + 128, 512:1024], in_=vr), r=[B_zc],
                     pw=[gv_t[s][c]], dma=True)
            P.op("pe", lambda e: e.matmul(out=PS(2), lhsT=cm[:, d, 640:768], rhs=W["sg"], start=True, stop=True),
                 r=[BW["sg"], B_c], w=pq[2])
            P.op("pe", lambda e: e.matmul(out=PS(3), lhsT=cm[:, d, 768:896], rhs=W["sg"], start=True, stop=True),
                 r=[BW["sg"], B_c], w=pq[3])
            gq = qslot()
            P.op("pe", [lambda e, h=h: e.matmul(out=PQ(4, gq, 1, parts=64)[:, 0:1] if False else psum[0:64, 4 * 512 + gq * 128 + h:4 * 512 + gq * 128 + h + 1],
                                                  lhsT=W["sg"][:, h * 64:(h + 1) * 64], rhs=negc[:, 0:1], start=True, stop=True)
                        for h in range(8)], r=[BW["sg"], B_c], w=[pq[4][gq]])
            P.op("act", lambda e: e.activation(out=gC[s][d], in_=PQ(4, gq, 8, parts=64), func=AF.Exp), r=[pq[4][gq]],
                 w=[B_gC[s][d]])
            P.op("act", lambda e: e.activation(out=W["E1"], in_=PS(2), func=AF.Exp), r=pq[2], w=[BW["E1"]])
            P.op("act", lambda e: e.activation(out=W["E2"], in_=PS(2), func=AF.Exp, scale=-1.0), r=pq[2], w=[BW["E2"]])
            P.op("act", lambda e: e.activation(out=W["E3"], in_=PS(3), func=AF.Exp), r=pq[3], w=[BW["E3"]])
            P.op("dve", lambda e: e.scalar_tensor_tensor(out=W["tmp"], in0=W["sg"], scalar=CE, in1=PS(2), op0=ALU.mult,
                                                         op1=ALU.add), r=[BW["sg"]] + pq[2], w=[BW["tmp"]])
            P.op("act", lambda e: e.activation(out=W["Ex"], in_=W["tmp"], func=AF.Exp), r=[BW["tmp"]], w=[BW["Ex"]])
            TTo("dve", W["kk"], kr, kkb, ALU.mult, [B_zc, B_c], [BW["kk"]])
            TTo("pool", W["sq"], W["kk"], W["kk"], ALU.mult, [BW["kk"]], [BW["sq"]])
            P.op("dve", lambda e: e.tensor_reduce(out=s8[:, 0:8], in_=W["sq"].rearrange("p (h k) -> p h k", k=64),
                                                  axis=AX.X, op=ALU.add), r=[BW["sq"]], w=[B_s8])
            rstd(s8[:, 8:16], s8[:, 0:8], 1.0, 8, [B_s8], [B_s8], eps=1e-24)
            P.op("dve", lambda e: e.tensor_tensor(out=W["kk"].rearrange("p (h k) -> p h k", k=64),
                                                  in0=W["kk"].rearrange("p (h k) -> p h k", k=64),
                                                  in1=bc(s8[:, 8:16].unsqueeze(2), [128, 8, 64]), op=ALU.mult),
                 r=[BW["kk"], B_s8], w=[BW["kk"]])
            P.op("dve", lambda e: e.scalar_tensor_tensor(out=W["kd"], in0=W["a"], scalar=-1.0, in1=kab, op0=ALU.add,
                                                         op1=ALU.mult), r=[BW["a"], B_c], w=[BW["kd"]])
            P.op("dve", lambda e: e.scalar_tensor_tensor(out=W["kd"], in0=W["kd"], scalar=1.0, in1=kr, op0=ALU.add,
                                                         op1=ALU.mult), r=[BW["kd"], B_zc], w=[BW["kd"]])
            TTo("pool", W["b"], W["kk"], W["a"], ALU.mult, [BW["kk"], BW["a"]], [BW["b"]])
            TTo("pool", W["t2"], r_, W["kd"], ALU.mult, [B_zc, BW["kd"]], [BW["t2"]])
            TTo("pool", W["t2"], W["t2"], rkb, ALU.mult, [BW["t2"], B_c], [BW["t2"]])
            P.op("dve", lambda e: e.tensor_reduce(out=ytile[:, 512:520], in_=W["t2"].rearrange("p (h k) -> p h k", k=64),
                                                  axis=AX.X, op=ALU.add), r=[BW["t2"]], w=[B_y])
            TTo("dve", WB["rbar"], r_, W["E1"], ALU.mult, [B_zc, BW["E1"]], [BWB["rbar"]])
            P.op("dve", lambda e: e.scalar_tensor_tensor(out=WB["abar"], in0=W["kk"], scalar=-1.0, in1=W["Ex"],
                                                         op0=ALU.mult, op1=ALU.mult), r=[BW["kk"], BW["Ex"]],
                 w=[BWB["abar"]])
            TTo("pool", WB["bbar"], W["b"], W["E2"], ALU.mult, [BW["b"], BW["E2"]], [BWB["bbar"]])
            TTo("dve", WB["kbar"], W["kd"], W["E2"], ALU.mult, [BW["kd"], BW["E2"]], [BWB["kbar"]])
            TTo("pool", WB["Bt"], W["b"], W["E3"], ALU.mult, [BW["b"], BW["E3"]], [BWB["Bt"]])
            TTo("pool", WB["Kt"], W["kd"], W["E3"], ALU.mult, [BW["kd"], BW["E3"]], [BWB["Kt"]])
            P.op("act", lambda e: e.copy(out=WB["vq"], in_=vr), r=[B_zc], w=[BWB["vq"]])

            for grp in range(2):
                hs = [4 * grp + i for i in range(4)]
                for i, h in enumerate(hs):
                    hsl = slice(h * 64, (h + 1) * 64)
                    tbk, tb = i // 2, i % 2
                    P.op("pe", [lambda e, ii=ii, nm=nm, tbk=tbk, tb=tb, hsl=hsl: e.transpose(
                        out=PSB(tbk, 128, parts=64, off=tb * 512 + ii * 128), in_=WB[nm][:, hsl], identity=ident_b)
                        for ii, nm in enumerate(("abar", "rbar", "bbar", "kbar"))],
                         r=[BWB["abar"], BWB["rbar"], BWB["bbar"], BWB["kbar"], B_ident],
                         w=pq[tbk] if tb == 0 else [], pw=[] if tb == 0 else pq[tbk])
                for i, h in enumerate(hs):
                    tbk, tb = i // 2, i % 2
                    P.op("act", lambda e, h=h, tbk=tbk, tb=tb: e.copy(out=TT[h], in_=PSB(tbk, 512, parts=64, off=tb * 512)),
                         r=pq[tbk], w=[B_TT[h]])
                for i, h in enumerate(hs):
                    P.op("pe", lambda e, h=h, i=i: e.matmul(out=PQ(4, i), lhsT=TT[h][:, 0:128], rhs=TT[h][:, 256:384],
                                                            start=True, stop=True),
                         r=[B_TT[h]], w=pq[4] if i == 0 else [], pw=[] if i == 0 else pq[4])
                for i, h in enumerate(hs):
                    mb = 2 + (i % 2)
                    P.op("pe", [lambda e, h=h, mb=mb: e.matmul(out=PS(mb, 256), lhsT=TT[h][:, 384:512], rhs=TT[h][:, 0:256],
                                                               start=True, stop=True),
                                lambda e, h=h, mb=mb: e.matmul(out=PS(mb, 256, off=256), lhsT=TT[h][:, 256:384],
                                                               rhs=TT[h][:, 0:256], start=True, stop=True)],
                         r=[B_TT[h]], w=pq[mb])
                    P.op("dve", lambda e, h=h, mb=mb: e.tensor_tensor(out=MM[h], in0=PS(mb), in1=cm[:, d, 0:512],
                                                                      op=ALU.mult), r=pq[mb] + [B_c], w=[B_MM[h]])
                    if i == 1:
                        for i2, h2 in enumerate(hs):
                            P.op("dve", lambda e, h2=h2, i2=i2: e.tensor_tensor(out=PA[h2][0], in0=PQ(4, i2),
                                                                                in1=cm[:, d, 512:640], op=ALU.mult),
                                 r=pq[4] + [B_c], w=[B_PA[h2][0]])
                for i, h in enumerate(hs):
                    hsl = slice(h * 64, (h + 1) * 64)
                    P.op("pe", lambda e, h=h, i=i, hsl=hsl: e.matmul(out=PQ(7, i, 64), lhsT=MM[h][:, 0:128],
                                                                     rhs=WB["vq"][:, hsl], start=True, stop=True),
                         r=[B_MM[h], BWB["vq"]], w=pq[7] if i == 0 else [], pw=[] if i == 0 else pq[7])
                    P.op("pool", lambda e, h=h, hsl=hsl: e.tensor_copy(out=Zb[h][:, 0:64], in_=WB["abar"][:, hsl]),
                         r=[BWB["abar"]], w=[B_Zb[h]])
                for i, h in enumerate(hs):
                    P.op("act", lambda e, h=h, i=i: e.copy(out=Zb[h][:, 64:128], in_=PQ(7, i, 64)), r=pq[7],
                         pw=[B_Zb[h]])
            if True:
                hs = list(range(8))
                cur = {h: 0 for h in hs}
                for j in range(7):
                    for i, h in enumerate(hs):
                        Bp = MM[h][:, 256:384] if j == 0 else PB[h][cur[h]]
                        Bb = B_MM[h] if j == 0 else B_PB[h][cur[h]]
                        P.op("pe", lambda e, h=h, i=i, Bp=Bp: e.matmul(out=PQ(i // 4, i % 4), lhsT=Bp, rhs=Zb[h], start=True, stop=True),
                             r=[Bb, B_Zb[h]], w=[pq[i // 4][i % 4]])
                    if j < 6:
                        for i, h in enumerate(hs):
                            Bp = MM[h][:, 256:384] if j == 0 else PB[h][cur[h]]
                            Bb = B_MM[h] if j == 0 else B_PB[h][cur[h]]
                            Ap = PA[h][cur[h]]
                            Ab = B_PA[h][cur[h]]
                            bk, qq = 3 + i // 2, (i % 2) * 2
                            P.op("pe", lambda e, Ap=Ap, Bp=Bp, bk=bk, qq=qq: e.matmul(out=PQ(bk, qq), lhsT=Ap, rhs=Bp,
                                                                                       start=True, stop=True),
                                 r=[Ab, Bb], w=[pq[bk][qq]])
                            if j < 5:
                                P.op("pe", lambda e, Ap=Ap, Bp=Bp, bk=bk, qq=qq: e.matmul(out=PQ(bk, qq + 1), lhsT=Bp, rhs=Ap,
                                                                                           start=True, stop=True),
                                     r=[Ab, Bb], w=[pq[bk][qq + 1]])
                    for i, h in enumerate(hs):
                        P.op("dve", lambda e, h=h, i=i: e.tensor_tensor(out=Zb[h], in0=PQ(i // 4, i % 4), in1=Zb[h], op=ALU.add),
                             r=[pq[i // 4][i % 4], B_Zb[h]], w=[B_Zb[h]])
                    if j < 6:
                        for i, h in enumerate(hs):
                            nx = 1 - cur[h]
                            bk, qq = 3 + i // 2, (i % 2) * 2
                            P.op("act", lambda e, h=h, nx=nx, bk=bk, qq=qq: e.copy(out=PB[h][nx], in_=PQ(bk, qq)),
                                 r=[pq[bk][qq]], w=[B_PB[h][nx]])
                            if j < 5:
                                P.op("act" if i % 2 else "dve",
                                     (lambda e, h=h, nx=nx, bk=bk, qq=qq: e.copy(out=PA[h][nx], in_=PQ(bk, qq + 1))) if i % 2 else
                                     (lambda e, h=h, nx=nx, bk=bk, qq=qq: e.tensor_scalar(out=PA[h][nx], in0=PQ(bk, qq + 1),
                                                                                            scalar1=1.0, scalar2=None,
                                                                                            op0=ALU.mult)),
                                     r=[pq[bk][qq + 1]], w=[B_PA[h][nx]])
                            cur[h] = nx
                for i, h in enumerate(hs):
                    P.op("pe", lambda e, h=h, i=i: e.transpose(out=PSB(0, 128, parts=64, off=i * 128), in_=Zb[h][:, 0:64],
                                                               identity=ident_b),
                         r=[B_Zb[h], B_ident], w=pq[0] if i == 0 else [], pw=[] if i == 0 else pq[0])
                for i, h in enumerate(hs):
                    P.op("act", lambda e, h=h, i=i: e.copy(out=WbT[h], in_=PSB(0, 128, parts=64, off=i * 128)),
                         r=pq[0], w=[B_WbT[h]])
                for i, h in enumerate(hs):
                    P.op("pe", lambda e, h=h, i=i: e.matmul(out=PS(1, 64, off=i * 64), lhsT=WbT[h], rhs=STb[s][d][h], start=True,
                                                            stop=True),
                         r=[B_WbT[h], B_STb[s][d][h]], w=pq[1] if i == 0 else [], pw=[] if i == 0 else pq[1])
                for i, h in enumerate(hs):
                    P.op("dve", lambda e, h=h, i=i: e.tensor_tensor(out=Ub[h], in0=PS(1, 64, off=i * 64), in1=Zb[h][:, 64:128],
                                                                    op=ALU.add), r=pq[1] + [B_Zb[h]], w=[B_Ub[h]])
                for i, h in enumerate(hs):
                    hsl = slice(h * 64, (h + 1) * 64)
                    P.op("pe", [lambda e, h=h, i=i, hsl=hsl: e.matmul(out=PS(2, 64, off=i * 64), lhsT=TT[h][:, 128:256],
                                                                      rhs=STb[s][d][h], start=True, stop=False),
                                lambda e, h=h, i=i, hsl=hsl: e.matmul(out=PS(2, 64, off=i * 64), lhsT=MM[h][:, 384:512],
                                                                      rhs=Ub[h], start=False, stop=False),
                                lambda e, h=h, i=i, hsl=hsl: e.matmul(out=PS(2, 64, off=i * 64), lhsT=MM[h][:, 128:256],
                                                                      rhs=WB["vq"][:, hsl], start=False, stop=True)],
                         r=[B_TT[h], B_STb[s][d][h], B_MM[h], B_Ub[h], BWB["vq"]],
                         w=pq[2] if i == 0 else [], pw=[] if i == 0 else pq[2])
                    P.op("pe", [lambda e, h=h, i=i, hsl=hsl: e.matmul(out=PS(4, 64, parts=64, off=i * 64), lhsT=WB["Bt"][:, hsl],
                                                                      rhs=Ub[h], start=True, stop=False),
                                lambda e, h=h, i=i, hsl=hsl: e.matmul(out=PS(4, 64, parts=64, off=i * 64), lhsT=WB["Kt"][:, hsl],
                                                                      rhs=WB["vq"][:, hsl], start=False, stop=True)],
                         r=[BWB["Bt"], BWB["Kt"], B_Ub[h], BWB["vq"]], w=pq[4] if i == 0 else [], pw=[] if i == 0 else pq[4])
                for i, h in enumerate(hs):
                    P.op("dve", lambda e, h=h, i=i: e.scalar_tensor_tensor(
                        out=ST[s][d][h], in0=ST[s][d][h], scalar=gC[s][d][:, h:h + 1], in1=PS(4, 64, parts=64, off=i * 64),
                        op0=ALU.mult, op1=ALU.add), r=[B_ST[s][d][h], B_gC[s][d]] + pq[4], w=[B_ST[s][d][h]])
                    P.op("pool", lambda e, h=h: e.tensor_copy(out=STb[s][d][h], in_=ST[s][d][h]), r=[B_ST[s][d][h]],
                         w=[B_STb[s][d][h]])
                P.op("act", lambda e: e.copy(out=ytile[:, 0:512], in_=PS(2)), r=pq[2], pw=[B_y])
            P.op("sp", lambda e: e.dma_start(out=yd_d[d][s, r0:r0 + 128, :], in_=ytile), r=[B_y], w=[yd_t[d][s][c]],
                 dma=True)

        for i in range(NT):
            for s in range(NS):
                process(s, 0, i)
                process(s, 1, NT - 1 - i)

        yf = zc[:, 0:520]
        yb = zp[:, 0:520]
        gvt = zn[:, 0:1024]
        for s in range(NS):
            for c in range(NT):
                r0 = c * 128
                P.op("sp", lambda e, s=s, r0=r0: e.dma_start(out=yf, in_=yd_d[0][s, r0:r0 + 128, :]), r=[yd_t[0][s][c]],
                     w=[B_zc], dma=True)
                P.op("sp", lambda e, s=s, r0=r0: e.dma_start(out=yb, in_=yd_d[1][s, r0:r0 + 128, :]), r=[yd_t[1][s][c]],
                     w=[B_zp], dma=True)
                P.op("sp", lambda e, s=s, r0=r0: e.dma_start(out=gvt, in_=gv_d[s, r0:r0 + 128, :]), r=[gv_t[s][c]],
                     w=[B_zn], dma=True)
                y3 = yf[:, 0:512].rearrange("p (h k) -> p h k", k=64)
                P.op("dve", lambda e: e.tensor_tensor(out=yf, in0=yf, in1=yb, op=ALU.add), r=[B_zc, B_zp], w=[B_zc])
                P.op("dve", lambda e: e.tensor_reduce(out=s8[:, 0:8], in_=y3, axis=AX.X, op=ALU.add), r=[B_zc], w=[B_s8])
                P.op("dve", lambda e: e.tensor_scalar(out=s8[:, 0:8], in0=s8[:, 0:8], scalar1=1.0 / 64.0, scalar2=None,
                                                      op0=ALU.mult), r=[B_s8], w=[B_s8])
                P.op("dve", lambda e: e.tensor_tensor(out=y3, in0=y3, in1=bc(s8[:, 0:8].unsqueeze(2), [128, 8, 64]),
                                                      op=ALU.subtract), r=[B_zc, B_s8], w=[B_zc])
                P.op("pool", lambda e: e.tensor_tensor(out=W["sq"], in0=yf[:, 0:512], in1=yf[:, 0:512], op=ALU.mult),
                     r=[B_zc], w=[BW["sq"]])
                P.op("dve", lambda e: e.tensor_reduce(out=s8[:, 0:8], in_=W["sq"].rearrange("p (h k) -> p h k", k=64),
                                                      axis=AX.X, op=ALU.add), r=[BW["sq"]], w=[B_s8])
                rstd(s8[:, 8:16], s8[:, 0:8], 1.0 / 64.0, 8, [B_s8], [B_s8], eps=GN_EPS)
                P.op("dve", lambda e: e.tensor_tensor(out=y3, in0=y3, in1=bc(s8[:, 8:16].unsqueeze(2), [128, 8, 64]),
                                                      op=ALU.mult), r=[B_zc, B_s8], w=[B_zc])
                P.op("dve", lambda e: e.tensor_tensor(out=yf[:, 0:512], in0=yf[:, 0:512], in1=lgb, op=ALU.mult),
                     r=[B_zc, B_c], w=[B_zc])
                P.op("dve", lambda e: e.tensor_tensor(out=yf[:, 0:512], in0=yf[:, 0:512], in1=lbb, op=ALU.add),
                     r=[B_zc, B_c], w=[B_zc])
                P.op("pool", lambda e: e.scalar_tensor_tensor(
                    out=W["t2"].rearrange("p (h k) -> p h k", k=64), in0=gvt[:, 512:1024].rearrange("p (h k) -> p h k", k=64),
                    scalar=0.5, in1=bc(yf[:, 512:520].unsqueeze(2), [128, 8, 64]), op0=ALU.mult, op1=ALU.mult)
                    if False else e.tensor_tensor(
                    out=W["t2"].rearrange("p (h k) -> p h k", k=64), in0=gvt[:, 512:1024].rearrange("p (h k) -> p h k", k=64),
                    in1=bc(yf[:, 512:520].unsqueeze(2), [128, 8, 64]), op=ALU.mult), r=[B_zc, B_zn], w=[BW["t2"]])
                P.op("dve", lambda e: e.scalar_tensor_tensor(out=yf[:, 0:512], in0=W["t2"], scalar=0.5, in1=yf[:, 0:512],
                                                             op0=ALU.mult, op1=ALU.add), r=[BW["t2"], B_zc], w=[B_zc])
                P.op("dve", lambda e: e.tensor_tensor(out=yf[:, 0:512], in0=yf[:, 0:512], in1=gvt[:, 0:512], op=ALU.mult),
                     r=[B_zc, B_zn], w=[B_zc])
                P.op("sp", lambda e, s=s, r0=r0: e.dma_start(out=rw_d[s, r0:r0 + 128, :], in_=yf[:, 0:512]), r=[B_zc],
                     w=[rw_t[s][c]], dma=True)
        P.barrier()

    @_phase(4)
    def _p4():
        ar.reset()
        w_out_b = ar.alloc([8, D], BF16)
        Ws_b = ar.alloc([8, 2048], BF16)
        gT2 = ar.alloc([4], F32)
        g2_b = ar.alloc([D], F32)
        NG = 16
        gsl = ar.alloc([NG, 2 * D], BF16)
        B_g = [Buf(f"gs{k_}") for k_ in range(NG)]
        stage = [gsl[:, 0:2, :].rearrange("p a b -> p (a b)").bitcast(F32),
                 gsl[:, 2:4, :].rearrange("p a b -> p (a b)").bitcast(F32)]
        B_stage = [[B_g[0], B_g[1]], [B_g[2], B_g[3]]]
        uv_d = nc.dram_tensor("uv_s", [NE, 2 * D], BF16, kind="Internal").ap()
        B_uv = Buf("uv")
        for (src_, co_) in ((expert_u, 0), (expert_v, D)):
            for q_ in range(8):
                P.op("pool", lambda e, src_=src_, co_=co_, q_=q_: e.dma_start(
                    out=uv_d[q_ * 2048:(q_ + 1) * 2048, co_:co_ + D], in_=src_[q_ * 2048:(q_ + 1) * 2048, :]),
                    pw=[B_uv], dma=True)
        B_w = Buf("w4")
        B_ws = Buf("ws")
        B_gT = Buf("gT4")
        B_g2 = Buf("g2")
        load_colT(gT2[:, 0:4], attn_out_g, 4, B_gT)
        P.op("sp", lambda e: e.dma_start(out=g2_b, in_=bc(norm2_g.unsqueeze(0), [128, D])), w=[B_g2], dma=True)
        for c in range(8):
            st, bs = stage[c % 2], B_stage[c % 2]
            P.op("sp", lambda e, st=st, c=c: e.dma_start(out=st[:, 0:D], in_=w_out[c * 128:(c + 1) * 128, :]), w=bs, dma=True)
            if c < 4:
                P.op("dve", lambda e, st=st, c=c: e.tensor_scalar(out=w_out_b[:, c, :], in0=st[:, 0:D], scalar1=gT2[:, c:c + 1],
                                                                  scalar2=None, op0=ALU.mult), r=bs + [B_gT], pw=[B_w])
            else:
                P.op("dve", lambda e, st=st, c=c: e.tensor_copy(out=w_out_b[:, c, :], in_=st[:, 0:D]), r=bs, pw=[B_w])
        skT = ar.alloc([2, 128], F32)
        B_sk = Buf("skT")
        wT4 = ar.alloc([4, 128], F32)
        B_wT4 = Buf("wT4")
        for hf in range(2):
            P.op("sp", lambda e, hf=hf: e.dma_start(out=stage[0][:, hf * 128:(hf + 1) * 128], in_=sub_keys[hf]),
                 w=B_stage[0] if hf == 0 else [], pw=[] if hf == 0 else B_stage[0], dma=True)
        P.op("pe", [lambda e, hf=hf: e.transpose(out=PS(0, 128, off=hf * 128), in_=stage[0][:, hf * 128:(hf + 1) * 128],
                                                 identity=ident_f) for hf in range(2)], r=B_stage[0] + [B_ident], w=[pbank[0]])
        P.op("act", lambda e: e.copy(out=skT.rearrange("p a b -> p (a b)"), in_=PS(0, 256)), r=[pbank[0]], w=[B_sk])
        for dc in range(8):
            st, bs = stage[(dc + 1) % 2], B_stage[(dc + 1) % 2]
            P.op("sp", lambda e, st=st, dc=dc: e.dma_start(out=st, in_=w_pq[dc * 128:(dc + 1) * 128, :]), w=bs, dma=True)
            for g4 in range(4):
                P.op("pe", [lambda e, st=st, jj=jj, g4=g4: e.transpose(
                    out=PS(1, 128, off=jj * 128), in_=st[:, (g4 * 4 + jj) * 128:(g4 * 4 + jj + 1) * 128], identity=ident_f)
                    for jj in range(4)], r=bs + [B_ident], w=[pbank[1]])
                P.op("act", lambda e: e.copy(out=wT4.rearrange("p a b -> p (a b)"), in_=PS(1)), r=[pbank[1]], w=[B_wT4])
                P.op("pe", [lambda e, jj=jj: e.matmul(out=PS(2, 128, off=jj * 128), lhsT=wT4[:, jj, :], rhs=skT[:, jj % 2, :],
                                                      start=True, stop=True) for jj in range(4)],
                     r=[B_wT4, B_sk], w=[pbank[2]])
                P.op("act", lambda e, dc=dc, g4=g4: e.copy(out=Ws_b[:, dc, g4 * 512:(g4 + 1) * 512], in_=PS(2)),
                     r=[pbank[2]], pw=[B_ws])

        _xs = ar.alloc([D], F32)
        xs = [_xs, _xs]
        _bxs = Buf("xs")
        B_xs = [_bxs, _bxs]
        at = ar.alloc([D], F32)
        B_at = Buf("at")
        st4 = ar.alloc([8], F32)
        B_st = Buf("st4")
        catb = ar.alloc([D], BF16)
        B_catb = Buf("catb")
        catT = ar.alloc([8, 128], BF16)
        B_catT = Buf("catT")
        h_sb = [ar.alloc([D], F32) for _ in range(2)]
        B_h = [Buf("h0"), Buf("h1")]

        hnb = [ar.alloc([D], BF16) for _ in range(2)]
        B_hnb = [Buf("hnb0"), Buf("hnb1")]
        _tk = ar.alloc([2048], F32)
        s_sb = _tk.rearrange("p (a b) -> p a b", b=128)
        B_s = Buf("tk")
        scr2 = ar.alloc([2048], F32)
        B_scr2 = Buf("scr2")
        hn = scr2[:, D:2 * D]
        B_hn = B_scr2
        cand = _tk.rearrange("p (a b) -> p a b", b=256)
        B_cand = B_s
        eq = _tk.rearrange("p (a b c) -> p a b c", b=16, c=16)
        B_eq = B_s
        v16 = ar.alloc([16, 16], F32)
        i16 = ar.alloc([16, 16], U32)
        i16f = ar.alloc([16, 16], F32)
        B_v16, B_i16, B_i16f = Buf("v16"), Buf("i16"), Buf("i16f")
        b16 = ar.alloc([8, 16], F32)
        p16 = ar.alloc([8, 16], U32)
        pa = ar.alloc([8, 16], U32)
        pb_ = ar.alloc([8, 16], U32)
        paf = ar.alloc([8, 16], F32)
        pbf = ar.alloc([8, 16], F32)
        B_b16, B_p16, B_pab = Buf("b16"), Buf("p16"), Buf("pab")
        sel = ar.alloc([2, 8, 16], F32)
        B_sel = Buf("sel")
        idxf = ar.alloc([128], F32)
        B_idxf = Buf("idxf")
        idxu = [ar.alloc([128], U32) for _ in range(2)]
        B_idx = [Buf("idx0"), Buf("idx1")]
        gate = [ar.alloc([8, 16], F32) for _ in range(2)]
        B_gate = [Buf("gate0"), Buf("gate1")]
        gs8 = ar.alloc([16], F32)
        B_gs8 = Buf("gs8")
        actv = ar.alloc([128], F32)
        B_actv = [Buf(f"actv{h_}") for h_ in range(8)]
        wgt = ar.alloc([128], F32)
        B_wgt = [Buf(f"wgt{h_}") for h_ in range(8)]
        acc = ar.alloc([D], F32)
        B_acc = Buf("acc")
        junkb = acc.bitcast(BF16)[:, 0:D]
        B_junkb = B_acc
        NDG = 4
        dg = [ar.alloc([128], BF16) for _ in range(NDG)]
        B_dg = [Buf(f"dg{i_}") for i_ in range(NDG)]
        iota16 = ar.alloc([16], F32)
        B_io = Buf("iota")
        P.op("pool", lambda e: e.iota(iota16, pattern=[[1, 16]], base=0, channel_multiplier=0,
                                      allow_small_or_imprecise_dtypes=True), w=[B_io])
        junk = scr2[:, 0:D]
        gk = [0]

        def front(s, t, par):
            r0 = t * 128
            xsp, bxs, hp, bh = xs[par], B_xs[par], h_sb[par], B_h[par]
            P.op("sp", lambda e: e.dma_start(out=xsp, in_=x[s, r0:r0 + 128, :]), w=[bxs], dma=True)
            P.op("sp", lambda e: e.dma_start(out=at[:, 0:512], in_=attn_d[s, r0:r0 + 128, :]),
                 r=[attn_t[s][t]], w=[B_at], dma=True)
            P.op("sp", lambda e: e.dma_start(out=at[:, 512:1024], in_=rw_d[s, r0:r0 + 128, :]),
                 r=[rw_t[s][t]], pw=[B_at], dma=True)
            P.op("act", lambda e: e.activation(out=junk[:, 0:512], in_=at[:, 0:512], func=AF.Square,
                                               scale=float(512 ** -0.5), accum_out=st4[:, 0:1]),
                 r=[B_at], w=[B_scr2], pw=[B_st])
            rstd(st4[:, 1:2], st4[:, 0:1], 1.0, 1, [B_st], [B_st])
            P.op("dve", lambda e: e.tensor_scalar(out=catb[:, 0:512], in0=at[:, 0:512],
                                                  scalar1=st4[:, 1:2], scalar2=None, op0=ALU.mult),
                 r=[B_at, B_st], w=[B_catb])
            P.op("act", lambda e: e.copy(out=catb[:, 512:1024], in_=at[:, 512:1024]), r=[B_at], pw=[B_catb])
            P.op("pe", [lambda e, c=c: e.transpose(out=PSB(0, 128, off=c * 128), in_=catb[:, c * 128:(c + 1) * 128],
                                                   identity=ident_b) for c in range(8)],
                 r=[B_catb, B_ident], w=[pbank[0]])
            P.op("act", lambda e: e.copy(out=catT.rearrange("p a b -> p (a b)"), in_=PSB(0)), r=[pbank[0]], w=[B_catT])
            fns = []
            for j in range(2):
                for c in range(8):
                    fns.append(lambda e, j=j, c=c: e.matmul(out=PS(1 + j), lhsT=catT[:, c, :],
                                                            rhs=w_out_b[:, c, j * 512:(j + 1) * 512],
                                                            start=(c == 0), stop=(c == 7)))
            P.op("pe", fns, r=[B_catT, B_w], w=[pbank[1], pbank[2]])
            for j in range(2):
                P.op("dve", lambda e, j=j: e.tensor_tensor(
                    out=hp[:, j * 512:(j + 1) * 512], in0=PS(1 + j), in1=xsp[:, j * 512:(j + 1) * 512], op=ALU.add),
                    r=[pbank[1 + j], bxs], w=[bh] if j == 0 else [], pw=[] if j == 0 else [bh])
            yield
            P.op("act", lambda e: e.activation(out=junk, in_=hp, func=AF.Square, scale=1.0 / 32.0,
                                               accum_out=st4[:, 2:3]), r=[bh], w=[B_scr2], pw=[B_st])
            rstd(st4[:, 3:4], st4[:, 2:3], 1.0, 1, [B_st], [B_st])
            P.op("dve", lambda e: e.scalar_tensor_tensor(out=hn, in0=hp, scalar=st4[:, 3:4], in1=g2_b,
                                                         op0=ALU.mult, op1=ALU.mult),
                 r=[bh, B_st, B_g2], w=[B_hn])
            P.op("act", lambda e: e.copy(out=catb, in_=hn), r=[B_hn], w=[B_catb])
            P.op("act", lambda e: e.copy(out=hnb[par], in_=hn), r=[B_hn], w=[B_hnb[par]])
            P.op("pe", [lambda e, c=c: e.transpose(out=PSB(0, 128, off=c * 128), in_=catb[:, c * 128:(c + 1) * 128],
                                                   identity=ident_b) for c in range(8)],
                 r=[B_catb, B_ident], w=[pbank[0]])
            P.op("act", lambda e: e.copy(out=catT.rearrange("p a b -> p (a b)"), in_=PSB(0)), r=[pbank[0]], w=[B_catT])
            sf = s_sb.rearrange("p a b -> p (a b)")
            for half in range(2):
                fns = []
                for jj in range(2):
                    j = half * 2 + jj
                    for c in range(8):
                        fns.append(lambda e, j=j, jj=jj, c=c: e.matmul(out=PS(3 + jj), lhsT=catT[:, c, :],
                                                                      rhs=Ws_b[:, c, j * 512:(j + 1) * 512],
                                                                      start=(c == 0), stop=(c == 7)))
                P.op("pe", fns, r=[B_catT, B_ws], w=[pbank[3], pbank[4]])
                for jj in range(2):
                    j = half * 2 + jj
                    P.op("act", lambda e, j=j, jj=jj: e.copy(out=sf[:, j * 512:(j + 1) * 512], in_=PS(3 + jj)),
                         r=[pbank[3 + jj]], w=[B_s] if j == 0 else [], pw=[] if j == 0 else [B_s])
            yield
            s2 = scr2.rearrange("p (a b) -> p a b", b=128)
            for j in range(16):
                fl = (j == 0)
                P.op("dve", lambda e, j=j: e.max(out=v16[:, j, 0:8], in_=s_sb[:, j, :]), r=[B_s],
                     w=[B_v16] if fl else [], pw=[] if fl else [B_v16])
                P.op("dve", lambda e, j=j: e.max_index(out=i16[:, j, 0:8], in_max=v16[:, j, 0:8], in_values=s_sb[:, j, :]),
                     r=[B_s, B_v16], w=[B_i16] if fl else [], pw=[] if fl else [B_i16])
                P.op("dve", lambda e, j=j: e.match_replace(out=s2[:, j, :], in_to_replace=v16[:, j, 0:8],
                                                           in_values=s_sb[:, j, :], imm_value=-1e30),
                     r=[B_s, B_v16], w=[B_scr2] if fl else [], pw=[] if fl else [B_scr2])
                P.op("dve", lambda e, j=j: e.max(out=v16[:, j, 8:16], in_=s2[:, j, :]), r=[B_scr2], pw=[B_v16])
                P.op("dve", lambda e, j=j: e.max_index(out=i16[:, j, 8:16], in_max=v16[:, j, 8:16], in_values=s2[:, j, :]),
                     r=[B_scr2, B_v16], pw=[B_i16])
                if j % 4 == 3:
                    yield
            P.op("dve", lambda e: e.tensor_copy(out=i16f, in_=i16), r=[B_i16], w=[B_i16f])
            v4 = v16.rearrange("p (h f) k -> p h f k", f=2)
            i4 = i16f.rearrange("p (h f) k -> p h f k", f=2)
            P.op("dve", lambda e: e.tensor_tensor(out=cand.rearrange("p h (a b) -> p h a b", b=16),
                                                  in0=bc(v4[:, :, 0, :].unsqueeze(3), [128, 8, 16, 16]),
                                                  in1=bc(v4[:, :, 1, :].unsqueeze(2), [128, 8, 16, 16]), op=ALU.add),
                 r=[B_v16], w=[B_cand])
            c2 = scr2.rearrange("p (a b) -> p a b", b=256)
            for h in range(8):
                fl = (h == 0)
                P.op("dve", lambda e, h=h: e.max(out=b16[:, h, 0:8], in_=cand[:, h, :]), r=[B_cand],
                     w=[B_b16] if fl else [], pw=[] if fl else [B_b16])
                P.op("dve", lambda e, h=h: e.max_index(out=p16[:, h, 0:8], in_max=b16[:, h, 0:8], in_values=cand[:, h, :]),
                     r=[B_cand, B_b16], w=[B_p16] if fl else [], pw=[] if fl else [B_p16])
                P.op("dve", lambda e, h=h: e.match_replace(out=c2[:, h, :], in_to_replace=b16[:, h, 0:8],
                                                           in_values=cand[:, h, :], imm_value=-1e30),
                     r=[B_cand, B_b16], w=[B_scr2] if fl else [], pw=[] if fl else [B_scr2])
                P.op("dve", lambda e, h=h: e.max(out=b16[:, h, 8:16], in_=c2[:, h, :]), r=[B_scr2], pw=[B_b16])
                P.op("dve", lambda e, h=h: e.max_index(out=p16[:, h, 8:16], in_max=b16[:, h, 8:16], in_values=c2[:, h, :]),
                     r=[B_scr2, B_b16], pw=[B_p16])
                if h % 4 == 3:
                    yield
            P.op("dve", lambda e: e.tensor_single_scalar(out=pa, in_=p16, scalar=4, op=ALU.logical_shift_right),
                 r=[B_p16], w=[B_pab])
            P.op("dve", lambda e: e.tensor_single_scalar(out=pb_, in_=p16, scalar=15, op=ALU.bitwise_and),
                 r=[B_p16], pw=[B_pab])
            P.op("dve", lambda e: e.tensor_copy(out=paf, in_=pa), r=[B_pab], pw=[B_pab])
            P.op("dve", lambda e: e.tensor_copy(out=pbf, in_=pb_), r=[B_pab], pw=[B_pab])
            io4 = bc(iota16.unsqueeze(1).unsqueeze(1), [128, 8, 16, 16])
            for (k_, pf) in ((0, paf), (1, pbf)):
                P.op("dve", lambda e, pf=pf: e.tensor_tensor(out=eq, in0=io4, in1=bc(pf.unsqueeze(3), [128, 8, 16, 16]),
                                                             op=ALU.is_equal), r=[B_io, B_pab], w=[B_eq])
                P.op("dve", lambda e, k_=k_: e.tensor_tensor(out=eq, in0=eq,
                                                             in1=bc(i4[:, :, k_, :].unsqueeze(2), [128, 8, 16, 16]),
                                                             op=ALU.mult), r=[B_eq, B_i16f], w=[B_eq])
                P.op("dve", lambda e, k_=k_: e.tensor_reduce(out=sel[:, k_], in_=eq, axis=AX.X, op=ALU.add),
                     r=[B_eq], w=[B_sel] if k_ == 0 else [], pw=[] if k_ == 0 else [B_sel])
                yield
            P.op("dve", lambda e: e.scalar_tensor_tensor(out=idxf.rearrange("p (h k) -> p h k", k=16), in0=sel[:, 0],
                                                         scalar=128.0, in1=sel[:, 1], op0=ALU.mult, op1=ALU.add),
                 r=[B_sel], w=[B_idxf])
            P.op("dve", lambda e: e.tensor_scalar(out=idxf, in0=idxf, scalar1=0.0, scalar2=float(NE - 1), op0=ALU.max,
                                                  op1=ALU.min), r=[B_idxf], w=[B_idxf])
            P.op("dve", lambda e: e.tensor_copy(out=idxu[par], in_=idxf), r=[B_idxf], w=[B_idx[par]])
            gt = gate[par]
            P.op("dve", lambda e: e.tensor_tensor(out=gt, in0=b16, in1=bc(b16[:, :, 0:1], [128, 8, 16]),
                                                  op=ALU.subtract), r=[B_b16], w=[B_gate[par]])
            P.op("act", lambda e: e.activation(out=gt, in_=gt, func=AF.Exp), r=[B_gate[par]], w=[B_gate[par]])
            P.op("dve", lambda e: e.tensor_reduce(out=gs8[:, 0:8], in_=gt, axis=AX.X, op=ALU.add), r=[B_gate[par]],
                 w=[B_gs8])
            P.op("dve", lambda e: e.reciprocal(out=gs8[:, 8:16], in_=gs8[:, 0:8]), r=[B_gs8], pw=[B_gs8])
            P.op("dve", lambda e: e.tensor_tensor(out=gt, in0=gt, in1=bc(gs8[:, 8:16].unsqueeze(2), [128, 8, 16]),
                                                  op=ALU.mult), r=[B_gate[par], B_gs8], w=[B_gate[par]])
            yield

        def back(s, t, par, nxt):
            r0 = t * 128
            hp, bh = h_sb[par], B_h[par]

            def adv():
                if nxt is not None:
                    next(nxt, None)

            for h in range(8):
                hs16 = slice(h * 16, (h + 1) * 16)
                for jj in range(16):
                    j = h * 16 + jj
                    P.op("pool", lambda e, j=j, jj=jj: e.indirect_dma_start(
                        out=gsl[:, jj, :], out_offset=None, in_=uv_d,
                        in_offset=bass.IndirectOffsetOnAxis(ap=idxu[par][:, j:j + 1], axis=0)),
                        r=[B_idx[par], B_uv], w=[B_g[jj]], dma=True)
                    P.op("dve", lambda e, j=j, jj=jj: e.scalar_tensor_tensor(
                        out=junkb, in0=gsl[:, jj, 0:D], scalar=1.0, in1=hnb[par], op0=ALU.mult, op1=ALU.mult,
                        accum_out=actv[:, j:j + 1]), r=[B_g[jj], B_hnb[par]], w=[B_junkb],
                        pw=[B_actv[h]])
                P.op("act", lambda e, hs16=hs16: e.activation(out=wgt[:, hs16], in_=actv[:, hs16], func=AF.Gelu),
                     r=[B_actv[h]], w=[B_wgt[h]])
                P.op("dve", lambda e, h=h, hs16=hs16: e.tensor_tensor(out=wgt[:, hs16], in0=wgt[:, hs16], in1=gate[par][:, h, :],
                                                                      op=ALU.mult), r=[B_wgt[h], B_gate[par]], w=[B_wgt[h]])
                adv()
                for jj in range(16):
                    j = h * 16 + jj
                    kd_ = j % NDG
                    P.op("act", lambda e, j=j, kd_=kd_: e.activation(out=dg[kd_], in_=ident_b, func=AF.Copy,
                                                                     scale=wgt[:, j:j + 1]),
                         r=[B_ident, B_wgt[h]], w=[B_dg[kd_]])
                    P.op("pe", [lambda e, j=j, jj=jj, kd_=kd_, hh=hh: e.matmul(
                        out=PS(5 + hh), lhsT=dg[kd_], rhs=gsl[:, jj, D + hh * 512:D + (hh + 1) * 512],
                        start=(j == 0), stop=(j == 127)) for hh in range(2)], r=[B_g[jj], B_dg[kd_]],
                        w=[pbank[5], pbank[6]] if j == 0 else [], pw=[] if j == 0 else [pbank[5], pbank[6]])
                adv()
            for hh in range(2):
                P.op("dve", lambda e, hh=hh: e.tensor_tensor(out=acc[:, hh * 512:(hh + 1) * 512], in0=PS(5 + hh),
                                                             in1=hp[:, hh * 512:(hh + 1) * 512], op=ALU.add),
                     r=[pbank[5 + hh], bh], w=[B_acc] if hh == 0 else [], pw=[] if hh == 0 else [B_acc])
            o = P.op("sp", lambda e: e.dma_start(out=y[s, r0:r0 + 128, :], in_=acc), r=[B_acc], dma=True)
            out_ops.append(o)

        tiles = [(s, t) for s in range(NS) for t in range(NT)]
        g0 = front(tiles[0][0], tiles[0][1], 0)
        for _ in g0:
            pass
        for i_, (s, t) in enumerate(tiles):
            nxt = front(tiles[i_ + 1][0], tiles[i_ + 1][1], (i_ + 1) % 2) if i_ + 1 < len(tiles) else None
            back(s, t, i_ % 2, nxt)
            if nxt is not None:
                for _ in nxt:
                    pass
    P.barrier()
    P.finalize()
    es.close()
    return nc


_CACHE = {}


def _consts(SEQ):
    ident = np.eye(128, dtype=np.float32)
    half = 16
    inv = 1.0 / (10000.0 ** (np.arange(half, dtype=np.float32) / half))
    ang = np.arange(SEQ, dtype=np.float32)[:, None] * inv[None, :].astype(np.float32)
    rope = np.concatenate([np.cos(ang), np.sin(ang)], axis=1).astype(np.float32)
    ce = np.float32(np.exp(-0.5))
    idx = np.arange(128)
    cmask = np.zeros((128, 2, 896), np.float32)
    for d in range(2):
        strict = (idx[:, None] < idx[None, :]) if d == 0 else (idx[:, None] > idx[None, :])
        strict = strict.astype(np.float32)
        incl = strict + np.eye(128, dtype=np.float32)
        cmask[:, d, 0:128] = strict
        cmask[:, d, 128:256] = incl
        cmask[:, d, 256:384] = strict
        cmask[:, d, 384:512] = incl
        cmask[:, d, 512:640] = strict.T
        cmask[:, d, 640:768] = -ce * incl
        cmask[:, d, 768:896] = -ce * strict.T
    cmask = cmask.reshape(128, 1792)
    return dict(ident=ident, rope=rope, cmask=cmask)


WNAMES = ["norm1_g", "w_in", "q_lat_g", "w_uq", "kv_lat_g", "w_ukv", "q_norm_g", "k_norm_g", "attn_out_g",
          "mu_prev", "mu_next", "w0", "w_up", "a0", "a_up", "g_up", "k_k", "k_a", "r_k", "ln_x_g", "ln_x_b",
          "w_out", "norm2_g", "w_pq", "sub_keys", "expert_u", "expert_v"]


def kernel(**inputs):
    xp = np.asarray(inputs["x_prompt"], dtype=np.float32)
    xsm = np.asarray(inputs["x_sample"], dtype=np.float32)
    SEQ = xp.shape[1]
    seqs = [xp[i] for i in range(xp.shape[0])] + [xsm[i] for i in range(xsm.shape[0])]
    n = 8
    assign = [(c, 8 + c if 8 + c < len(seqs) else c) for c in range(n)]
    key = (SEQ, 2)
    if key not in _CACHE:
        _CACHE[key] = build(SEQ, 2)
    nc = _CACHE[key]
    base = {k: np.ascontiguousarray(np.asarray(inputs[k], dtype=np.float32)[0]) for k in WNAMES}
    base.update(_consts(SEQ))
    in_maps = []
    for c in range(n):
        m = dict(base)
        m["x"] = np.ascontiguousarray(np.stack([seqs[assign[c][0]], seqs[assign[c][1]]]))
        in_maps.append(m)
    res = run_bass_kernel_spmd(nc, in_maps, core_ids=list(range(n)))
    outs = [None] * len(seqs)
    for c in range(n):
        yc = res.results[c]["y"]
        outs[assign[c][0]] = yc[0]
        if 8 + c < len(seqs):
            outs[8 + c] = yc[1]
    nb = xp.shape[0]
    return (np.stack(outs[:nb]).astype(np.float32), np.stack(outs[nb:]).astype(np.float32))
```
